# Optimizing a Trainium2 kernel written in Bass

```python
import functools
import jax, jax.numpy as jnp
from jax import lax
import numpy as np

D_MODEL = 2048
BATCH = 4
SEQ = 2048
DEPTH = 1
DEC_BATCH = 32
DEC_SEQ = 4
PAST_LEN = 8192
PAGE_SIZE = 128

RWKV_HEAD_DIM = 64
RWKV_WIDTH = D_MODEL // 2
RWKV_HEADS = RWKV_WIDTH // RWKV_HEAD_DIM
DECAY_LORA = 64
ICLR_LORA = 64
GN_EPS = 64e-5
SHIFT_WIDTH = 3 * RWKV_WIDTH + DECAY_LORA + ICLR_LORA

ATT_HEADS = 4
ATT_WIDTH = D_MODEL // 4
ATT_HEAD_DIM = ATT_WIDTH // ATT_HEADS
IDX_HEADS = 16
IDX_DIM = 64
TOPK_MAX = 256
Q_BLOCK = 128
ROPE_THETA = 10000.0

MEM_TOKENS = 256
MEM_HEADS = 4
MEM_WIDTH = D_MODEL // 4
MEM_HEAD_DIM = MEM_WIDTH // MEM_HEADS

MIX_WIDTH = RWKV_WIDTH + ATT_WIDTH + MEM_WIDTH
RMS_EPS = 1e-6
IN_SPLITS = (SHIFT_WIDTH, RWKV_WIDTH,
             ATT_WIDTH, ATT_WIDTH, ATT_WIDTH, ATT_WIDTH,
             IDX_HEADS * IDX_DIM, IDX_HEADS, IDX_DIM,
             MEM_WIDTH, MEM_WIDTH)
IN_WIDTH = sum(IN_SPLITS)

kernel_name = 'hymba_rwkv7_dsa_memory_step'


def rmsnorm(x, w):
    xf = x.astype(jnp.float32)
    y = xf * lax.rsqrt(jnp.mean(xf * xf, axis=-1, keepdims=True) + RMS_EPS)
    return (y * w.astype(jnp.float32)).astype(x.dtype)


def rope(x, pos):
    d = x.shape[-1]
    half = d // 2
    inv_freq = ROPE_THETA ** (-jnp.arange(half, dtype=jnp.float32) * (2.0 / d))
    ang = pos.astype(jnp.float32)[:, None] * inv_freq[None, :]
    cos = jnp.cos(ang)[:, None, :]
    sin = jnp.sin(ang)[:, None, :]
    xf = x.astype(jnp.float32)
    x1, x2 = xf[..., :half], xf[..., half:]
    return jnp.concatenate([x1 * cos - x2 * sin, x2 * cos + x1 * sin], axis=-1).astype(x.dtype)


def rwkv_time_mix(sh, prev_shift, wkv0, mu, w0, w2, a0, a2, k_k, k_a, r_k, gn_w, gn_b):
    Bh, Th, _ = sh.shape
    C = RWKV_WIDTH
    f32 = jnp.float32
    prev = jnp.concatenate([prev_shift[:, None, :].astype(sh.dtype), sh[:, :-1]], axis=1)
    xs = sh + (prev - sh) * mu
    r, k, v, cw, ca = jnp.split(xs, [C, 2 * C, 3 * C, 3 * C + DECAY_LORA], axis=-1)
    w_log = -jax.nn.softplus(-(w0 + jnp.tanh(cw) @ w2).astype(f32)) - 0.5
    decay = jnp.exp(-jnp.exp(w_log))
    a = jax.nn.sigmoid((a0 + ca @ a2).astype(f32))
    hd = lambda t: t.astype(f32).reshape(Bh, Th, RWKV_HEADS, RWKV_HEAD_DIM)
    kk = hd(k * k_k)
    kk = kk / jnp.maximum(jnp.sqrt(jnp.sum(kk * kk, axis=-1, keepdims=True)), 1e-12)
    k = hd(k * (1.0 + (a - 1.0) * k_a))
    r, v, decay, a = hd(r), hd(v), hd(decay), hd(a)

    def step(S, inp):
        r_t, k_t, v_t, w_t, kk_t, a_t = inp
        sa = jnp.einsum('bhvk,bhk->bhv', S, -kk_t)
        S = (S * w_t[:, :, None, :] + sa[..., None] * (kk_t * a_t)[:, :, None, :]
             + v_t[..., None] * k_t[:, :, None, :])
        return S, jnp.einsum('bhvk,bhk->bhv', S, r_t)

    tm = lambda t: jnp.swapaxes(t, 0, 1)
    S_fin, y = lax.scan(step, wkv0.astype(f32), (tm(r), tm(k), tm(v), tm(decay), tm(kk), tm(a)))
    y = tm(y)
    mean = jnp.mean(y, axis=-1, keepdims=True)
    var = jnp.mean(jnp.square(y - mean), axis=-1, keepdims=True)
    y = ((y - mean) * lax.rsqrt(var + GN_EPS) * gn_w.reshape(RWKV_HEADS, RWKV_HEAD_DIM)
         + gn_b.reshape(RWKV_HEADS, RWKV_HEAD_DIM))
    y = y + jnp.sum(r * k * r_k, axis=-1, keepdims=True) * v
    return y.reshape(Bh, Th, C).astype(sh.dtype), S_fin.astype(sh.dtype), sh[:, -1]


def indexer_scores(qi, wi, ki):
    dots = jnp.einsum('bqhd,bld->bqhl', qi, ki).astype(jnp.float32) * IDX_DIM ** -0.5
    return jnp.einsum('bqhl,bqh->bql', jax.nn.relu(dots), wi.astype(jnp.float32) * IDX_HEADS ** -0.5)


def sparse_attend(q, ks, vs, valid):
    s = jnp.einsum('bqhd,bqkhd->bqhk', q, ks).astype(jnp.float32) * ATT_HEAD_DIM ** -0.5
    s = jnp.where(valid[:, :, None, :], s, -jnp.inf)
    p = jax.nn.softmax(s, axis=-1).astype(vs.dtype)
    return jnp.einsum('bqhk,bqkhd->bqhd', p, vs)


_take_rows = jax.vmap(lambda rows, idx: rows[idx])


def dsa_prompt(q, k, v, qi, wi, ki, topk):
    S = q.shape[1]
    key_pos = jnp.arange(S)

    def block(start):
        sl = lambda t: lax.dynamic_slice_in_dim(t, start, Q_BLOCK, axis=1)
        qpos = start + jnp.arange(Q_BLOCK)
        score = indexer_scores(sl(qi), sl(wi), ki)
        score = jnp.where(key_pos[None, None, :] <= qpos[None, :, None], score, -jnp.inf)
        _, idx = lax.top_k(score, topk)
        valid = idx <= qpos[None, :, None]
        return sparse_attend(sl(q), _take_rows(k, idx), _take_rows(v, idx), valid)

    out = lax.map(block, jnp.arange(0, S, Q_BLOCK))
    return jnp.moveaxis(out, 0, 1).reshape(q.shape)


def dsa_sample(q, k, v, qi, wi, ki, cache_k, cache_v, cache_kidx, page_table, topk):
    DB, T = q.shape[:2]
    past = page_table.shape[1] * PAGE_SIZE
    ki_past = cache_kidx[page_table].reshape(DB, past, IDX_DIM).astype(ki.dtype)
    ki_all = jnp.concatenate([ki_past, ki], axis=1)
    qpos = past + jnp.arange(T)
    score = indexer_scores(qi, wi, ki_all)
    score = jnp.where(jnp.arange(past + T)[None, None, :] <= qpos[None, :, None], score, -jnp.inf)
    _, idx = lax.top_k(score, topk)
    valid = idx <= qpos[None, :, None]
    is_past = idx < past
    pidx = jnp.minimum(idx, past - 1)
    phys = _take_rows(page_table, pidx // PAGE_SIZE) * PAGE_SIZE + pidx % PAGE_SIZE
    nidx = jnp.clip(idx - past, 0, T - 1)

    def gather(pool, new):
        from_pool = pool.reshape(-1, ATT_HEADS, ATT_HEAD_DIM)[phys].astype(new.dtype)
        return jnp.where(is_past[..., None, None], from_pool, _take_rows(new, nidx))

    return sparse_attend(q, gather(cache_k, k), gather(cache_v, v), valid)


def memory_kv(mem, mem_norm_w, w_mem_kv):
    Bm, M, _ = mem.shape
    mk, mv = jnp.split(rmsnorm(mem, mem_norm_w) @ w_mem_kv, 2, axis=-1)
    return (mk.reshape(Bm, M, MEM_HEADS, MEM_HEAD_DIM), mv.reshape(Bm, M, MEM_HEADS, MEM_HEAD_DIM))


def mem_attend(q, mk, mv):
    s = jnp.einsum('bthd,bmhd->bhtm', q, mk.astype(q.dtype)).astype(jnp.float32) * MEM_HEAD_DIM ** -0.5
    p = jax.nn.softmax(s, axis=-1).astype(q.dtype)
    return jnp.einsum('bhtm,bmhd->bthd', p, mv.astype(q.dtype))


def mixer_layer(h, pos, prev_shift, wkv0, attend, mem_k, mem_v, norm_w, w_in, w_out, rwkv_params):
    Bh, Th, _ = h.shape
    xn = rmsnorm(h, norm_w)
    offsets = np.cumsum(IN_SPLITS)[:-1].tolist()
    (sh, g_r, q, k, v, g_a, qi, wi, ki, qm, g_m) = jnp.split(xn @ w_in, offsets, axis=-1)
    y_r, wkv, shift = rwkv_time_mix(sh, prev_shift, wkv0, *rwkv_params)
    y_r = y_r * jax.nn.silu(g_r)
    q = rope(q.reshape(Bh, Th, ATT_HEADS, ATT_HEAD_DIM), pos)
    k = rope(k.reshape(Bh, Th, ATT_HEADS, ATT_HEAD_DIM), pos)
    v = v.reshape(Bh, Th, ATT_HEADS, ATT_HEAD_DIM)
    qi = rope(qi.reshape(Bh, Th, IDX_HEADS, IDX_DIM), pos)
    ki = rope(ki[:, :, None, :], pos)[:, :, 0, :]
    y_a = attend(q, k, v, qi, wi, ki).reshape(Bh, Th, ATT_WIDTH) * jax.nn.silu(g_a)
    y_m = mem_attend(qm.reshape(Bh, Th, MEM_HEADS, MEM_HEAD_DIM), mem_k, mem_v).reshape(Bh, Th, MEM_WIDTH)
    y_m = y_m * jax.nn.silu(g_m)
    y = jnp.concatenate([y_r, y_a, y_m], axis=-1) @ w_out
    return h + y.astype(h.dtype), wkv, shift, k, v, ki


def setup_inputs(seed: int = 0) -> dict:
    key = jax.random.key(seed)
    ks = jax.random.split(key, 32)
    f32 = jnp.float32
    n_pages = PAST_LEN // PAGE_SIZE
    n_pool = (DEC_BATCH * n_pages * 5) // 4
    C = RWKV_WIDTH
    nrm = lambda k, shape, s=1.0: s * jax.random.normal(k, shape, f32)
    perm = jax.random.permutation(ks[0], n_pool)
    page_table = perm[:DEC_BATCH * n_pages].reshape(DEC_BATCH, n_pages).astype(jnp.int32)
    return {
        'x_prompt': nrm(ks[1], (BATCH, SEQ, D_MODEL)),
        'x_sample': nrm(ks[2], (DEC_BATCH, DEC_SEQ, D_MODEL)),
        'state_wkv': nrm(ks[3], (DEPTH, DEC_BATCH, RWKV_HEADS, RWKV_HEAD_DIM, RWKV_HEAD_DIM), 0.5),
        'state_shift': nrm(ks[4], (DEPTH, DEC_BATCH, SHIFT_WIDTH)),
        'cache_k': nrm(ks[5], (DEPTH, n_pool, PAGE_SIZE, ATT_HEADS, ATT_HEAD_DIM)),
        'cache_v': nrm(ks[6], (DEPTH, n_pool, PAGE_SIZE, ATT_HEADS, ATT_HEAD_DIM)),
        'cache_kidx': nrm(ks[7], (DEPTH, n_pool, PAGE_SIZE, IDX_DIM)),
        'cache_mem_k': nrm(ks[8], (DEPTH, DEC_BATCH, MEM_TOKENS, MEM_HEADS, MEM_HEAD_DIM)),
        'cache_mem_v': nrm(ks[9], (DEPTH, DEC_BATCH, MEM_TOKENS, MEM_HEADS, MEM_HEAD_DIM)),
        'page_table': page_table,
        'mem_prompt': nrm(ks[10], (BATCH, MEM_TOKENS, D_MODEL)),
        'norm_w': 1.0 + nrm(ks[11], (DEPTH, D_MODEL), 0.02),
        'w_in': nrm(ks[12], (DEPTH, D_MODEL, IN_WIDTH), D_MODEL ** -0.5),
        'shift_mu': jax.random.uniform(ks[13], (DEPTH, SHIFT_WIDTH), f32),
        'w0': jax.random.uniform(ks[14], (DEPTH, C), f32, -6.0, 0.0),
        'w2': nrm(ks[15], (DEPTH, DECAY_LORA, C), 0.1),
        'a0': nrm(ks[16], (DEPTH, C), 0.1),
        'a2': nrm(ks[17], (DEPTH, ICLR_LORA, C), 0.1),
        'k_k': 0.85 + nrm(ks[18], (DEPTH, C), 0.02),
        'k_a': 1.0 + nrm(ks[19], (DEPTH, C), 0.02),
        'r_k': nrm(ks[20], (DEPTH, RWKV_HEADS, RWKV_HEAD_DIM), 0.1),
        'gn_w': 1.0 + nrm(ks[21], (DEPTH, C), 0.02),
        'gn_b': nrm(ks[22], (DEPTH, C), 0.02),
        'mem_norm_w': 1.0 + nrm(ks[23], (DEPTH, D_MODEL), 0.02),
        'w_mem_kv': nrm(ks[24], (DEPTH, D_MODEL, 2 * MEM_WIDTH), D_MODEL ** -0.5),
        'w_out': nrm(ks[25], (DEPTH, MIX_WIDTH, D_MODEL), MIX_WIDTH ** -0.5),
        'final_norm_w': 1.0 + nrm(ks[26], (D_MODEL,), 0.02),
    }


def reference(x_prompt, x_sample, state_wkv, state_shift, cache_k, cache_v, cache_kidx,
              cache_mem_k, cache_mem_v, page_table, mem_prompt, norm_w, w_in, shift_mu,
              w0, w2, a0, a2, k_k, k_a, r_k, gn_w, gn_b, mem_norm_w, w_mem_kv, w_out,
              final_norm_w):
    B, S, _ = x_prompt.shape
    DB, T, _ = x_sample.shape
    past = page_table.shape[1] * PAGE_SIZE
    pos_p = jnp.arange(S)
    pos_s = past + jnp.arange(T)
    topk_p = min(TOPK_MAX, S // 4)
    topk_s = min(TOPK_MAX, (past + T) // 4)
    hp, hs = x_prompt, x_sample
    wkv_p, shift_p, k_p, v_p, kidx_p, memk_p, memv_p = [], [], [], [], [], [], []
    wkv_s, shift_s, k_s, v_s, kidx_s = [], [], [], [], []
    for l in range(DEPTH):
        rw = (shift_mu[l], w0[l], w2[l], a0[l], a2[l], k_k[l], k_a[l], r_k[l], gn_w[l], gn_b[l])
        mk, mv = memory_kv(mem_prompt, mem_norm_w[l], w_mem_kv[l])
        hp, n_wkv, n_shift, n_k, n_v, n_ki = mixer_layer(
            hp, pos_p, jnp.zeros((B, SHIFT_WIDTH), hp.dtype),
            jnp.zeros((B, RWKV_HEADS, RWKV_HEAD_DIM, RWKV_HEAD_DIM), jnp.float32),
            functools.partial(dsa_prompt, topk=topk_p), mk, mv,
            norm_w[l], w_in[l], w_out[l], rw)
        wkv_p.append(n_wkv); shift_p.append(n_shift); k_p.append(n_k); v_p.append(n_v)
        kidx_p.append(n_ki); memk_p.append(mk); memv_p.append(mv)
        attend_s = functools.partial(dsa_sample, cache_k=cache_k[l], cache_v=cache_v[l],
                                     cache_kidx=cache_kidx[l], page_table=page_table, topk=topk_s)
        hs, n_wkv, n_shift, n_k, n_v, n_ki = mixer_layer(
            hs, pos_s, state_shift[l], state_wkv[l], attend_s, cache_mem_k[l], cache_mem_v[l],
            norm_w[l], w_in[l], w_out[l], rw)
        wkv_s.append(n_wkv); shift_s.append(n_shift); k_s.append(n_k); v_s.append(n_v)
        kidx_s.append(n_ki)
    y_prompt = rmsnorm(hp, final_norm_w)
    y_sample = rmsnorm(hs, final_norm_w)
    new_wkv_prompt = jnp.stack(wkv_p)
    new_shift_prompt = jnp.stack(shift_p)
    new_k_prompt = jnp.stack(k_p)
    new_v_prompt = jnp.stack(v_p)
    new_kidx_prompt = jnp.stack(kidx_p)
    new_mem_k_prompt = jnp.stack(memk_p)
    new_mem_v_prompt = jnp.stack(memv_p)
    new_wkv_sample = jnp.stack(wkv_s)
    new_shift_sample = jnp.stack(shift_s)
    new_k_sample = jnp.stack(k_s)
    new_v_sample = jnp.stack(v_s)
    new_kidx_sample = jnp.stack(kidx_s)
    return (y_prompt, y_sample, new_wkv_prompt, new_shift_prompt, new_k_prompt, new_v_prompt,
            new_kidx_prompt, new_mem_k_prompt, new_mem_v_prompt, new_wkv_sample,
            new_shift_sample, new_k_sample, new_v_sample, new_kidx_sample)
```

```python
import numpy as np
import ml_dtypes
import concourse.bass as bass
import concourse.mybir as mybir
from concourse.bass_utils import run_bass_kernel_spmd

F32 = mybir.dt.float32
BF16 = mybir.dt.bfloat16
I32 = mybir.dt.int32
AF = mybir.ActivationFunctionType
ALU = mybir.AluOpType
AX = mybir.AxisListType

NBF = ml_dtypes.bfloat16

D = 2048
KT = 16
SEQ = 2048
TALL = 2304
NOWN = 1024
NOWNS = 1152
C = 64
SROWS = [0, 32, 64, 68]
EPS = 1e-6
GN_EPS = 64e-5


class _PoolEv:
    __slots__ = ("idx", "resolved")

    def __init__(self, idx):
        self.idx = idx
        self.resolved = None


class Sched:
    NP = 56

    def __init__(self, nc, ndma=24):
        self.nc = nc
        self.eng = {"pe": nc.tensor, "act": nc.scalar, "dve": nc.vector,
                    "pool": nc.gpsimd, "sync": nc.sync}
        self.prog = {k: [] for k in self.eng}
        self.cnt = {k: 0 for k in self.eng}
        self.waited = {k: {} for k in self.eng}
        self.res = {}
        self.sems = {}
        self.ndma = ndma
        self.dtot = [0] * ndma
        self.dnext = 0
        self.stack = []
        self.psum_names = set()
        self.ppend = [None] * self.NP
        self.pnext = 0
        self.nnote = 0
        self.pdone = []

    def open(self):
        names = list(self.eng) + ["pnote"]
        for k in names:
            cm = self.nc.semaphore("sem_" + k)
            self.sems[k] = cm.__enter__()
            self.stack.append(cm)
        for i in range(self.ndma):
            cm = self.nc.semaphore("semd%d" % i)
            self.sems[("d", i)] = cm.__enter__()
            self.stack.append(cm)
        for i in range(self.NP):
            cm = self.nc.semaphore("semp%d" % i)
            self.sems[("p", i)] = cm.__enter__()
            self.stack.append(cm)

    def close(self):
        for cm in reversed(self.stack):
            cm.__exit__(None, None, None)

    @staticmethod
    def key(x):
        if isinstance(x, tuple):
            return (x[0].tensor.name, x[1])
        if isinstance(x, str):
            return x
        return x.tensor.name

    def _note(self, pev):
        if pev.resolved is not None:
            return pev.resolved
        i = pev.idx
        psem = self.sems[("p", i)]
        nsem = self.sems["pnote"]

        def fn(eng):
            eng.sem_inc(psem, -16)
            eng.sem_inc(nsem, 1)
            return None
        self.prog["pool"].append(([(("p", i), 16)], fn, None))
        self.nnote += 1
        pev.resolved = ("pnote", self.nnote)
        self.ppend[i] = None
        return pev.resolved

    def _collect(self, e, reads, writes):
        need = {}

        def add(ev):
            if ev is None:
                return
            if isinstance(ev, _PoolEv):
                ev = self._note(ev)
            sk, val = ev
            if sk == "pe" and e == "pe":
                return
            if need.get(sk, 0) < val:
                need[sk] = val
        for k in reads:
            r = self.res.get(k)
            if r:
                add(r["w"])
                nm = k[0] if isinstance(k, tuple) else k
                if nm in self.psum_names:
                    for ev in r["r"]:
                        if isinstance(ev, _PoolEv) or ev[0] != e:
                            add(ev)
        for k in writes:
            r = self.res.get(k)
            if r:
                add(r["w"])
                for ev in r["r"]:
                    add(ev)
        out = []
        wd = self.waited[e]
        for sk, val in need.items():
            if wd.get(sk, 0) >= val:
                continue
            wd[sk] = val
            out.append((sk, val))
        return out

    def _record(self, ev, reads, writes):
        for k in reads:
            r = self.res.setdefault(k, {"w": None, "r": []})
            if isinstance(ev, _PoolEv):
                r["r"].append(ev)
            else:
                for j, old in enumerate(r["r"]):
                    if not isinstance(old, _PoolEv) and old[0] == ev[0]:
                        if old[1] < ev[1]:
                            r["r"][j] = ev
                        break
                else:
                    r["r"].append(ev)
        for k in writes:
            self.res[k] = {"w": ev, "r": []}

    def op(self, e, fn, reads=(), writes=()):
        reads = [self.key(x) for x in reads]
        writes = [self.key(x) for x in writes]
        waits = self._collect(e, reads, writes)
        self.cnt[e] += 1
        ev = (e, self.cnt[e])
        self.prog[e].append((waits, fn, (e, 1)))
        self._record(ev, reads, writes)

    def dma(self, q, fn, reads=(), writes=()):
        reads = [self.key(x) for x in reads]
        writes = [self.key(x) for x in writes]
        if q == "pool":
            i = self.pnext
            self.pnext += 1
            assert i < self.NP, "out of pool-DMA semaphores"
            sk = ("p", i)
            waits = self._collect(q, reads, writes)
            ev = (sk, 16)
            self.prog[q].append((waits, fn, (sk, 16)))
            self._record(ev, reads, writes)
            self.pdone.append(ev)
            return
        i = self.dnext
        self.dnext = (self.dnext + 1) % self.ndma
        sk = ("d", i)
        waits = self._collect(q, reads, writes)
        if self.dtot[i] > 0 and self.waited[q].get(sk, 0) < self.dtot[i]:
            self.waited[q][sk] = self.dtot[i]
            waits.append((sk, self.dtot[i]))
        self.dtot[i] += 16
        ev = (sk, self.dtot[i])
        self.prog[q].append((waits, fn, (sk, 16)))
        self._record(ev, reads, writes)

    def _all_events(self):
        for pev in self.ppend:
            if pev is not None:
                self._note(pev)
        allw = [(e, self.cnt[e]) for e in ("pe", "act", "dve", "pool") if self.cnt[e] > 0]
        allw += [(("d", i), self.dtot[i]) for i in range(self.ndma) if self.dtot[i] > 0]
        allw += list(self.pdone)
        return allw

    def barrier(self):
        allw = self._all_events()
        for q in self.eng:
            waits = []
            for sk, val in allw:
                if sk == q and q == "pe":
                    continue
                if self.waited[q].get(sk, 0) < val:
                    self.waited[q][sk] = val
                    waits.append((sk, val))
            if waits:
                self.prog[q].append((waits, None, None))

    def finish(self):
        allw = self._all_events()
        waits = [(sk, val) for sk, val in allw if self.waited["sync"].get(sk, 0) < val]
        self.prog["sync"].append((waits, None, None))

    def flush(self):
        block = self.block

        def runner(name):
            items = self.prog[name]
            self.prog[name] = []

            def body(eng):
                for waits, fn, inc in items:
                    for sk, val in waits:
                        eng.wait_ge(self.sems[sk], val)
                    if fn is not None:
                        ins = fn(eng)
                        if inc is not None:
                            ins.then_inc(self.sems[inc[0]], inc[1])
            return body
        block.sync(runner("sync"))
        block.tensor(runner("pe"))
        block.scalar(runner("act"))
        block.vector(runner("dve"))
        block.gpsimd(runner("pool"))


class KB:
    def __init__(self, nc):
        self.nc = nc
        self.S = Sched(nc)
        self.cms = []

    def sb(self, name, shape, dt):
        cm = self.nc.sbuf_tensor(name, list(shape), dt)
        t = cm.__enter__()
        self.cms.append(cm)
        return t

    def ps(self, name, shape, dt=F32):
        cm = self.nc.psum_tensor(name, list(shape), dt)
        t = cm.__enter__()
        self.cms.append(cm)
        self.S.psum_names.add(name)
        return t

    def mark(self):
        return len(self.cms)

    def release(self, mark):
        self.S.barrier()
        self.S.flush()
        while len(self.cms) > mark:
            self.cms.pop().__exit__(None, None, None)

    @staticmethod
    def ap(x):
        return x[0] if isinstance(x, tuple) else x

    def mm(self, out, lhsT, rhs, start=True, stop=True, extra_r=()):
        o, l, r = self.ap(out), self.ap(lhsT), self.ap(rhs)
        self.S.op("pe", lambda eng: eng.matmul(o, l, r, start=start, stop=stop),
                  reads=[lhsT, rhs] + list(extra_r), writes=[out])

    def tr(self, out, in_, ident):
        o, i, d = self.ap(out), self.ap(in_), self.ap(ident)
        self.S.op("pe", lambda eng: eng.transpose(o, i, d), reads=[in_], writes=[out])

    def act(self, out, in_, func, bias=None, scale=None, accum=None, e="act"):
        o, i = self.ap(out), self.ap(in_)
        kw = {}
        reads = [in_]
        writes = [out]
        if bias is not None:
            kw["bias"] = self.ap(bias) if not isinstance(bias, float) else bias
            if not isinstance(bias, float):
                reads.append(bias)
        if scale is not None:
            kw["scale"] = self.ap(scale) if not isinstance(scale, float) else scale
            if not isinstance(scale, float):
                reads.append(scale)
        if accum is not None:
            kw["accum_out"] = self.ap(accum)
            writes.append(accum)
        self.S.op("act", lambda eng: eng.activation(o, i, func, **kw), reads=reads, writes=writes)

    def tt(self, out, in0, in1, op, e="dve"):
        o, a, b = self.ap(out), self.ap(in0), self.ap(in1)
        self.S.op(e, lambda eng: eng.tensor_tensor(o, a, b, op), reads=[in0, in1], writes=[out])

    def ts(self, out, in0, s1, op0, s2=None, op1=None, accum=None, e="dve"):
        o, a = self.ap(out), self.ap(in0)
        reads = [in0]
        writes = [out]
        v1 = s1
        v2 = s2
        if not isinstance(s1, (int, float)):
            v1 = self.ap(s1)
            reads.append(s1)
        if s2 is not None and not isinstance(s2, (int, float)):
            v2 = self.ap(s2)
            reads.append(s2)
        kw = {}
        if accum is not None:
            kw["accum_out"] = self.ap(accum)
            writes.append(accum)
        if op1 is None:
            self.S.op(e, lambda eng: eng.tensor_scalar(o, a, v1, None, op0, **kw), reads=reads, writes=writes)
        else:
            self.S.op(e, lambda eng: eng.tensor_scalar(o, a, v1, v2, op0, op1, **kw), reads=reads, writes=writes)

    def stt(self, out, in0, scalar, in1, op0, op1, accum=None):
        o, a, b = self.ap(out), self.ap(in0), self.ap(in1)
        reads = [in0, in1]
        writes = [out]
        sv = scalar
        if not isinstance(scalar, (int, float)):
            sv = self.ap(scalar)
            reads.append(scalar)
        kw = {}
        if accum is not None:
            kw["accum_out"] = self.ap(accum)
            writes.append(accum)
        self.S.op("dve", lambda eng: eng.scalar_tensor_tensor(o, a, sv, b, op0, op1, **kw),
                  reads=reads, writes=writes)

    def copy(self, out, in_, e="dve"):
        o, i = self.ap(out), self.ap(in_)
        if e == "act":
            self.S.op("act", lambda eng: eng.copy(o, i), reads=[in_], writes=[out])
        else:
            self.S.op(e, lambda eng: eng.tensor_copy(o, i), reads=[in_], writes=[out])

    def memset(self, out, val, e="dve"):
        o = self.ap(out)
        self.S.op(e, lambda eng: eng.memset(o, val), reads=[], writes=[out])

    def recip(self, out, in_):
        o, i = self.ap(out), self.ap(in_)
        self.S.op("dve", lambda eng: eng.reciprocal(o, i), reads=[in_], writes=[out])

    def reduce(self, out, in_, op, axis=AX.X):
        o, i = self.ap(out), self.ap(in_)
        self.S.op("dve", lambda eng: eng.tensor_reduce(o, i, axis, op), reads=[in_], writes=[out])

    def scan(self, out, d0, d1, init, op0, op1):
        o, a, b = self.ap(out), self.ap(d0), self.ap(d1)
        self.S.op("dve", lambda eng: eng.tensor_tensor_scan(o, a, b, init, op0, op1),
                  reads=[d0, d1], writes=[out])

    def cpred(self, out, mask, data):
        o, m, d = self.ap(out), self.ap(mask), self.ap(data)
        self.S.op("dve", lambda eng: eng.copy_predicated(o, m, d), reads=[mask, data, out], writes=[out])

    def ttr(self, out, in0, in1, op0, op1, accum, scale=1.0, scalar=0.0):
        o, a, b, ac = self.ap(out), self.ap(in0), self.ap(in1), self.ap(accum)
        self.S.op("dve", lambda eng: eng.tensor_tensor_reduce(o, a, b, op0, op1, scale, scalar, accum_out=ac)
                  if False else eng.tensor_tensor_reduce(out=o, in0=a, in1=b, op0=op0, op1=op1,
                                                         scale=scale, scalar=scalar, accum_out=ac),
                  reads=[in0, in1], writes=[out, accum])

    def vmax(self, out, in_):
        o, i = self.ap(out), self.ap(in_)
        self.S.op("dve", lambda eng: eng.max(o, i), reads=[in_], writes=[out])

    def vmaxidx(self, out, in_max, in_values):
        o, m, v = self.ap(out), self.ap(in_max), self.ap(in_values)
        self.S.op("dve", lambda eng: eng.max_index(o, m, v), reads=[in_max, in_values], writes=[out])

    def vmr(self, out, in_rep, in_values, imm):
        o, r, v = self.ap(out), self.ap(in_rep), self.ap(in_values)
        self.S.op("dve", lambda eng: eng.match_replace(o, r, v, imm), reads=[in_rep, in_values], writes=[out])

    def dma(self, out, in_, q="sync", **kw):
        o, i = self.ap(out), self.ap(in_)
        self.S.dma(q, lambda eng: eng.dma_start(out=o, in_=i, **kw), reads=[in_], writes=[out])

    def load_w(self, dst, src, rows, free_shape, key=None):
        n = 1
        for d_ in free_shape:
            n *= d_
        assert n <= 2048, n
        st = self._wst[self._wsi % len(self._wst)]
        self._wsi += 1
        sv = st[0:rows, 0:n]
        if len(free_shape) == 2:
            sv = sv.rearrange("p (a b) -> p a b", b=free_shape[1])
        self.dma(sv, src)
        self.copy(dst if key is None else (dst, key), sv, e="pool")

    def gather(self, out, in_, idx, bounds_check=None, **kw):
        o, i, ix = self.ap(out), self.ap(in_), self.ap(idx)
        regs = self.__dict__.setdefault("_bregs", {})

        def fn(eng):
            extra = dict(kw)
            if bounds_check is not None:
                if bounds_check not in regs:
                    regs[bounds_check] = eng.to_reg(bounds_check)
                extra["bounds_check"] = regs[bounds_check]
            return eng.indirect_dma_start(out=o, out_offset=None, in_=i,
                                          in_offset=bass.IndirectOffsetOnAxis(ap=ix, axis=0), **extra)
        self.S.dma("pool", fn, reads=[in_, idx], writes=[out])


def build_program(stages):
    nc = bass.Bass("TRN2", target_bir_lowering=False)
    kb = KB(nc)
    kb.S.open()
    blk_cm = nc.Block()
    kb.S.block = blk_cm.__enter__()
    dram = {}

    def din(name, shape, dt=F32):
        dram[name] = nc.dram_tensor(name, list(shape), dt, kind="ExternalInput").ap()
        return dram[name]

    def dout(name, shape, dt=F32):
        dram[name] = nc.dram_tensor(name, list(shape), dt, kind="ExternalOutput").ap()
        return dram[name]

    xT_all = din("xT_all", [D, TALL])
    c_onesb = din("c_onesb", [128, 128], BF16)
    c_identb = din("c_identb", [128, 128], BF16)
    p_normw = din("p_normw", [128, KT])
    w_kv = din("w_kv", [D, 1152])
    rope_all = din("rope_all", [128, 18, 2, 64])
    rope_idx = din("rope_idx", [128, 18, 2, 32])
    o_k = dout("o_k", [TALL, 512])
    o_v = dout("o_v", [TALL, 512])
    o_ki = dout("o_ki", [TALL, 64])

    onesb = kb.sb("onesb", [128, 128], BF16)
    identb = kb.sb("identb", [128, 128], BF16)
    normw = kb.sb("normw", [128, KT], F32)
    kb.dma(onesb[:], c_onesb)
    kb.dma(identb[:], c_identb)
    kb.dma(normw[:], p_normw)

    kb._wst = [kb.sb("wstage%d" % i, [128, 2048], F32) for i in range(1)]
    kb._wsi = 0
    yT = kb.sb("yT", [128, 8, NOWN], BF16)
    yTs = kb.sb("yTs", [128, 8, 128], BF16)
    kb.memset(yTs[:], 0.0)
    kT = kb.sb("kT", [128, 4, TALL], BF16)
    vtok = kb.sb("vtok", [128, 18, 512], BF16)
    kiT = kb.sb("kiT", [128, TALL], BF16)
    m_x = kb.mark()
    xnT = kb.sb("xnT", [128, KT, TALL], BF16)
    m_a1 = kb.mark()
    xstage = [kb.sb("xstage%d" % i, [128, KT, 256], F32) for i in range(2)]
    sq = [kb.sb("sq%d" % i, [128, 256], BF16) for i in range(2)]
    rstd = kb.sb("rstd", [128, 256], F32)
    ps_ss = kb.ps("ps_ss", [128, 512])
    xT_v = xT_all.rearrange("(kt p) t -> p kt t", p=128)
    blocks = [(0, 512), (512, 512), (1024, 512), (1536, 512), (2048, 256)]
    for bi, t0 in enumerate(range(0, TALL, 256)):
        nb = 256
        xs = xstage[bi % 2]
        for half in range(2):
            kb.dma((xs[:, half * 8:(half + 1) * 8, 0:nb], half), xT_v[:, half * 8:(half + 1) * 8, t0:t0 + nb])
        for kt in range(KT):
            s = sq[kt % 2]
            kb.act(s[:, 0:nb], (xs[:, kt, 0:nb], kt // 8), AF.Square)
            kb.mm(ps_ss[:, 0:nb], onesb[:], s[:, 0:nb], start=(kt == 0), stop=(kt == KT - 1))
        kb.act(rstd[:, 0:nb], ps_ss[:, 0:nb], AF.Sqrt, bias=EPS, scale=1.0 / D)
        kb.recip(rstd[:, 0:nb], rstd[:, 0:nb])
        for kt in range(KT):
            kb.stt(xnT[:, kt, t0:t0 + nb], (xs[:, kt, 0:nb], kt // 8), normw[:, kt:kt + 1], rstd[:, 0:nb],
                   ALU.mult, ALU.mult)
    kb.release(m_a1)


    if "a3" in stages:
        m_a3 = kb.mark()
        PS = [kb.ps("PS%d" % i, [128, 512]) for i in range(6)]
        PSB = [kb.ps("PSB%d" % i, [128, 1024], BF16) for i in range(2)]
        cst = {}
        for nm, shp, dt in [("c_blkb", [128, 128], BF16), ("c_mask12", [128, 2, 2, 128], F32),
                            ("c_mask3", [128, 2, 128], F32), ("c_identbd", [128, 2, 128], F32),
                            ("c_resetm", [128, 256], F32), ("c_padm", [128, 256], F32),
                            ("c_emask", [128, 2, 128], I32),
                            ("p_mu", [128, 25], F32), ("p_w0", [128, 8], F32), ("p_a0", [128, 8], F32),
                            ("p_kk", [128, 8], F32), ("p_ka", [128, 8], F32), ("p_rk", [128, 8], F32),
                            ("p_gnw", [128, 8, 64], F32), ("p_gnb", [128, 8, 64], F32),
                            ("sprevT", [128, 25, 4], F32)]:
            d = din(nm, shp, dt)
            t = kb.sb("s_" + nm, shp, dt)
            kb.dma(t[:], d)
            cst[nm] = t
        blkb, mask12, mask3, identbd = cst["c_blkb"], cst["c_mask12"], cst["c_mask3"], cst["c_identbd"]
        resetm, padm, emask = cst["c_resetm"], cst["c_padm"], cst["c_emask"]
        mu, sprevT = cst["p_mu"], cst["sprevT"]
        omm = kb.sb("omm", [128, 25], F32)
        kb.ts(omm[:], mu[:], -1.0, ALU.mult, 1.0, ALU.add)
        w0h = kb.sb("w0h", [128, 8], F32)
        a0h = kb.sb("a0h", [128, 8], F32)
        omka = kb.sb("omka", [128, 8], F32)
        kb.ts(w0h[:], cst["p_w0"][:], 0.5, ALU.mult)
        kb.ts(a0h[:], cst["p_a0"][:], 0.5, ALU.mult)
        kb.ts(omka[:], cst["p_ka"][:], -1.0, ALU.mult, 1.0, ALU.add)
        w_c = din("w_c", [D, 128])
        w_rkv = din("w_rkv", [8, D, 384])
        w2a2_d = din("w2a2", [128, 1024])
        st_wkvT = din("st_wkvT", [4, 8, 128, 64])
        o_shift = dout("o_shift", [128, 25, 5])
        o_wkv = dout("o_wkv", [5, 8, 128, 64])
        w2a2 = kb.sb("w2a2s", [128, 1024], BF16)
        kb.load_w(w2a2[:], w2a2_d, 128, [1024])
        shst = kb.sb("shst", [128, 25, 5], F32)

        lora_in = kb.sb("lora_in", [128, TALL], BF16)
        m_a2 = kb.mark()
        wc = kb.sb("wc", [128, KT, 128], BF16)
        kb.load_w(wc[:], w_c.rearrange("(kt p) c -> p kt c", p=128), 128, [KT, 128])
        sh24 = kb.sb("sh24", [128, TALL + 1], F32)
        tmp24 = kb.sb("tmp24", [128, TALL], F32)
        xs24 = kb.sb("xs24", [128, TALL], F32)
        kb.memset(sh24[:, 0:1], 0.0)
        for bi, (t0, nb) in enumerate(blocks):
            pp = PS[bi % 2]
            for kt in range(KT):
                kb.mm(pp[:, 0:nb], wc[:, kt, :], xnT[:, kt, t0:t0 + nb], start=(kt == 0), stop=(kt == KT - 1))
            kb.copy(sh24[:, 1 + t0:1 + t0 + nb], pp[:, 0:nb], e="act")
        kb.ts(tmp24[:], sh24[:, 1:TALL + 1], omm[:, 24:25], ALU.mult)
        kb.stt(xs24[:], sh24[:, 0:TALL], mu[:, 24:25], tmp24[:], ALU.mult, ALU.add)
        kb.stt(xs24[:, SEQ:TALL:64], sprevT[:, 24, :], mu[:, 24:25], tmp24[:, SEQ:TALL:64], ALU.mult, ALU.add)
        kb.copy(shst[:, 24, 0:1], sh24[:, SEQ:SEQ + 1])
        kb.copy(shst[:, 24, 1:5], sh24[:, 1 + SEQ + 3:1 + TALL:64])
        kb.act(lora_in[0:64, :], xs24[0:64, :], AF.Tanh)
        kb.copy(lora_in[64:128, :], xs24[64:128, :], e="act")
        kb.release(m_a2)

        wr = [kb.sb("wr0", [128, KT, 384], BF16)] * 2
        blocks3 = [(t0_, 256) for t0_ in range(0, SEQ, 256)] + [(SEQ, 256)]
        LASTP, SAMPB = 7, 8
        raw = [kb.sb("raw0", [128, 3, 257], F32)] * 2
        carry = kb.sb("carry", [128, 3, 1], F32)
        xs = kb.sb("xs", [128, 3, 256], F32)
        f = {nm: kb.sb("f_" + nm, [128, 256], F32) for nm in
             ["logw", "a", "kk", "nrm", "kmod", "beta", "cum", "Pinv", "Pex"]}
        f["P"] = f["nrm"]
        f["btf"] = f["cum"]
        f["ktf"] = f["a"]
        f["thw"] = f["logw"]
        f["kkn"] = f["kk"]
        f["t1"] = f["kmod"]
        f["dl"] = f["Pex"]
        kksq = kb.sb("kksq", [128, 256], BF16)
        NCH = 4
        ARbd = kb.sb("ARbd", [128, NCH, 2, 128], BF16)
        Btbd = kb.sb("Btbd", [128, NCH, 128], BF16)
        Ktbd = kb.sb("Ktbd", [128, NCH, 128], BF16)
        Bhbd = kb.sb("Bhbd", [128, NCH, 128], BF16)
        Khbd = kb.sb("Khbd", [128, NCH, 128], BF16)
        Vbd = kb.sb("Vbd", [128, NCH, 128], BF16)
        prodbd = kb.sb("prodbd", [128, NCH, 128], BF16)
        ybd = kb.sb("ybd", [128, 4, 128], BF16)
        for t_ in (ARbd, Btbd, Ktbd, Bhbd, Khbd, Vbd, prodbd, ybd):
            kb.memset(t_[:], 0.0, e="pool")
        BhS = kb.sb("BhS", [128, NCH, 128], BF16)
        KhS = kb.sb("KhS", [128, NCH, 128], BF16)
        VS = kb.sb("VS", [128, NCH, 64], BF16)
        S1 = kb.sb("S1", [128, 2, 2, 128], BF16)
        S2 = kb.sb("S2", [128, 2, 2, 128], BF16)
        XX = [kb.sb("XX%d" % i, [128, 2, 2, 128], BF16) for i in range(2)]
        TT = [kb.sb("TT%d" % i, [128, 2, 128], BF16) for i in range(2)]
        Z = kb.sb("Z", [128, 64], F32)
        Zb = kb.sb("Zb", [128, 64], BF16)
        Zs = kb.sb("Zs", [128, 4, 64], F32)
        Wb = kb.sb("Wb", [128, 64], BF16)
        Ub = kb.sb("Ub", [128, 64], BF16)
        Ybuf = kb.sb("Ybuf", [128, NCH, 64], F32)
        cen = kb.sb("cen", [128, NCH, 64], F32)
        csq = Ybuf
        st8 = {nm: kb.sb("st_" + nm, [128, NCH], F32) for nm in ["mean", "var", "rstd", "bon"]}
        yfin = Ybuf

        def cview(t_, nb):
            return t_[:, 0:nb].rearrange("p (c t) -> p c t", t=64)

        import os as _os
        for p in range(int(_os.environ.get('A3_PAIRS', '8'))):
            wrp = wr[p % 2]
            wv = w_rkv[p].rearrange("(kt p) c -> p kt c", p=128)
            for q4 in range(4):
                kb.load_w(wrp[:, q4 * 4:(q4 + 1) * 4, :], wv[:, q4 * 4:(q4 + 1) * 4, :], 128, [4, 384], key=q4 // 2)
            kb.memset(carry[:], 0.0)
            kb.memset(Z[:], 0.0)
            kb.memset(Zb[:], 0.0)
            kb.dma(Zs[:], st_wkvT[:, p].rearrange("s q v -> q s v"))
            for bi, (t0, nb) in enumerate(blocks3):
                nch = nb // 64
                rw = raw[bi % 2]
                samp = (bi == SAMPB)
                for j in range(3):
                    pp = PS[j % 2]
                    for kt in range(KT):
                        kb.mm(pp[:, 0:nb], (wrp[:, kt, j * 128:(j + 1) * 128], kt // 8), xnT[:, kt, t0:t0 + nb],
                              start=(kt == 0), stop=(kt == KT - 1))
                    kb.copy(rw[:, j, 0:1], carry[:, j, :])
                    kb.copy(rw[:, j, 1:1 + nb], pp[:, 0:nb], e="act")
                    tj = j * 8 + p
                    tmpj = f[("P", "Pinv", "Pex")[j]]
                    kb.ts(tmpj[:, 0:nb], rw[:, j, 1:1 + nb], omm[:, tj:tj + 1], ALU.mult)
                    kb.stt(xs[:, j, 0:nb], rw[:, j, 0:nb], mu[:, tj:tj + 1], tmpj[:, 0:nb], ALU.mult, ALU.add)
                    if samp:
                        kb.stt(xs[:, j, 0:nb:64], sprevT[:, tj, :], mu[:, tj:tj + 1], tmpj[:, 0:nb:64],
                               ALU.mult, ALU.add)
                        kb.copy(shst[:, tj, 1:5], rw[:, j, 4:1 + nb:64])
                        if j > 0:
                            kb.tt(xs[:, j, 0:nb], xs[:, j, 0:nb], padm[:, 0:nb], ALU.mult)
                    else:
                        kb.copy(carry[:, j, :], rw[:, j, nb:nb + 1])
                        if bi == LASTP:
                            kb.copy(shst[:, tj, 0:1], rw[:, j, nb:nb + 1])
                xr, xk, xv = xs[:, 0, 0:nb], xs[:, 1, 0:nb], xs[:, 2, 0:nb]
                kb.mm(PS[0][:, 0:nb], w2a2[0:64, p * 128:(p + 1) * 128], lora_in[0:64, t0:t0 + nb])
                kb.mm(PS[1][:, 0:nb], w2a2[64:128, p * 128:(p + 1) * 128], lora_in[64:128, t0:t0 + nb])
                kb.act(f["thw"][:, 0:nb], PS[0][:, 0:nb], AF.Tanh, bias=w0h[:, p:p + 1], scale=0.5)
                kb.ts(f["logw"][:, 0:nb], f["thw"][:, 0:nb], -0.30326533, ALU.mult, -0.30326533, ALU.add)
                if samp:
                    kb.tt(f["logw"][:, 0:nb], f["logw"][:, 0:nb], padm[:, 0:nb], ALU.mult)
                kb.act(f["a"][:, 0:nb], PS[1][:, 0:nb], AF.Tanh, bias=a0h[:, p:p + 1], scale=0.5)
                kb.ts(f["a"][:, 0:nb], f["a"][:, 0:nb], 0.5, ALU.mult, 0.5, ALU.add)
                kb.ts(f["kk"][:, 0:nb], xk, cst["p_kk"][:, p:p + 1], ALU.mult)
                kb.act(kksq[:, 0:nb], f["kk"][:, 0:nb], AF.Square)
                kb.mm(PS[0][:, 0:nb], blkb[:], kksq[:, 0:nb])
                kb.act(f["nrm"][:, 0:nb], PS[0][:, 0:nb], AF.Sqrt)
                kb.ts(f["nrm"][:, 0:nb], f["nrm"][:, 0:nb], 1e-12, ALU.max)
                kb.recip(f["nrm"][:, 0:nb], f["nrm"][:, 0:nb])
                kb.tt(f["kkn"][:, 0:nb], f["kk"][:, 0:nb], f["nrm"][:, 0:nb], ALU.mult)
                kb.ts(f["t1"][:, 0:nb], f["a"][:, 0:nb], cst["p_ka"][:, p:p + 1], ALU.mult, omka[:, p:p + 1], ALU.add)
                kb.tt(f["kmod"][:, 0:nb], xk, f["t1"][:, 0:nb], ALU.mult)
                kb.tt(f["beta"][:, 0:nb], f["kkn"][:, 0:nb], f["a"][:, 0:nb], ALU.mult)
                kb.scan(f["cum"][:, 0:nb], resetm[:, 0:nb], f["logw"][:, 0:nb], 0.0, ALU.mult, ALU.add)
                kb.act(f["P"][:, 0:nb], f["cum"][:, 0:nb], AF.Exp)
                kb.act(f["Pinv"][:, 0:nb], f["cum"][:, 0:nb], AF.Exp, scale=-1.0)
                kb.tt(f["dl"][:, 0:nb], f["cum"][:, 0:nb], f["logw"][:, 0:nb], ALU.subtract)
                kb.act(f["Pex"][:, 0:nb], f["dl"][:, 0:nb], AF.Exp)
                kb.tt(f["btf"][:, 0:nb], f["beta"][:, 0:nb], f["Pinv"][:, 0:nb], ALU.mult)
                kb.tt(f["ktf"][:, 0:nb], f["kmod"][:, 0:nb], f["Pinv"][:, 0:nb], ALU.mult)
                PCb = cview(f["P"], nb)[:, :, 63:64].to_broadcast([128, nch, 64])
                for h2 in range(2):
                    hs = slice(h2 * 64, h2 * 64 + 64)
                    cs = slice(h2 * 64, h2 * 64 + 64)
                    kb.stt(ARbd[hs, 0:nch, 0, cs], cview(f["kkn"], nb)[hs], -1.0, cview(f["Pex"], nb)[hs],
                           ALU.mult, ALU.mult)
                    kb.tt(ARbd[hs, 0:nch, 1, cs], cview(xs[:, 0, :], nb)[hs], cview(f["P"], nb)[hs], ALU.mult,
                          e="pool")
                    kb.copy(Btbd[hs, 0:nch, cs], cview(f["btf"], nb)[hs], e="pool")
                    kb.copy(Ktbd[hs, 0:nch, cs], cview(f["ktf"], nb)[hs], e="pool")
                    kb.tt(Bhbd[hs, 0:nch, cs], cview(f["btf"], nb)[hs], PCb[hs], ALU.mult)
                    kb.tt(Khbd[hs, 0:nch, cs], cview(f["ktf"], nb)[hs], PCb[hs], ALU.mult, e="pool")
                    kb.copy(Vbd[hs, 0:nch, cs], cview(xs[:, 2, :], nb)[hs], e="pool")
                    kb.stt(prodbd[hs, 0:nch, cs], cview(xs[:, 0, :], nb)[hs], cst["p_rk"][hs, p:p + 1],
                           cview(f["kmod"], nb)[hs], ALU.mult, ALU.mult)
                for src, dst in ((Bhbd, BhS), (Khbd, KhS)):
                    pb = PSB[0]
                    for c in range(nch):
                        kb.tr(pb[:, c * 128:(c + 1) * 128], src[:, c, :], identb[:])
                    kb.copy(dst[:, 0:nch, :].rearrange("p c k -> p (c k)"), pb[:, 0:nch * 128], e="act")
                pb = PSB[1]
                for c in range(nch):
                    kb.tr(pb[:, c * 128:(c + 1) * 128], Vbd[:, c, :], identb[:])
                for h2 in range(2):
                    hs = slice(h2 * 64, h2 * 64 + 64)
                    kb.copy(VS[hs, 0:nch, :], pb[hs, 0:nch * 128].rearrange("p (c k) -> p c k", k=128)[:, :, h2 * 64:h2 * 64 + 64],
                            e="act")
                for c in range(nch):
                    kb.mm(PS[5][:, 448 + c * 2:450 + c * 2], prodbd[:, c, :], onesb[:, 0:2])
                kb.copy(st8["bon"][:, 0:nch], PS[5][:, 448:448 + 2 * nch:2])
                for cg in range(nch // 2):
                    ps1 = PS[2][:].rearrange("p (g w k) -> p g w k", w=2, g=2)
                    ps2 = PS[3][:].rearrange("p (g w k) -> p g w k", w=2, g=2)
                    ps3 = PS[4][:, 0:256].rearrange("p (g k) -> p g k", g=2)
                    pB = PS[4][:, 256:512].rearrange("p (g k) -> p g k", g=2)
                    for g in range(2):
                        c = cg * 2 + g
                        kb.mm(ps1[:, g, :, :], Btbd[:, c, :], ARbd[:, c, :, :])
                        kb.mm(ps2[:, g, :, :], Ktbd[:, c, :], ARbd[:, c, :, :])
                        kb.mm(ps3[:, g, :], ARbd[:, c, 0, :], Btbd[:, c, :])
                    kb.tt(S1[:], ps1, mask12[:], ALU.mult)
                    kb.tt(S2[:], ps2, mask12[:], ALU.mult)
                    X0 = XX[0]
                    kb.tt(X0[:, :, 0, :], ps3, mask3[:], ALU.mult)
                    kb.copy(X0[:, :, 1, :], S1[:, :, 0, :], e="pool")
                    kb.tt(TT[0][:], S1[:, :, 0, :], identbd[:], ALU.add, e="pool")
                    cur = 0
                    for lvl in range(5):
                        Xc, Xn = XX[cur], XX[1 - cur]
                        Tc, Tn = TT[cur], TT[1 - cur]
                        pA = PS[cur][:].rearrange("p (g w k) -> p g w k", g=2, w=2)
                        for g in range(2):
                            kb.mm(pA[:, g, 0, :], Xc[:, g, 1, :], Xc[:, g, 0, :])
                            if lvl < 4:
                                kb.mm(pA[:, g, 1, :], Xc[:, g, 0, :], Xc[:, g, 1, :])
                        if lvl < 4:
                            kb.copy(Xn[:], pA, e="act")
                        else:
                            kb.copy(Xn[:, :, 0, :], pA[:, :, 0, :], e="act")
                        for g in range(2):
                            kb.mm(pB[:, g, :], Xn[:, g, 0, :], Tc[:, g, :])
                        kb.tt(Tn[:], pB, Tc[:], ALU.add)
                        cur = 1 - cur
                    Tf = TT[cur]
                    for g in range(2):
                        c = cg * 2 + g
                        sq_ = PS[5]
                        if samp:
                            kb.copy(Z[:], Zs[:, c, :])
                            kb.copy(Zb[:], Zs[:, c, :], e="act")
                        kb.mm(sq_[:, 0:64], ARbd[:, c, 0, :], Zb[:], start=True, stop=False)
                        kb.mm(sq_[:, 0:64], S2[:, g, 0, :], VS[:, c, :], start=False, stop=True)
                        kb.copy(Wb[:], sq_[:, 0:64], e="act")
                        kb.mm(sq_[:, 64:128], Tf[:, g, :], Wb[:])
                        kb.copy(Ub[:], sq_[:, 64:128])
                        kb.mm(sq_[:, 192:256], BhS[:, c, :], Ub[:], start=True, stop=False)
                        kb.mm(sq_[:, 192:256], KhS[:, c, :], VS[:, c, :], start=False, stop=True)
                        py_ = PS[4][:, 0:64]
                        kb.mm(py_, ARbd[:, c, 1, :], Zb[:], start=True, stop=False)
                        kb.mm(py_, S1[:, g, 1, :], Ub[:], start=False, stop=False)
                        kb.mm(py_, S2[:, g, 1, :], VS[:, c, :], start=False, stop=True)
                        kb.stt(Zb[:], Z[:], f["P"][:, c * 64 + 63:c * 64 + 64], sq_[:, 192:256], ALU.mult, ALU.add)
                        kb.stt(Z[:], Z[:], f["P"][:, c * 64 + 63:c * 64 + 64], sq_[:, 192:256], ALU.mult, ALU.add)
                        kb.copy(Ybuf[:, c, :], py_, e="act")
                        if samp:
                            kb.dma(o_wkv[1 + c, p], Z[:])
                if bi == LASTP:
                    kb.dma(o_wkv[0, p], Z[:])
                Yb = Ybuf[:, 0:nch, :]
                kb.reduce(st8["mean"][:, 0:nch], Yb, ALU.add)
                kb.ts(st8["mean"][:, 0:nch], st8["mean"][:, 0:nch], 1.0 / 64, ALU.mult)
                kb.tt(cen[:, 0:nch, :], Yb, st8["mean"][:, 0:nch].rearrange("p (c o) -> p c o", o=1).to_broadcast([128, nch, 64]),
                      ALU.subtract)
                kb.act(csq[:, 0:nch, :], cen[:, 0:nch, :], AF.Square)
                kb.reduce(st8["var"][:, 0:nch], csq[:, 0:nch, :], ALU.add)
                kb.act(st8["rstd"][:, 0:nch], st8["var"][:, 0:nch], AF.Sqrt, bias=GN_EPS, scale=1.0 / 64)
                kb.recip(st8["rstd"][:, 0:nch], st8["rstd"][:, 0:nch])
                kb.tt(cen[:, 0:nch, :], cen[:, 0:nch, :],
                      st8["rstd"][:, 0:nch].rearrange("p (c o) -> p c o", o=1).to_broadcast([128, nch, 64]), ALU.mult)
                kb.tt(cen[:, 0:nch, :], cen[:, 0:nch, :], cst["p_gnw"][:, p:p + 1, :].to_broadcast([128, nch, 64]), ALU.mult)
                kb.tt(cen[:, 0:nch, :], cen[:, 0:nch, :], cst["p_gnb"][:, p:p + 1, :].to_broadcast([128, nch, 64]), ALU.add)
                kb.tt(csq[:, 0:nch, :], VS[:, 0:nch, :],
                      st8["bon"][:, 0:nch].rearrange("p (c o) -> p c o", o=1).to_broadcast([128, nch, 64]), ALU.mult)
                kb.tt(yfin[:, 0:nch, :], cen[:, 0:nch, :], csq[:, 0:nch, :], ALU.add)
                pb = PSB[0]
                if not samp:
                    yv = yfin[:, 0:4, :].rearrange("p (m two) v -> p m two v", two=2)
                    for h2 in range(2):
                        hs = slice(h2 * 64, h2 * 64 + 64)
                        cs = slice(h2 * 64, h2 * 64 + 64)
                        kb.copy(ybd[hs, 0:2, cs], yv[hs, :, 0, :])
                        kb.cpred(ybd[hs, 0:2, cs], emask[hs, 0:2, 0:64], yv[hs, :, 1, :])
                    for m_ in range(2):
                        kb.tr(pb[:, m_ * 128:(m_ + 1) * 128], ybd[:, m_, :], identb[:])
                    pv4 = pb[:, 0:256].rearrange("p (m k) -> p m k", k=128)
                    for h2 in range(2):
                        hs = slice(h2 * 64, h2 * 64 + 64)
                        kb.copy(yT[hs, p, bi * 128:(bi + 1) * 128].rearrange("p (m t) -> p m t", t=64),
                                pv4[hs, :, h2 * 64:h2 * 64 + 64], e="act")
                else:
                    for h2 in range(2):
                        hs = slice(h2 * 64, h2 * 64 + 64)
                        cs = slice(h2 * 64, h2 * 64 + 64)
                        kb.copy(ybd[hs, :, cs], yfin[hs, 0:4, :])
                    for m_ in range(4):
                        kb.tr(pb[:, m_ * 128:(m_ + 1) * 128], ybd[:, m_, :], identb[:])
                    pv4 = pb[:, 0:512].rearrange("p (m k) -> p m k", k=128)
                    for h2 in range(2):
                        hs = slice(h2 * 64, h2 * 64 + 64)
                        for sb_ in range(4):
                            kb.copy(yTs[hs, p, SROWS[sb_]:SROWS[sb_] + 4], pv4[hs, sb_, h2 * 64:h2 * 64 + 4], e="act")
        kb.dma(o_shift, shst[:])
        kb.release(m_a3)

    if "a4" in stages:
        m_a4 = kb.mark()
        ropeA = kb.sb("ropeA", [128, 18, 2, 64], F32)
        kb.dma(ropeA[:], rope_all)
        ropeI = kb.sb("ropeI", [128, 18, 2, 32], F32)
        kb.dma(ropeI[:], rope_idx)
        import os as _os
        wkv = kb.sb("wkv", [128, KT, 512], BF16)
        w_kv_v = w_kv.rearrange("(kt p) c -> p kt c", p=128)
        ps_k = [kb.ps("ps_k%d" % i, [128, 512]) for i in range(2)]
        ps_t = kb.ps("ps_t", [128, 4, 128], BF16)
        kraw = [kb.sb("kraw%d" % i, [128, 4, 2, 64], F32) for i in range(2)]
        kro = [kb.sb("kro%d" % i, [128, 4, 2, 64], F32) for i in range(2)]
        krob = [kb.sb("krob%d" % i, [128, 4, 128], BF16) for i in range(2)]
        rt = [kb.sb("rt%d" % i, [128, 4, 64], F32) for i in range(4)]

        def load_cols(c0, ncol):
            if _os.environ.get('A4_CASTDMA'):
                for q4 in range(4):
                    kb.dma((wkv[:, q4 * 4:(q4 + 1) * 4, 0:ncol], q4), w_kv_v[:, q4 * 4:(q4 + 1) * 4, c0:c0 + ncol], q="pool")
                return
            for q4 in range(4):
                kb.load_w(wkv[:, q4 * 4:(q4 + 1) * 4, 0:ncol], w_kv_v[:, q4 * 4:(q4 + 1) * 4, c0:c0 + ncol],
                          128, [4, ncol], key=q4)

        def proj(pk, tok, ncol):
            for kt in range(KT):
                kb.mm(pk[:, 0:ncol], xnT[:, kt, tok], (wkv[:, kt, 0:ncol], kt // 4), start=(kt == 0), stop=(kt == KT - 1))

        load_cols(0, 512)
        for tt in range(18 if 'k' in _os.environ.get('A4_PARTS', 'kvi') else 0):
            pk = ps_k[tt % 2]
            tok = slice(tt * 128, (tt + 1) * 128)
            proj(pk, tok, 512)
            kr, ko, kbf = kraw[tt % 2], kro[tt % 2], krob[tt % 2]
            kb.copy(kr[:].rearrange("p a b c -> p (a b c)"), pk[:], e="act")
            cos = ropeA[:, tt:tt + 1, 0, :].to_broadcast([128, 4, 64])
            sin = ropeA[:, tt:tt + 1, 1, :].to_broadcast([128, 4, 64])
            x1, x2 = kr[:, :, 0, :], kr[:, :, 1, :]
            kb.tt(rt[0][:], x1, cos, ALU.mult)
            kb.tt(rt[1][:], x2, sin, ALU.mult, e="pool")
            kb.tt(ko[:, :, 0, :], rt[0][:], rt[1][:], ALU.subtract)
            kb.tt(rt[2][:], x2, cos, ALU.mult, e="pool")
            kb.tt(rt[3][:], x1, sin, ALU.mult)
            kb.tt(ko[:, :, 1, :], rt[2][:], rt[3][:], ALU.add, e="pool")
            kb.dma(o_k[tok, :], ko[:].rearrange("p a b c -> p (a b c)"))
            kb.copy(kbf[:], ko[:].rearrange("p a b c -> p a (b c)"), e="act")
            for h in range(4):
                kb.tr(ps_t[:, h, :], kbf[:, h, :], identb[:])
            kb.copy(kT[:, :, tok], ps_t[:], e="dve")
        load_cols(512, 512)
        for tt in range(int(_os.environ.get('A4_NT', '18')) if 'v' in _os.environ.get('A4_PARTS', 'kvi') else 0):
            pk = ps_k[tt % 2]
            tok = slice(tt * 128, (tt + 1) * 128)
            proj(pk, tok, 512)
            vf_ = kraw[tt % 2]
            kb.copy(vf_[:].rearrange("p a b c -> p (a b c)"), pk[:], e="act")
            kb.dma(o_v[tok, :], vf_[:].rearrange("p a b c -> p (a b c)"))
            kb.copy(vtok[:, tt, :], vf_[:].rearrange("p a b c -> p (a b c)"), e="dve")
        load_cols(1024, 128)
        kiraw = kb.sb("kiraw", [128, 2, 2, 32], F32)
        kiro = [kb.sb("kiro%d" % i, [128, 2, 2, 32], F32) for i in range(2)]
        kib = kb.sb("kib", [128, 128], BF16)
        rti = [kb.sb("rti%d" % i, [128, 2, 32], F32) for i in range(4)]
        for tt in range(18 if 'i' in _os.environ.get('A4_PARTS', 'kvi') else 0):
            pk = ps_k[tt % 2]
            tok = slice(tt * 128, (tt + 1) * 128)
            proj(pk, tok, 128)
            kb.copy(kiraw[:].rearrange("p a b c -> p (a b c)"), pk[:, 0:128], e="act")
            kio = kiro[tt % 2]
            cosi = ropeI[:, tt:tt + 1, 0, :].to_broadcast([128, 2, 32])
            sini = ropeI[:, tt:tt + 1, 1, :].to_broadcast([128, 2, 32])
            y1, y2 = kiraw[:, :, 0, :], kiraw[:, :, 1, :]
            kb.tt(rti[0][:], y1, cosi, ALU.mult)
            kb.tt(rti[1][:], y2, sini, ALU.mult, e="pool")
            kb.tt(kio[:, :, 0, :], rti[0][:], rti[1][:], ALU.subtract)
            kb.tt(rti[2][:], y2, cosi, ALU.mult, e="pool")
            kb.tt(rti[3][:], y1, sini, ALU.mult)
            kb.tt(kio[:, :, 1, :], rti[2][:], rti[3][:], ALU.add, e="pool")
            kb.dma(o_ki[tok, :], kio[:, 0, :, :].rearrange("p b c -> p (b c)"))
            kb.copy(kib[:], kio[:].rearrange("p a b c -> p (a b c)"), e="act")
            kb.tr(ps_t[:, 0, :], kib[:], identb[:])
            kb.copy(kiT[:, tok], ps_t[:, 0, :], e="dve")
        kb.release(m_a4)


    kb.release(m_x)
    if "b" in stages:
        import os as _os
        NT = int(_os.environ.get("B_NT", "8"))
        xT_own = din("xT_own", [D, NOWNS])
        memT = din("memT", [D, 256])
        p_memnw = din("p_memnw", [128, KT])
        w_gr = din("w_gr", [D, 1024])
        w_qm = din("w_qm", [D, 512])
        w_tok = din("w_tok", [D, 2576])
        w_mem = din("w_mem", [D, 1024])
        w_out_d = din("w_out", [D, D])
        rope_own = din("rope_own", [128, 9, 2, 64])
        rope_owni = din("rope_owni", [128, 9, 2, 32])
        qpos_d = din("qpos", [128, 9])
        kpos_d = din("c_kpos", [128, SEQ])
        pw2_d = din("c_pw2", [128, 24])
        fnw_d = din("p_fnw", [128, D])
        x_own_tok = din("x_own_tok", [NOWNS, D])
        o_memk = dout("o_memk", [128, 4, 256])
        o_memv = dout("o_memv", [256, 512])
        o_y = dout("o_y", [NOWNS, D])
        SC = 128.0 ** -0.5

        yamT = kb.sb("yamT", [128, 8, NOWNS], BF16)
        m_bp = kb.mark()
        qT = kb.sb("qT", [128, 4, NOWNS], BF16)
        qmT = kb.sb("qmT", [128, 4, NOWNS], BF16)
        qiT = kb.sb("qiT", [128, 8, NOWNS], BF16)
        sga = kb.sb("sga", [128, 9, 512], BF16)
        sgm = kb.sb("sgm", [128, 9, 512], BF16)
        wis = kb.sb("wis", [128, 9, 16], F32)
        mkT = kb.sb("mkT", [128, 4, 256], BF16)
        mvb = kb.sb("mvb", [128, 2, 512], BF16)
        qsb = kb.sb("qsb", [128, 512], BF16)
        qpos = kb.sb("qpos_s", [128, 9], F32)
        kb.dma(qpos[:], qpos_d)
        PS = [kb.ps("QS%d" % i, [128, 512]) for i in range(7)]
        PSB = kb.ps("QSB", [128, 1024], BF16)

        m_b1 = kb.mark()
        ropeO = kb.sb("ropeO", [128, 9, 2, 64], F32)
        ropeOI = kb.sb("ropeOI", [128, 9, 2, 32], F32)
        memnw = kb.sb("memnw", [128, KT], F32)
        kb.dma(ropeO[:], rope_own)
        kb.dma(ropeOI[:], rope_owni)
        kb.dma(memnw[:], p_memnw)

        def norm_T(src_d, dst, nw, blks, xst, sqb, rstdb):
            sv = src_d.rearrange("(kt p) t -> p kt t", p=128)
            for (t0, nb) in blks:
                for half in range(2):
                    kb.dma((xst[:, half * 8:(half + 1) * 8, 0:nb], half), sv[:, half * 8:(half + 1) * 8, t0:t0 + nb])
                for kt in range(KT):
                    s_ = sqb[kt % 2]
                    kb.act(s_[:, 0:nb], (xst[:, kt, 0:nb], kt // 8), AF.Square)
                    kb.mm(PS[0][:, 0:nb], onesb[:], s_[:, 0:nb], start=(kt == 0), stop=(kt == KT - 1))
                kb.act(rstdb[:, 0:nb], PS[0][:, 0:nb], AF.Sqrt, bias=EPS, scale=1.0 / D)
                kb.recip(rstdb[:, 0:nb], rstdb[:, 0:nb])
                for kt in range(KT):
                    kb.stt(dst[:, kt, t0:t0 + nb], (xst[:, kt, 0:nb], kt // 8), nw[:, kt:kt + 1], rstdb[:, 0:nb],
                           ALU.mult, ALU.mult)

        def staging(tag):
            return (kb.sb("xst" + tag, [128, KT, 128], F32),
                    [kb.sb("sqb%s%d" % (tag, i_), [128, 128], BF16) for i_ in range(2)],
                    kb.sb("rstdb" + tag, [128, 128], F32))

        def mk_load_wt(wt_):
            def load_wt(src_d, c0, ncol):
                sv = src_d.rearrange("(kt p) c -> p kt c", p=128)
                step = min(max(1, 2048 // ncol), 4)
                for k0 in range(0, KT, step):
                    kb.load_w(wt_[:, k0:k0 + step, 0:ncol], sv[:, k0:k0 + step, c0:c0 + ncol], 128, [step, ncol],
                              key=k0 // 4)
            return load_wt

        m_mem = kb.mark()
        wt = kb.sb("wtm", [128, KT, 512], BF16)
        load_wt = mk_load_wt(wt)
        memnT = kb.sb("memnT", [128, KT, 256], BF16)
        mkf = kb.sb("mkf", [128, 4, 256], F32)
        xst, sqb, rstdb = staging("m")
        norm_T(memT, memnT, memnw, [(0, 128), (128, 128)], xst, sqb, rstdb)
        load_wt(w_mem, 0, 512)
        for h in range(4):
            pg = PS[h % 2]
            for kt in range(KT):
                kb.mm(pg[:, 0:256], (wt[:, kt, h * 128:(h + 1) * 128], kt // 4), memnT[:, kt, :],
                      start=(kt == 0), stop=(kt == KT - 1))
            kb.copy(mkf[:, h, :], pg[:, 0:256], e="act")
        kb.dma(o_memk, mkf[:])
        kb.copy(mkT[:], mkf[:], e="dve")
        load_wt(w_mem, 512, 512)
        mvf = mkf[:].rearrange("p a b -> p (a b)").rearrange("p (m c) -> p m c", c=512)
        for mt in range(2):
            pg = PS[mt % 2]
            for kt in range(KT):
                kb.mm(pg[:], memnT[:, kt, mt * 128:(mt + 1) * 128], (wt[:, kt, :], kt // 4),
                      start=(kt == 0), stop=(kt == KT - 1))
            kb.copy(mvf[:, mt, :], pg[:], e="act")
            kb.dma(o_memv[mt * 128:(mt + 1) * 128, :], mvf[:, mt, :])
        kb.copy(mvb[:], mvf, e="dve")
        kb.release(m_mem)

        xnTo = kb.sb("xnTo", [128, KT, NOWNS], BF16)
        m_b0 = kb.mark()
        xst, sqb, rstdb = staging("o")
        norm_T(xT_own, xnTo, normw, [(t_, 128) for t_ in range(0, NOWNS, 128)], xst, sqb, rstdb)
        kb.release(m_b0)
        oblocks = [(0, 512), (512, 512), (1024, 128)]
        wt = kb.sb("wt", [128, KT, 512], BF16)
        load_wt = mk_load_wt(wt)
        tf1 = kb.sb("tf1", [128, 512], F32)
        tb1 = kb.sb("tb1", [128, 512], BF16)
        gtmp = [tf1, tf1]
        tf = [tf1, tf1]
        tb = [tb1, tb1]
        rr = [kb.sb("rr0", [128, 256], F32), kb._wst[0][:, 0:256]]
        for half in range(2):
            load_wt(w_gr, half * 512, 512)
            for pp_ in range(4):
                p = half * 4 + pp_
                for bi_, (t0, nb) in enumerate(oblocks):
                    pg = PS[bi_ % 2]
                    for kt in range(KT):
                        kb.mm(pg[:, 0:nb], (wt[:, kt, pp_ * 128:(pp_ + 1) * 128], kt // 4), xnTo[:, kt, t0:t0 + nb],
                              start=(kt == 0), stop=(kt == KT - 1))
                    g0 = gtmp[bi_ % 2]
                    kb.act(g0[:, 0:nb], pg[:, 0:nb], AF.Tanh, scale=0.5)
                    kb.stt(g0[:, 0:nb], g0[:, 0:nb], 1.0, pg[:, 0:nb], ALU.add, ALU.mult)
                    ydst = yT[:, p, t0:t0 + nb] if t0 < NOWN else yTs[:, p, :]
                    kb.stt(ydst, g0[:, 0:nb], 0.5, ydst, ALU.mult, ALU.mult)
        load_wt(w_qm, 0, 512)
        for h in range(4):
            for bi_, (t0, nb) in enumerate(oblocks):
                pg = PS[bi_ % 2]
                for kt in range(KT):
                    kb.mm(pg[:, 0:nb], (wt[:, kt, h * 128:(h + 1) * 128], kt // 4), xnTo[:, kt, t0:t0 + nb],
                          start=(kt == 0), stop=(kt == KT - 1))
                kb.copy(qmT[:, h, t0:t0 + nb], pg[:, 0:nb], e="act")
        groups = [("q", 0, 512), ("ga", 512, 512), ("gm", 1024, 512), ("qi0", 1536, 512), ("qi1", 2048, 512),
                  ("wi", 2560, 16)]
        for gname, c0, ncol in groups:
            load_wt(w_tok, c0, ncol)
            for i in range(9):
                nr = 128
                tok = slice(i * 128, i * 128 + nr)
                pg = PS[i % 2]
                for kt in range(KT):
                    kb.mm(pg[0:nr, 0:ncol], xnTo[:, kt, tok], (wt[:, kt, 0:ncol], kt // 4),
                          start=(kt == 0), stop=(kt == KT - 1))
                f_, b_ = tf[i % 2], tb[i % 2]
                if gname in ("ga", "gm"):
                    dst = sga if gname == "ga" else sgm
                    kb.act(f_[0:nr, :], pg[0:nr, :], AF.Tanh, scale=0.5)
                    kb.stt(f_[0:nr, :], f_[0:nr, :], 1.0, pg[0:nr, :], ALU.add, ALU.mult)
                    kb.ts(dst[0:nr, i, :], f_[0:nr, :], 0.5, ALU.mult)
                elif gname == "wi":
                    kb.act(wis[0:nr, i, :], pg[0:nr, 0:16], AF.Copy, scale=1.0 / 32.0)
                else:
                    hd = 64 if gname == "q" else 32
                    nh = 512 // (2 * hd)
                    tab = ropeO if gname == "q" else ropeOI
                    kb.copy(f_[0:nr, :], pg[0:nr, :], e="act")
                    xv = f_[0:nr, :].rearrange("p (h two d) -> p h two d", two=2, d=hd)
                    cos = tab[0:nr, i:i + 1, 0, :].to_broadcast([nr, nh, hd])
                    sin = tab[0:nr, i:i + 1, 1, :].to_broadcast([nr, nh, hd])
                    ov = b_[0:nr, :].rearrange("p (h two d) -> p h two d", two=2, d=hd)
                    r4 = [r_[0:nr, 0:256].rearrange("p (h d) -> p h d", d=hd) for r_ in rr]
                    kb.tt(r4[0], xv[:, :, 0, :], cos, ALU.mult)
                    kb.tt(r4[1], xv[:, :, 1, :], sin, ALU.mult, e="pool")
                    kb.tt(ov[:, :, 0, :], r4[0], r4[1], ALU.subtract)
                    kb.tt(r4[0], xv[:, :, 1, :], cos, ALU.mult, e="pool")
                    kb.tt(r4[1], xv[:, :, 0, :], sin, ALU.mult)
                    kb.tt(ov[:, :, 1, :], r4[0], r4[1], ALU.add, e="pool")
                    for j in range(4):
                        kb.tr(PSB[:, j * 128:j * 128 + nr], b_[0:nr, j * 128:(j + 1) * 128], identb[0:nr, 0:nr])
                    pv_ = PSB[:, 0:512].rearrange("p (j t) -> p j t", t=128)[:, :, 0:nr]
                    if gname == "q":
                        kb.copy(qT[:, :, tok], pv_, e="act")
                        if i == 8:
                            kb.copy(qsb[:], b_[:], e="pool")
                    else:
                        j0 = 0 if gname == "qi0" else 4
                        kb.copy(qiT[:, j0:j0 + 4, tok], pv_, e="act")
        kb.release(m_b1)

        m_b2 = kb.mark()
        kpos = kb.sb("kpos", [128, SEQ], F32)
        pw2 = kb.sb("pw2", [128, 24], F32)
        kb.dma(kpos[:], kpos_d)
        kb.dma(pw2[:], pw2_d)
        acc = kb.sb("acc", [128, SEQ], F32)
        junk = kb.sb("junk", [128, SEQ], BF16)
        sel = kb.sb("sel", [128, SEQ], BF16)
        ebuf = kb.sb("ebuf", [128, SEQ], BF16)
        pbuf = kb.sb("pbuf", [128, SEQ], BF16)
        pT = kb.sb("pT", [128, 16, 128], BF16)
        rl = [kb.sb("rl%d" % i, [128, 512], F32) for i in range(2)]
        cm = kb.sb("cm", [128, 256], F32)
        nbm = kb.sb("nbm", [128, 256], F32)
        sm = {nm: kb.sb("sm_" + nm, [128, 1], F32) for nm in
              ["amax", "w0", "lo", "mid", "cnt", "g", "m", "negm", "rs", "rinv"]}
        wk = kb.sb("wk", [128, 24], F32)
        mx4 = kb.sb("mx4", [128, 4], F32)
        yab = kb.sb("yab", [128, 1024], BF16)
        BIG = 1.0e30
        for i in range(NT):
            tok = slice(i * 128, (i + 1) * 128)
            L = (2 * i + 2) * 128
            nkb = L // 128
            chunks = [(c0, min(512, L - c0)) for c0 in range(0, L, 512)]
            kb.ts(cm[:], kpos[:, L - 256:L], qpos[:, i:i + 1], ALU.is_le)
            if i == 0:
                kb.copy(sel[:, 0:256], cm[:])
            else:
                for h in range(16):
                    hp, hh = h // 2, h % 2
                    hs = slice(hh * 64, hh * 64 + 64)
                    for ci, (c0, cn) in enumerate(chunks):
                        pg = PS[4 + (h * len(chunks) + ci) % 2]
                        kb.mm(pg[:, 0:cn], qiT[hs, hp, tok], kiT[hs, c0:c0 + cn])
                        r_ = rl[(h * len(chunks) + ci) % 2]
                        kb.act(r_[:, 0:cn], pg[:, 0:cn], AF.Relu)
                        if h == 0:
                            kb.ts(acc[:, c0:c0 + cn], r_[:, 0:cn], wis[:, i, h:h + 1], ALU.mult)
                        else:
                            kb.stt(acc[:, c0:c0 + cn], r_[:, 0:cn], wis[:, i, h:h + 1], acc[:, c0:c0 + cn],
                                   ALU.mult, ALU.add)
                kb.reduce(sm["amax"][:], acc[:, 0:L], ALU.max)
                kb.reduce(sm["g"][:], acc[:, 0:L], ALU.min)
                kb.ts(sm["g"][:], sm["g"][:], -1.0, ALU.mult)
                kb.tt(sm["amax"][:], sm["amax"][:], sm["g"][:], ALU.max)
                kb.ts(nbm[:], cm[:], BIG, ALU.mult, -BIG, ALU.add)
                kb.tt(acc[:, L - 256:L], acc[:, L - 256:L], cm[:], ALU.mult)
                kb.tt(acc[:, L - 256:L], acc[:, L - 256:L], nbm[:], ALU.add)
                kb.ts(sm["w0"][:], sm["amax"][:], 2.0, ALU.mult, 2.0, ALU.add)
                kb.ts(sm["lo"][:], sm["amax"][:], -1.0, ALU.mult, -1.0, ALU.add)
                kb.ts(wk[:], pw2[:], sm["w0"][:], ALU.mult)
                for k_ in range(24):
                    kb.tt(sm["mid"][:], sm["lo"][:], wk[:, k_:k_ + 1], ALU.add)
                    kb.ts(junk[:, 0:L], acc[:, 0:L], sm["mid"][:], ALU.is_ge, None, ALU.add, accum=sm["cnt"][:])
                    kb.ts(sm["g"][:], sm["cnt"][:], 255.5, ALU.is_ge, wk[:, k_:k_ + 1], ALU.mult)
                    kb.tt(sm["lo"][:], sm["lo"][:], sm["g"][:], ALU.add)
                kb.ts(sel[:, 0:L], acc[:, 0:L], sm["lo"][:], ALU.is_ge)
            for h in range(4):
                for ci, (c0, cn) in enumerate(chunks):
                    kb.mm(PS[ci][:, 0:cn], qT[:, h, tok], kT[:, h, c0:c0 + cn])
                    kb.reduce(mx4[:, ci:ci + 1], PS[ci][:, 0:cn], ALU.max)
                kb.reduce(sm["m"][:], mx4[:, 0:len(chunks)], ALU.max)
                kb.ts(sm["negm"][:], sm["m"][:], -SC, ALU.mult)
                for ci, (c0, cn) in enumerate(chunks):
                    kb.act(ebuf[:, c0:c0 + cn], PS[ci][:, 0:cn], AF.Exp, bias=sm["negm"][:], scale=SC)
                kb.stt(pbuf[:, 0:L], ebuf[:, 0:L], 1.0, sel[:, 0:L], ALU.mult, ALU.mult, accum=sm["rs"][:])
                kb.recip(sm["rinv"][:], sm["rs"][:])
                for k0 in range(0, nkb, 8):
                    kn = min(8, nkb - k0)
                    for kk_ in range(kn):
                        kb.tr(PSB[:, kk_ * 128:(kk_ + 1) * 128], pbuf[:, (k0 + kk_) * 128:(k0 + kk_ + 1) * 128], identb[:])
                    kb.copy(pT[:, k0:k0 + kn, :].rearrange("p a b -> p (a b)"), PSB[:, 0:kn * 128], e="act")
                po = PS[6]
                for kk_ in range(nkb):
                    kb.mm(po[:, 0:128], pT[:, kk_, :], vtok[:, kk_, h * 128:(h + 1) * 128],
                          start=(kk_ == 0), stop=(kk_ == nkb - 1))
                kb.stt(yab[:, h * 128:(h + 1) * 128], po[:, 0:128], sm["rinv"][:], sga[:, i, h * 128:(h + 1) * 128],
                       ALU.mult, ALU.mult)
            for h in range(4):
                pg = PS[h % 2]
                kb.mm(pg[:, 0:256], qmT[:, h, tok], mkT[:, h, :])
                kb.reduce(sm["m"][:], pg[:, 0:256], ALU.max)
                kb.ts(sm["negm"][:], sm["m"][:], -SC, ALU.mult)
                kb.act(ebuf[:, 0:256], pg[:, 0:256], AF.Exp, bias=sm["negm"][:], scale=SC, accum=sm["rs"][:])
                kb.recip(sm["rinv"][:], sm["rs"][:])
                for mt in range(2):
                    kb.tr(PSB[:, mt * 128:(mt + 1) * 128], ebuf[:, mt * 128:(mt + 1) * 128], identb[:])
                kb.copy(pT[:, 0:2, :].rearrange("p a b -> p (a b)"), PSB[:, 0:256], e="act")
                po = PS[6]
                for mt in range(2):
                    kb.mm(po[:, 0:128], pT[:, mt, :], mvb[:, mt, h * 128:(h + 1) * 128], start=(mt == 0), stop=(mt == 1))
                kb.stt(yab[:, 512 + h * 128:512 + (h + 1) * 128], po[:, 0:128], sm["rinv"][:],
                       sgm[:, i, h * 128:(h + 1) * 128], ALU.mult, ALU.mult)
            for j in range(8):
                kb.tr(PSB[:, j * 128:(j + 1) * 128], yab[:, j * 128:(j + 1) * 128], identb[:])
            kb.copy(yamT[:, :, tok], PSB[:].rearrange("p (j t) -> p j t", t=128), e="act")
        kb.release(m_b2)

        if "s" in stages:
            U32 = mybir.dt.uint32
            m_s = kb.mark()
            BIGS = 1.0e30
            NPOOLR = 2560 * 128
            pt_d = din("pt", [4, 64], I32)
            cache_ki2 = din("cache_ki2", [8 * 2560, 1024])
            cache_kv = din("cache_kv", [NPOOLR, 1024])
            mem_kT = din("mem_kT", [4, 4, 128, 256])
            mem_v = din("mem_v", [4, 256, 512])
            iota64 = kb.sb("iota64", [128, 64], F32)
            identf = kb.sb("identf", [128, 128], F32)
            newcm = kb.sb("newcm", [128, 8], F32)
            kb.dma(iota64[:], din("c_iota64", [128, 64]))
            kb.dma(identf[:], din("c_identf", [128, 128]))
            kb.dma(newcm[:], din("c_newcm", [128, 8]))
            onesf = kb.sb("onesf", [128, 128], F32)
            kb.memset(onesf[:], 1.0)
            SP = PS[0:6]
            SPB = PSB
            idxu = kb.sb("idxu", [128, 256], U32)
            vmx = kb.sb("vmx", [128, 8], F32)
            inew = kb.sb("inew", [128, 4], F32)
            yabs = kb.sb("yabs", [128, 1024], BF16)
            idxp = [kb.sb("idxp%d" % pr, [128, 1], I32) for pr in range(2)]
            idxpf = kb.sb("idxpf", [128, 2], F32)
            idxcf = kb.sb("idxcf", [128, 8, 2], F32)
            idxc = kb.sb("idxc", [128, 8, 2], I32)
            for pr in range(2):
                kb.dma(idxp[pr][:], pt_d[2 * pr:2 * pr + 2, :].rearrange("b (p o) -> (b p) o", o=1))
                kb.copy(idxpf[:, pr:pr + 1], idxp[pr][:])
            for ch_ in range(8):
                kb.ts(idxcf[:, ch_, :], idxpf[:], float(ch_ * 2560), ALU.add)
            kb.copy(idxc[:], idxcf[:])
            qipad = kb.sb("qipad", [128, 8, 2, 8], BF16)
            qmpad = kb.sb("qmpad", [128, 4, 2, 8], BF16)
            kb.memset(qipad[:], 0.0)
            kb.memset(qmpad[:], 0.0)
            kb.copy(qipad[:, :, 0, 0:4], qiT[:, :, 1024 + 64:1024 + 68])
            kb.copy(qipad[:, :, 1, 4:8], qiT[:, :, 1024 + 68:1024 + 72])
            kb.copy(qmpad[:, :, 0, 0:4], qmT[:, :, 1024 + 64:1024 + 68])
            kb.copy(qmpad[:, :, 1, 4:8], qmT[:, :, 1024 + 68:1024 + 72])

            def mm_rows(sb_, out_tile, cols, lhs_plain, lhs_pad, rhs, first=True, last=True):
                if sb_ < 2:
                    kb.mm(out_tile[SROWS[sb_]:SROWS[sb_] + 4, cols], lhs_plain, rhs, start=first, stop=last)
                else:
                    kb.mm(out_tile[64:72, cols], lhs_pad, rhs, start=(first and sb_ == 2), stop=(last and sb_ == 3))

            m_s1 = kb.mark()
            Isc = kb.sb("Isc", [128, 8200], F32)
            kb.memset(Isc[:, 8192:8200], -BIGS)
            kig = kb.sb("s_kig", [128, 16, 64], F32)
            kib = kb.sb("s_kib", [128, 16, 2, 64], BF16)
            kiTc = kb.sb("s_kiTc", [128, 16, 128], BF16)
            rls = [kb.sb("rls%d" % i_, [128, 512], F32) for i_ in range(2)]
            for ch in range(8):
                for pr in range(2):
                    kb.gather(kig[:].rearrange("p a b -> p (a b)"), cache_ki2, idxc[:, ch, pr:pr + 1],
                              bounds_check=8 * 2560 - 1, oob_is_err=False)
                    kb.copy(kib[:, :, 0, :], kig[:], e="act")
                    kb.copy(kib[:, :, 1, :], kig[:], e="pool")
                    for t8 in range(2):
                        for j in range(8):
                            kb.tr(SPB[:, j * 128:(j + 1) * 128],
                                  kib[:, t8 * 8 + j, :, :].rearrange("p a b -> p (a b)"), identb[:])
                        kb.copy(kiTc[:, t8 * 8:(t8 + 1) * 8, :].rearrange("p a b -> p (a b)"), SPB[:], e="act")
                    pslc = slice(pr * 64, pr * 64 + 64)
                    for h in range(16):
                        hp, hh = h // 2, h % 2
                        hs = slice(hh * 64, hh * 64 + 64)
                        for q2 in range(2):
                            pg = SP[(h * 2 + q2) % 4]
                            for bb in range(2):
                                sb_ = 2 * pr + bb
                                mm_rows(sb_, pg, slice(0, 512),
                                        qiT[hs, hp, 1024 + SROWS[sb_]:1024 + SROWS[sb_] + 4], qipad[hs, hp, sb_ % 2, :],
                                        kiTc[hs, q2 * 8:(q2 + 1) * 8, bb * 64:(bb + 1) * 64])
                            r_ = rls[(h * 2 + q2) % 2]
                            kb.act(r_[pslc, :], pg[pslc, :], AF.Relu)
                            c0 = ch * 1024 + q2 * 512
                            if h == 0:
                                kb.ts(Isc[pslc, c0:c0 + 512], r_[pslc, :], wis[pslc, 8, h:h + 1], ALU.mult)
                            else:
                                kb.stt(Isc[pslc, c0:c0 + 512], r_[pslc, :], wis[pslc, 8, h:h + 1], Isc[pslc, c0:c0 + 512],
                                       ALU.mult, ALU.add)
            for h in range(16):
                hp, hh = h // 2, h % 2
                hs = slice(hh * 64, hh * 64 + 64)
                pg = SP[4 + h % 2]
                for sb_ in range(4):
                    mm_rows(sb_, pg, slice(0, 4), qiT[hs, hp, 1024 + SROWS[sb_]:1024 + SROWS[sb_] + 4],
                            qipad[hs, hp, sb_ % 2, :], kiT[hs, SEQ + sb_ * 64:SEQ + sb_ * 64 + 4])
                r_ = rls[h % 2]
                kb.act(r_[:, 0:4], pg[:, 0:4], AF.Relu)
                if h == 0:
                    kb.ts(Isc[:, 8192:8196], r_[:, 0:4], wis[:, 8, h:h + 1], ALU.mult)
                else:
                    kb.stt(Isc[:, 8192:8196], r_[:, 0:4], wis[:, 8, h:h + 1], Isc[:, 8192:8196], ALU.mult, ALU.add)
            kb.tt(Isc[:, 8192:8196], Isc[:, 8192:8196], newcm[:, 0:4], ALU.mult)
            kb.tt(Isc[:, 8192:8196], Isc[:, 8192:8196], newcm[:, 4:8], ALU.add)
            kb.copy(inew[:], Isc[:, 8192:8196])
            for r in range(32):
                kb.vmax(vmx[:], Isc[:])
                kb.vmaxidx(idxu[:, r * 8:(r + 1) * 8], vmx[:], Isc[:])
                kb.vmr(Isc[:], vmx[:], Isc[:], -BIGS)
            kb.release(m_s1)
            idxf = kb.sb("s_idxf", [128, 256], F32)
            kb.copy(idxf[:], idxu[:])
            for half in range(2):
                kb.tr(SP[0][:, half * 128:(half + 1) * 128], idxf[:, half * 128:(half + 1) * 128], identf[:])
            qf = {nm: kb.sb("qf_" + nm, [128, 2, 4, 4], F32) for nm in ["idx", "pg", "tau", "isnew", "notnew", "ptv", "phys"]}
            qi_ = {nm: kb.sb("qi_" + nm, [128, 2, 4, 4], I32) for nm in ["idx", "pg", "tau", "phys"]}
            for sb_ in range(4):
                kb.copy(qf["idx"][:, :, sb_, :],
                        SP[0][:, 0:256].rearrange("p (a r) -> p a r", a=2)[:, :, SROWS[sb_]:SROWS[sb_] + 4])
            kb.copy(qi_["idx"][:], qf["idx"][:])
            kb.ts(qi_["pg"][:], qi_["idx"][:], 63, ALU.bitwise_and)
            kb.ts(qi_["tau"][:], qi_["idx"][:], 6, ALU.arith_shift_right)
            kb.copy(qf["pg"][:], qi_["pg"][:])
            kb.copy(qf["tau"][:], qi_["tau"][:])
            kb.ts(qf["isnew"][:], qf["idx"][:], 8191.5, ALU.is_ge)
            kb.ts(qf["notnew"][:], qf["isnew"][:], -1.0, ALU.mult, 1.0, ALU.add)
            ptbi = kb.sb("s_ptbi", [128, 256], I32)
            ptb = kb.sb("s_ptb", [128, 4, 64], F32)
            kb.dma(ptbi[:], pt_d.rearrange("(o b) p -> o (b p)", o=1).to_broadcast([128, 256]))
            kb.copy(ptb[:].rearrange("p a b -> p (a b)"), ptbi[:])
            oh = kb.sb("s_oh", [128, 8, 64], F32)
            for sb_ in range(4):
                ohv = oh[:].rearrange("p (a t) j -> p a t j", a=2)
                kb.tt(ohv, iota64[:].rearrange("p (a t j) -> p a t j", a=1, t=1).to_broadcast([128, 2, 4, 64]),
                      qf["pg"][:, :, sb_, :].rearrange("p a (t o) -> p a t o", o=1).to_broadcast([128, 2, 4, 64]),
                      ALU.is_equal)
                kb.tt(ohv, ohv, ptb[:, sb_:sb_ + 1, :].rearrange("p (a t) j -> p a t j", a=1).to_broadcast([128, 2, 4, 64]),
                      ALU.mult)
                kb.reduce(qf["ptv"][:, :, sb_, :], ohv, ALU.add)
            kb.stt(qf["phys"][:], qf["ptv"][:], 128.0, qf["tau"][:], ALU.mult, ALU.add)
            kb.tt(qf["phys"][:], qf["phys"][:], qf["notnew"][:], ALU.mult)
            kb.stt(qf["phys"][:], qf["isnew"][:], 330000.0, qf["phys"][:], ALU.mult, ALU.add)
            kb.copy(qi_["phys"][:], qf["phys"][:])
            fl = kb.sb("s_fl", [128, 4], F32)
            kb.ts(fl[:], inew[:], vmx[:, 7:8], ALU.is_ge)
            kb.tt(fl[:], fl[:], newcm[:, 0:4], ALU.mult)
            flTs = kb.sb("flTs", [128, 128], F32)
            kb.memset(flTs[:], 0.0)
            kb.tr(SP[0][0:4, 0:128], fl[:], identf[:])
            kb.copy(flTs[0:4, :], SP[0][0:4, 0:128])
            KVg = [kb.sb("KVg%d" % i_, [128, 1024], F32) for i_ in range(2)]
            KVn = kb.sb("KVn", [128, 1024], F32)
            Ph = [kb.sb("Ph%d" % i_, [128, 4, 128], F32) for i_ in range(2)]
            for t_ in KVg + [KVn] + Ph:
                kb.memset(t_[:], 0.0, e="pool")
            zr = kb.sb("s_zr", [128, 512], BF16)
            qbs = kb.sb("qbs", [128, 512], F32)
            tmpd = kb.sb("tmpd", [128, 4, 128], F32)
            scq = kb.sb("scq", [128, 3, 4], F32)
            pq = kb.sb("s_pq", [128, 3, 4], F32)
            m4 = kb.sb("s_m4", [128, 1], F32)
            dm = kb.sb("s_dm", [128, 4], F32)
            yacc, ysum = SP[1], SP[2]
            nmm = 16 * 3 * 4
            cnt = [0, 0]
            pcnt = 0
            for sb_ in range(4):
                r0 = SEQ + sb_ * 64
                kb.dma(KVn[0:4, 0:512], o_k[r0:r0 + 4, :])
                kb.dma(KVn[0:4, 512:1024], o_v[r0:r0 + 4, :])
                for t in range(4):
                    q = SROWS[sb_] + t
                    kb.ts(zr[:], qsb[:], identb[:, q:q + 1], ALU.mult)
                    kb.mm(SP[3][:], onesb[:], zr[:])
                    kb.copy(qbs[:], SP[3][:], e="act")
                    for half in range(2):
                        g = KVg[half]
                        kb.gather(g[:], cache_kv, qi_["phys"][:, half, sb_, t:t + 1],
                                  bounds_check=NPOOLR - 1, oob_is_err=False)
                        kb.tt(tmpd[:].rearrange("p a b -> p (a b)"), g[:, 0:512], qbs[:], ALU.mult)
                        kb.reduce(scq[:, half, :], tmpd[:], ALU.add)
                    kb.tt(tmpd[:].rearrange("p a b -> p (a b)"), KVn[:, 0:512], qbs[:], ALU.mult)
                    kb.reduce(scq[:, 2, :], tmpd[:], ALU.add)
                    for j in range(3):
                        kb.tr(SP[4][0:4, j * 128:(j + 1) * 128], scq[:, j, :], identf[:])
                    kb.reduce(m4[0:4, :], SP[4][0:4, 0:384], ALU.max)
                    kb.ts(dm[0:4, :], identf[0:4, 0:4], m4[0:4, 0:1], ALU.mult)
                    kb.mm(SP[5][:, 0:4], onesf[0:4, :], dm[0:4, :])
                    kb.tt(scq[:], scq[:], SP[5][:, 0:4].rearrange("p (a h) -> p a h", a=1).to_broadcast([128, 3, 4]),
                          ALU.subtract)
                    kb.act(pq[:].rearrange("p a h -> p (a h)"), scq[:].rearrange("p a h -> p (a h)"), AF.Exp, scale=SC)
                    kb.ts(pq[:, 0, :], pq[:, 0, :], qf["notnew"][:, 0, sb_, t:t + 1], ALU.mult)
                    kb.ts(pq[:, 1, :], pq[:, 1, :], qf["notnew"][:, 1, sb_, t:t + 1], ALU.mult)
                    kb.ts(pq[:, 2, :], pq[:, 2, :], flTs[:, q:q + 1], ALU.mult)
                    for j in range(3):
                        Pj = Ph[pcnt % 2]
                        pcnt += 1
                        src = KVg[j] if j < 2 else KVn
                        kb.copy(Pj[:, :, q], pq[:, j, :])
                        for h in range(4):
                            kb.mm(yacc[:, h * 128:(h + 1) * 128], Pj[:, h, :], src[:, 512 + h * 128:512 + (h + 1) * 128],
                                  start=(cnt[0] == 0), stop=(cnt[0] == nmm - 1))
                            cnt[0] += 1
                            kb.mm(ysum[:, 2 * h:2 * h + 2], Pj[:, h, :], onesf[:, 0:2],
                                  start=(cnt[1] == 0), stop=(cnt[1] == nmm - 1))
                            cnt[1] += 1
                        kb.memset(Pj[:, :, q], 0.0)
            rinvs = kb.sb("rinvs", [128, 4], F32)
            kb.ts(rinvs[:], ysum[:, 0:8:2], 1.0e-30, ALU.add)
            kb.recip(rinvs[:], rinvs[:])
            kb.tt(tmpd[:], yacc[:].rearrange("p (h d) -> p h d", d=128),
                  rinvs[:].rearrange("p (h o) -> p h o", o=1).to_broadcast([128, 4, 128]), ALU.mult)
            kb.tt(yabs[:, 0:512], tmpd[:].rearrange("p a b -> p (a b)"), sga[:, 8, :], ALU.mult)
            mkTs = kb.sb("mkTs", [128, 4, 4, 256], BF16)
            mvs = kb.sb("mvs", [128, 4, 2, 512], BF16)
            for sb_ in range(4):
                kb.load_w(mkTs[:, sb_, :, :], mem_kT[sb_].rearrange("h d m -> d h m"), 128, [4, 256])
                kb.load_w(mvs[:, sb_, :, :], mem_v[sb_].rearrange("(mt m) c -> m mt c", m=128), 128, [2, 512])
            e2 = kb.sb("s_e2", [128, 256], BF16)
            pT2 = kb.sb("pT2", [128, 2, 128], BF16)
            pT2pad = kb.sb("pT2pad", [128, 2, 2, 8], BF16)
            kb.memset(pT2pad[:], 0.0)
            sm2 = {nm: kb.sb("sm2_" + nm, [128, 1], F32) for nm in ["m", "negm", "rs", "rinv"]}
            for h in range(4):
                pg = SP[0]
                for sb_ in range(4):
                    mm_rows(sb_, pg, slice(0, 256), qmT[:, h, 1024 + SROWS[sb_]:1024 + SROWS[sb_] + 4],
                            qmpad[:, h, sb_ % 2, :], mkTs[:, sb_, h, :])
                kb.reduce(sm2["m"][:], pg[:, 0:256], ALU.max)
                kb.ts(sm2["negm"][:], sm2["m"][:], -SC, ALU.mult)
                kb.act(e2[:], pg[:, 0:256], AF.Exp, bias=sm2["negm"][:], scale=SC, accum=sm2["rs"][:])
                kb.recip(sm2["rinv"][:], sm2["rs"][:])
                for mt in range(2):
                    kb.tr(SPB[:, mt * 128:(mt + 1) * 128], e2[:, mt * 128:(mt + 1) * 128], identb[:])
                kb.copy(pT2[:].rearrange("p a b -> p (a b)"), SPB[:, 0:256], e="act")
                kb.copy(pT2pad[:, :, 0, 0:4], pT2[:, :, 64:68])
                kb.copy(pT2pad[:, :, 1, 4:8], pT2[:, :, 68:72])
                po = SP[3]
                for sb_ in range(4):
                    for mt in range(2):
                        mm_rows(sb_, po, slice(0, 128), pT2[:, mt, SROWS[sb_]:SROWS[sb_] + 4], pT2pad[:, mt, sb_ % 2, :],
                                mvs[:, sb_, mt, h * 128:(h + 1) * 128], first=(mt == 0), last=(mt == 1))
                kb.stt(yabs[:, 512 + h * 128:512 + (h + 1) * 128], po[:, 0:128], sm2["rinv"][:],
                       sgm[:, 8, h * 128:(h + 1) * 128], ALU.mult, ALU.mult)
            for j in range(8):
                kb.tr(SPB[:, j * 128:(j + 1) * 128], yabs[:, j * 128:(j + 1) * 128], identb[:])
            kb.copy(yamT[:, :, 1024:1152], SPB[:].rearrange("p (j t) -> p j t", t=128), e="act")
            kb.release(m_s)
        kb.release(m_bp)

        m_c = kb.mark()
        PS = [kb.ps("CS%d" % i, [128, 512]) for i in range(2)]
        wo = kb.sb("wo", [128, KT, 512], BF16)
        NTC = 9 if "s" in stages else NT
        hbuf = kb.sb("hbuf", [128, 9, D], F32)
        xres = [kb.sb("xres%d" % i, [128, 512], F32) for i in range(2)]
        fnw = kb.sb("fnw", [128, D], F32)
        kb.dma(fnw[:], fnw_d)
        wov = w_out_d.rearrange("(kt p) c -> p kt c", p=128)
        for nb_ in range(4):
            for k0 in range(0, KT, 4):
                kb.load_w(wo[:, k0:k0 + 4, :], wov[:, k0:k0 + 4, nb_ * 512:(nb_ + 1) * 512], 128, [4, 512], key=k0 // 4)
            for i in range(NTC):
                tok = slice(i * 128, (i + 1) * 128)
                pg = PS[i % 2]
                for kt in range(KT):
                    if kt < 8:
                        lhs = yT[:, kt, tok] if i < 8 else yTs[:, kt, :]
                    else:
                        lhs = yamT[:, kt - 8, tok]
                    kb.mm(pg[:], lhs, (wo[:, kt, :], kt // 4), start=(kt == 0), stop=(kt == KT - 1))
                xr_ = xres[i % 2]
                kb.dma(xr_[:], x_own_tok[tok, nb_ * 512:(nb_ + 1) * 512])
                kb.tt(hbuf[:, i, nb_ * 512:(nb_ + 1) * 512], pg[:], xr_[:], ALU.add)
        ysq = kb.sb("ysq", [128, D], BF16)
        fs = {nm: kb.sb("fs_" + nm, [128, 1], F32) for nm in ["ss", "rstd"]}
        yo = [kb.sb("yo0", [128, D], F32)] * 2
        for i in range(NTC):
            tok = slice(i * 128, (i + 1) * 128)
            kb.act(ysq[:], hbuf[:, i, :], AF.Square, accum=fs["ss"][:])
            kb.act(fs["rstd"][:], fs["ss"][:], AF.Sqrt, bias=EPS, scale=1.0 / D)
            kb.recip(fs["rstd"][:], fs["rstd"][:])
            kb.stt(yo[i % 2][:], hbuf[:, i, :], fs["rstd"][:], fnw[:], ALU.mult, ALU.mult)
            kb.dma(o_y[tok, :], yo[i % 2][:])
        kb.release(m_c)

    kb.S.finish()
    kb.S.flush()
    blk_cm.__exit__(None, None, None)
    while kb.cms:
        kb.cms.pop().__exit__(None, None, None)
    kb.S.close()
    return nc


def _rope_table(pos, d):
    half = d // 2
    inv_freq = (np.float32(10000.0) ** (-np.arange(half, dtype=np.float32) * np.float32(2.0 / d))).astype(np.float32)
    ang = pos.astype(np.float32)[:, None] * inv_freq[None, :]
    return np.cos(ang).astype(np.float32), np.sin(ang).astype(np.float32)


def _pos_all():
    pos = np.zeros(TALL, np.int64)
    pos[:SEQ] = np.arange(SEQ)
    for sb in range(4):
        pos[SEQ + sb * 64: SEQ + (sb + 1) * 64] = 8192 + np.minimum(np.arange(64), 3)
    return pos


_NC_CACHE = {}


def kernel(**inp):
    import os as _os
    stages = inp.pop("_stages", tuple(_os.environ.get("STAGES", "a4,a3,b,s").split(",")))
    f32 = np.float32
    x_prompt = np.asarray(inp["x_prompt"], f32)
    x_sample = np.asarray(inp["x_sample"], f32)
    w_in = np.asarray(inp["w_in"], f32)[0]
    key = tuple(stages)
    if key not in _NC_CACHE:
        _NC_CACHE[key] = build_program(stages)
    nc = _NC_CACHE[key]

    pos = _pos_all()
    ca, sa = _rope_table(pos, 128)
    ci, si = _rope_table(pos, 64)
    rope_all = np.stack([ca, sa], axis=1).reshape(18, 128, 2, 64).transpose(1, 0, 2, 3).copy()
    rope_idx = np.stack([ci, si], axis=1).reshape(18, 128, 2, 32).transpose(1, 0, 2, 3).copy()
    o_att = 3200 + 1024
    w_kv = np.concatenate([w_in[:, o_att + 512:o_att + 1024], w_in[:, o_att + 1024:o_att + 1536],
                           w_in[:, o_att + 2048 + 1024 + 16:o_att + 2048 + 1024 + 16 + 64],
                           w_in[:, o_att + 2048 + 1024 + 16:o_att + 2048 + 1024 + 16 + 64]], axis=1)
    w_kv = np.ascontiguousarray(w_kv)
    normw = np.ascontiguousarray(np.asarray(inp["norm_w"], f32)[0].reshape(KT, 128).T)
    onesb = np.ones((128, 128), NBF)
    identb = np.eye(128, dtype=f32).astype(NBF)
    g = lambda k: np.asarray(inp[k], f32)
    pi = np.arange(128)
    h2i, si = pi // 64, pi % 64
    same = (h2i[:, None] == h2i[None, :])
    strict = same & (si[:, None] < si[None, :])
    incl = same & (si[:, None] <= si[None, :])
    mask12 = np.zeros((128, 2, 2, 128), f32)
    mask12[:, :, 0, :] = strict[:, None, :]
    mask12[:, :, 1, :] = incl[:, None, :]
    mask3 = np.zeros((128, 2, 128), f32)
    mask3[:] = (same & (si[None, :] < si[:, None]))[:, None, :]
    identbd = np.zeros((128, 2, 128), f32)
    identbd[:] = np.eye(128, dtype=f32)[:, None, :]
    resetm = np.ones((128, 256), f32)
    resetm[:, ::64] = 0.0
    padm = np.zeros((128, 256), f32)
    for q_ in range(4):
        padm[:, q_::64] = 1.0
    blkb = same.astype(f32).astype(NBF)
    col = lambda v, n: np.ascontiguousarray(v.reshape(n, 128).T)
    p_mu = col(g("shift_mu")[0], 25)
    p_w0, p_a0, p_kk, p_ka = col(g("w0")[0], 8), col(g("a0")[0], 8), col(g("k_k")[0], 8), col(g("k_a")[0], 8)
    p_rk = col(g("r_k")[0].reshape(1024), 8)

    def gnl(v):
        v = v.reshape(8, 2, 64)
        o = np.zeros((128, 8, 64), f32)
        for h2 in range(2):
            o[h2 * 64:(h2 + 1) * 64] = v[None, :, h2, :]
        return o
    p_gnw, p_gnb = gnl(g("gn_w")[0]), gnl(g("gn_b")[0])
    w_c = np.ascontiguousarray(w_in[:, 3072:3200])
    w_rkv = np.stack([np.concatenate([w_in[:, p_ * 128:(p_ + 1) * 128], w_in[:, 1024 + p_ * 128:1024 + (p_ + 1) * 128],
                                      w_in[:, 2048 + p_ * 128:2048 + (p_ + 1) * 128]], axis=1) for p_ in range(8)])
    w2a2 = np.ascontiguousarray(np.concatenate([g("w2")[0], g("a2")[0]], axis=0))
    state_shift = g("state_shift")[0]
    state_wkv = g("state_wkv")[0]
    mem_prompt = g("mem_prompt")
    p_memnw = col(g("mem_norm_w")[0], KT)
    w_gr = np.ascontiguousarray(w_in[:, 3200:4224])
    w_qm = np.ascontiguousarray(w_in[:, 7376:7888])
    w_tok = np.ascontiguousarray(np.concatenate([w_in[:, 4224:4736], w_in[:, 5760:6272], w_in[:, 7888:8400],
                                                 w_in[:, 6272:7296], w_in[:, 7296:7312]], axis=1))
    w_mem = np.ascontiguousarray(g("w_mem_kv")[0])
    w_out_h = np.ascontiguousarray(g("w_out")[0])
    c_kpos = np.ascontiguousarray(np.broadcast_to(np.arange(SEQ, dtype=f32)[None, :], (128, SEQ)))
    c_pw2 = np.ascontiguousarray(np.broadcast_to((0.5 ** np.arange(1, 25)).astype(f32)[None, :], (128, 24)))
    p_fnw = np.ascontiguousarray(np.broadcast_to(g("final_norm_w")[None, :], (128, D)))
    s_inputs = {}
    if "s" in stages:
        ck = np.asarray(inp["cache_k"], f32)[0].reshape(-1, 512)
        cv = np.asarray(inp["cache_v"], f32)[0].reshape(-1, 512)
        s_inputs["cache_kv"] = np.concatenate([ck, cv], axis=1)
        cki = np.asarray(inp["cache_kidx"], f32)[0]
        s_inputs["cache_ki2"] = np.ascontiguousarray(
            np.concatenate([cki[:, ch_ * 16:(ch_ + 1) * 16, :].reshape(2560, 1024) for ch_ in range(8)], axis=0))
        s_inputs["c_iota64"] = np.ascontiguousarray(np.broadcast_to(np.arange(64, dtype=f32)[None, :], (128, 64)))
        s_inputs["c_identf"] = np.eye(128, dtype=f32)
        ncm = np.zeros((128, 8), f32)
        ncm[:, 4:8] = -1.0e30
        for sb_ in range(4):
            for t_ in range(4):
                ncm[SROWS[sb_] + t_, 0:t_ + 1] = 1.0
                ncm[SROWS[sb_] + t_, 4:4 + t_ + 1] = 0.0
        s_inputs["c_newcm"] = ncm
        page_table = np.asarray(inp["page_table"], np.int32)
        cmk = np.asarray(inp["cache_mem_k"], f32)[0]
        cmv = np.asarray(inp["cache_mem_v"], f32)[0]
    in_maps = []
    own_toks = []
    for c in range(8):
        b = c // 2
        e_ = c % 2
        xT = np.zeros((D, TALL), f32)
        xT[:, :SEQ] = x_prompt[b].T
        for sb in range(4):
            xT[:, SEQ + sb * 64: SEQ + sb * 64 + 4] = x_sample[4 * c + sb].T
        e_ = c % 2
        sprevT = np.ascontiguousarray(state_shift[4 * c:4 * c + 4].reshape(4, 25, 128).transpose(2, 1, 0))
        st_wkvT = np.ascontiguousarray(state_wkv[4 * c:4 * c + 4].transpose(0, 1, 3, 2).reshape(4, 8, 128, 64))
        own_tok = np.zeros(NOWNS, np.int64)
        own_pos = np.zeros(9 * 128, np.int64)
        for i_ in range(8):
            for j_ in range(2):
                t_ = (4 * i_ + 2 * j_ + e_) * 64 + np.arange(64)
                own_tok[i_ * 128 + j_ * 64: i_ * 128 + (j_ + 1) * 64] = t_
                own_pos[i_ * 128 + j_ * 64: i_ * 128 + (j_ + 1) * 64] = t_
        x_own_tok = np.zeros((NOWNS, D), f32)
        x_own_tok[:NOWN] = x_prompt[b][own_tok[:NOWN]]
        for sb_ in range(4):
            own_pos[1024 + SROWS[sb_]: 1024 + SROWS[sb_] + 4] = 8192 + np.arange(4)
            x_own_tok[NOWN + SROWS[sb_]: NOWN + SROWS[sb_] + 4] = x_sample[4 * c + sb_]
        co, so = _rope_table(own_pos, 128)
        coi, soi = _rope_table(own_pos, 64)
        rope_own = np.stack([co, so], axis=1).reshape(9, 128, 2, 64).transpose(1, 0, 2, 3).copy()
        rope_owni = np.stack([coi, soi], axis=1).reshape(9, 128, 2, 32).transpose(1, 0, 2, 3).copy()
        qpos = np.ascontiguousarray(own_pos.reshape(9, 128).T.astype(f32))
        extra = {
            "xT_own": np.ascontiguousarray(x_own_tok.T), "memT": np.ascontiguousarray(mem_prompt[b].T),
            "p_memnw": p_memnw, "w_gr": w_gr, "w_qm": w_qm, "w_tok": w_tok, "w_mem": w_mem, "w_out": w_out_h,
            "rope_own": rope_own, "rope_owni": rope_owni, "qpos": qpos, "c_kpos": c_kpos, "c_pw2": c_pw2,
            "p_fnw": p_fnw, "x_own_tok": x_own_tok,
        }
        own_toks.append(own_tok)
        sx = dict(s_inputs)
        if "s" in stages:
            sx["pt"] = np.ascontiguousarray(page_table[4 * c:4 * c + 4])
            sx["mem_kT"] = np.ascontiguousarray(cmk[4 * c:4 * c + 4].transpose(0, 2, 3, 1))
            sx["mem_v"] = np.ascontiguousarray(cmv[4 * c:4 * c + 4].reshape(4, 256, 512))
        in_maps.append({
            **(extra if "b" in stages else {}), **sx,
            "xT_all": xT, "c_onesb": onesb, "c_identb": identb, "p_normw": normw,
            "w_kv": w_kv, "rope_all": rope_all, "rope_idx": rope_idx,
            "c_blkb": blkb, "c_mask12": mask12, "c_mask3": mask3, "c_identbd": identbd,
            "c_resetm": resetm, "c_padm": padm, "c_emask": np.full((128, 2, 128), e_, np.int32),
            "p_mu": p_mu, "p_w0": p_w0, "p_a0": p_a0, "p_kk": p_kk, "p_ka": p_ka, "p_rk": p_rk,
            "p_gnw": p_gnw, "p_gnb": p_gnb, "sprevT": sprevT, "w_c": w_c, "w_rkv": w_rkv, "w2a2": w2a2,
            "st_wkvT": st_wkvT,
        })
    res = run_bass_kernel_spmd(nc, in_maps, core_ids=list(range(8)))
    R = [dict(r) for r in res.results]
    for r in R:
        r.setdefault("o_wkv", np.zeros((5, 8, 128, 64), f32))
        r.setdefault("o_shift", np.zeros((128, 25, 5), f32))

    def samp(name, width):
        out = np.zeros((32, 4, width), f32)
        for c in range(8):
            for sb in range(4):
                out[4 * c + sb] = R[c][name][SEQ + sb * 64: SEQ + sb * 64 + 4]
        return out
    new_k_p = np.stack([R[2 * b]["o_k"][:SEQ] for b in range(4)]).reshape(1, 4, SEQ, 4, 128)
    new_v_p = np.stack([R[2 * b]["o_v"][:SEQ] for b in range(4)]).reshape(1, 4, SEQ, 4, 128)
    new_ki_p = np.stack([R[2 * b]["o_ki"][:SEQ] for b in range(4)]).reshape(1, 4, SEQ, 64)
    new_k_s = samp("o_k", 512).reshape(1, 32, 4, 4, 128)
    new_v_s = samp("o_v", 512).reshape(1, 32, 4, 4, 128)
    new_ki_s = samp("o_ki", 64).reshape(1, 32, 4, 64)
    z = lambda *sh: np.zeros(sh, f32)
    wkvfix = lambda a: a.reshape(16, 64, 64).transpose(0, 2, 1)
    wkv_p = np.stack([wkvfix(R[2 * b]["o_wkv"][0]) for b in range(4)])[None]
    wkv_s = np.stack([wkvfix(R[c]["o_wkv"][1 + sb]) for c in range(8) for sb in range(4)])[None]
    sh_p = np.stack([R[2 * b]["o_shift"][:, :, 0].T.reshape(3200) for b in range(4)])[None]
    sh_s = np.stack([R[c]["o_shift"][:, :, 1 + sb].T.reshape(3200) for c in range(8) for sb in range(4)])[None]
    y_p = z(4, SEQ, D)
    y_s = z(32, 4, D)
    memk_p, memv_p = z(1, 4, 256, 4, 128), z(1, 4, 256, 4, 128)
    if "b" in stages:
        for c in range(8):
            y_p[c // 2][own_toks[c][:NOWN]] = R[c]["o_y"][:NOWN]
            for sb_ in range(4):
                y_s[4 * c + sb_] = R[c]["o_y"][NOWN + SROWS[sb_]:NOWN + SROWS[sb_] + 4]
        for b in range(4):
            memk_p[0, b] = R[2 * b]["o_memk"].transpose(2, 1, 0)
            memv_p[0, b] = R[2 * b]["o_memv"].reshape(256, 4, 128)
    return (y_p, y_s, wkv_p, sh_p, new_k_p, new_v_p, new_ki_p,
            memk_p, memv_p, wkv_s, sh_s,
            new_k_s, new_v_s, new_ki_s)
```

```python
import numpy as np
import ml_dtypes
import concourse.bass as bass
import concourse.mybir as mybir
from concourse.bass_utils import run_bass_kernel_spmd

F32 = mybir.dt.float32
BF16 = mybir.dt.bfloat16
I32 = mybir.dt.int32
AF = mybir.ActivationFunctionType
ALU = mybir.AluOpType
AX = mybir.AxisListType

NBF = ml_dtypes.bfloat16

D = 2048
KT = 16
SEQ = 2048
TALL = 2304
NOWN = 1024
NOWNS = 1152
C = 64
SROWS = [0, 32, 64, 68]
EPS = 1e-6
GN_EPS = 64e-5


class _PoolEv:
    __slots__ = ("idx", "resolved")

    def __init__(self, idx):
        self.idx = idx
        self.resolved = None


class Sched:
    NP = 56

    def __init__(self, nc, ndma=24):
        self.nc = nc
        self.eng = {"pe": nc.tensor, "act": nc.scalar, "dve": nc.vector,
                    "pool": nc.gpsimd, "sync": nc.sync}
        self.prog = {k: [] for k in self.eng}
        self.cnt = {k: 0 for k in self.eng}
        self.waited = {k: {} for k in self.eng}
        self.res = {}
        self.sems = {}
        self.ndma = ndma
        self.dtot = [0] * ndma
        self.dnext = 0
        self.stack = []
        self.psum_names = set()
        self.ppend = [None] * self.NP
        self.pnext = 0
        self.nnote = 0
        self.pdone = []

    def open(self):
        names = list(self.eng) + ["pnote"]
        for k in names:
            cm = self.nc.semaphore("sem_" + k)
            self.sems[k] = cm.__enter__()
            self.stack.append(cm)
        for i in range(self.ndma):
            cm = self.nc.semaphore("semd%d" % i)
            self.sems[("d", i)] = cm.__enter__()
            self.stack.append(cm)
        for i in range(self.NP):
            cm = self.nc.semaphore("semp%d" % i)
            self.sems[("p", i)] = cm.__enter__()
            self.stack.append(cm)

    def close(self):
        for cm in reversed(self.stack):
            cm.__exit__(None, None, None)

    @staticmethod
    def key(x):
        if isinstance(x, tuple):
            return (x[0].tensor.name, x[1])
        if isinstance(x, str):
            return x
        return x.tensor.name

    def _note(self, pev):
        if pev.resolved is not None:
            return pev.resolved
        i = pev.idx
        psem = self.sems[("p", i)]
        nsem = self.sems["pnote"]

        def fn(eng):
            eng.sem_inc(psem, -16)
            eng.sem_inc(nsem, 1)
            return None
        self.prog["pool"].append(([(("p", i), 16)], fn, None))
        self.nnote += 1
        pev.resolved = ("pnote", self.nnote)
        self.ppend[i] = None
        return pev.resolved

    def _collect(self, e, reads, writes):
        need = {}

        def add(ev):
            if ev is None:
                return
            if isinstance(ev, _PoolEv):
                ev = self._note(ev)
            sk, val = ev
            if sk == "pe" and e == "pe":
                return
            if need.get(sk, 0) < val:
                need[sk] = val
        for k in reads:
            r = self.res.get(k)
            if r:
                add(r["w"])
                nm = k[0] if isinstance(k, tuple) else k
                if nm in self.psum_names:
                    for ev in r["r"]:
                        if isinstance(ev, _PoolEv) or ev[0] != e:
                            add(ev)
        for k in writes:
            r = self.res.get(k)
            if r:
                add(r["w"])
                for ev in r["r"]:
                    add(ev)
        out = []
        wd = self.waited[e]
        for sk, val in need.items():
            if wd.get(sk, 0) >= val:
                continue
            wd[sk] = val
            out.append((sk, val))
        return out

    def _record(self, ev, reads, writes):
        for k in reads:
            r = self.res.setdefault(k, {"w": None, "r": []})
            if isinstance(ev, _PoolEv):
                r["r"].append(ev)
            else:
                for j, old in enumerate(r["r"]):
                    if not isinstance(old, _PoolEv) and old[0] == ev[0]:
                        if old[1] < ev[1]:
                            r["r"][j] = ev
                        break
                else:
                    r["r"].append(ev)
        for k in writes:
            self.res[k] = {"w": ev, "r": []}

    def op(self, e, fn, reads=(), writes=()):
        reads = [self.key(x) for x in reads]
        writes = [self.key(x) for x in writes]
        waits = self._collect(e, reads, writes)
        self.cnt[e] += 1
        ev = (e, self.cnt[e])
        self.prog[e].append((waits, fn, (e, 1)))
        self._record(ev, reads, writes)

    def dma(self, q, fn, reads=(), writes=()):
        reads = [self.key(x) for x in reads]
        writes = [self.key(x) for x in writes]
        if q == "pool":
            i = self.pnext
            self.pnext += 1
            assert i < self.NP, "out of pool-DMA semaphores"
            sk = ("p", i)
            waits = self._collect(q, reads, writes)
            ev = (sk, 16)
            self.prog[q].append((waits, fn, (sk, 16)))
            self._record(ev, reads, writes)
            self.pdone.append(ev)
            return
        i = self.dnext
        self.dnext = (self.dnext + 1) % self.ndma
        sk = ("d", i)
        waits = self._collect(q, reads, writes)
        if self.dtot[i] > 0 and self.waited[q].get(sk, 0) < self.dtot[i]:
            self.waited[q][sk] = self.dtot[i]
            waits.append((sk, self.dtot[i]))
        self.dtot[i] += 16
        ev = (sk, self.dtot[i])
        self.prog[q].append((waits, fn, (sk, 16)))
        self._record(ev, reads, writes)

    def _all_events(self):
        for pev in self.ppend:
            if pev is not None:
                self._note(pev)
        allw = [(e, self.cnt[e]) for e in ("pe", "act", "dve", "pool") if self.cnt[e] > 0]
        allw += [(("d", i), self.dtot[i]) for i in range(self.ndma) if self.dtot[i] > 0]
        allw += list(self.pdone)
        return allw

    def barrier(self):
        allw = self._all_events()
        for q in self.eng:
            waits = []
            for sk, val in allw:
                if sk == q and q == "pe":
                    continue
                if self.waited[q].get(sk, 0) < val:
                    self.waited[q][sk] = val
                    waits.append((sk, val))
            if waits:
                self.prog[q].append((waits, None, None))

    def finish(self):
        allw = self._all_events()
        waits = [(sk, val) for sk, val in allw if self.waited["sync"].get(sk, 0) < val]
        self.prog["sync"].append((waits, None, None))

    def flush(self):
        block = self.block

        def runner(name):
            items = self.prog[name]
            self.prog[name] = []

            def body(eng):
                for waits, fn, inc in items:
                    for sk, val in waits:
                        eng.wait_ge(self.sems[sk], val)
                    if fn is not None:
                        ins = fn(eng)
                        if inc is not None:
                            ins.then_inc(self.sems[inc[0]], inc[1])
            return body
        block.sync(runner("sync"))
        block.tensor(runner("pe"))
        block.scalar(runner("act"))
        block.vector(runner("dve"))
        block.gpsimd(runner("pool"))


class KB:
    def __init__(self, nc):
        self.nc = nc
        self.S = Sched(nc)
        self.cms = []

    def sb(self, name, shape, dt):
        cm = self.nc.sbuf_tensor(name, list(shape), dt)
        t = cm.__enter__()
        self.cms.append(cm)
        return t

    def ps(self, name, shape, dt=F32):
        cm = self.nc.psum_tensor(name, list(shape), dt)
        t = cm.__enter__()
        self.cms.append(cm)
        self.S.psum_names.add(name)
        return t

    def mark(self):
        return len(self.cms)

    def release(self, mark):
        self.S.barrier()
        self.S.flush()
        while len(self.cms) > mark:
            self.cms.pop().__exit__(None, None, None)

    @staticmethod
    def ap(x):
        return x[0] if isinstance(x, tuple) else x

    def mm(self, out, lhsT, rhs, start=True, stop=True, extra_r=()):
        o, l, r = self.ap(out), self.ap(lhsT), self.ap(rhs)
        self.S.op("pe", lambda eng: eng.matmul(o, l, r, start=start, stop=stop),
                  reads=[lhsT, rhs] + list(extra_r), writes=[out])

    def tr(self, out, in_, ident):
        o, i, d = self.ap(out), self.ap(in_), self.ap(ident)
        self.S.op("pe", lambda eng: eng.transpose(o, i, d), reads=[in_], writes=[out])

    def act(self, out, in_, func, bias=None, scale=None, accum=None, e="act"):
        o, i = self.ap(out), self.ap(in_)
        kw = {}
        reads = [in_]
        writes = [out]
        if bias is not None:
            kw["bias"] = self.ap(bias) if not isinstance(bias, float) else bias
            if not isinstance(bias, float):
                reads.append(bias)
        if scale is not None:
            kw["scale"] = self.ap(scale) if not isinstance(scale, float) else scale
            if not isinstance(scale, float):
                reads.append(scale)
        if accum is not None:
            kw["accum_out"] = self.ap(accum)
            writes.append(accum)
        self.S.op("act", lambda eng: eng.activation(o, i, func, **kw), reads=reads, writes=writes)

    def tt(self, out, in0, in1, op, e="dve"):
        o, a, b = self.ap(out), self.ap(in0), self.ap(in1)
        self.S.op(e, lambda eng: eng.tensor_tensor(o, a, b, op), reads=[in0, in1], writes=[out])

    def ts(self, out, in0, s1, op0, s2=None, op1=None, accum=None, e="dve"):
        o, a = self.ap(out), self.ap(in0)
        reads = [in0]
        writes = [out]
        v1 = s1
        v2 = s2
        if not isinstance(s1, (int, float)):
            v1 = self.ap(s1)
            reads.append(s1)
        if s2 is not None and not isinstance(s2, (int, float)):
            v2 = self.ap(s2)
            reads.append(s2)
        kw = {}
        if accum is not None:
            kw["accum_out"] = self.ap(accum)
            writes.append(accum)
        if op1 is None:
            self.S.op(e, lambda eng: eng.tensor_scalar(o, a, v1, None, op0, **kw), reads=reads, writes=writes)
        else:
            self.S.op(e, lambda eng: eng.tensor_scalar(o, a, v1, v2, op0, op1, **kw), reads=reads, writes=writes)

    def stt(self, out, in0, scalar, in1, op0, op1, accum=None):
        o, a, b = self.ap(out), self.ap(in0), self.ap(in1)
        reads = [in0, in1]
        writes = [out]
        sv = scalar
        if not isinstance(scalar, (int, float)):
            sv = self.ap(scalar)
            reads.append(scalar)
        kw = {}
        if accum is not None:
            kw["accum_out"] = self.ap(accum)
            writes.append(accum)
        self.S.op("dve", lambda eng: eng.scalar_tensor_tensor(o, a, sv, b, op0, op1, **kw),
                  reads=reads, writes=writes)

    def copy(self, out, in_, e="dve"):
        o, i = self.ap(out), self.ap(in_)
        if e == "act":
            self.S.op("act", lambda eng: eng.copy(o, i), reads=[in_], writes=[out])
        else:
            self.S.op(e, lambda eng: eng.tensor_copy(o, i), reads=[in_], writes=[out])

    def memset(self, out, val, e="dve"):
        o = self.ap(out)
        self.S.op(e, lambda eng: eng.memset(o, val), reads=[], writes=[out])

    def recip(self, out, in_):
        o, i = self.ap(out), self.ap(in_)
        self.S.op("dve", lambda eng: eng.reciprocal(o, i), reads=[in_], writes=[out])

    def reduce(self, out, in_, op, axis=AX.X):
        o, i = self.ap(out), self.ap(in_)
        self.S.op("dve", lambda eng: eng.tensor_reduce(o, i, axis, op), reads=[in_], writes=[out])

    def scan(self, out, d0, d1, init, op0, op1):
        o, a, b = self.ap(out), self.ap(d0), self.ap(d1)
        self.S.op("dve", lambda eng: eng.tensor_tensor_scan(o, a, b, init, op0, op1),
                  reads=[d0, d1], writes=[out])

    def cpred(self, out, mask, data):
        o, m, d = self.ap(out), self.ap(mask), self.ap(data)
        self.S.op("dve", lambda eng: eng.copy_predicated(o, m, d), reads=[mask, data, out], writes=[out])

    def ttr(self, out, in0, in1, op0, op1, accum, scale=1.0, scalar=0.0):
        o, a, b, ac = self.ap(out), self.ap(in0), self.ap(in1), self.ap(accum)
        self.S.op("dve", lambda eng: eng.tensor_tensor_reduce(o, a, b, op0, op1, scale, scalar, accum_out=ac)
                  if False else eng.tensor_tensor_reduce(out=o, in0=a, in1=b, op0=op0, op1=op1,
                                                         scale=scale, scalar=scalar, accum_out=ac),
                  reads=[in0, in1], writes=[out, accum])

    def vmax(self, out, in_):
        o, i = self.ap(out), self.ap(in_)
        self.S.op("dve", lambda eng: eng.max(o, i), reads=[in_], writes=[out])

    def vmaxidx(self, out, in_max, in_values):
        o, m, v = self.ap(out), self.ap(in_max), self.ap(in_values)
        self.S.op("dve", lambda eng: eng.max_index(o, m, v), reads=[in_max, in_values], writes=[out])

    def vmr(self, out, in_rep, in_values, imm):
        o, r, v = self.ap(out), self.ap(in_rep), self.ap(in_values)
        self.S.op("dve", lambda eng: eng.match_replace(o, r, v, imm), reads=[in_rep, in_values], writes=[out])

    def dma(self, out, in_, q="sync", **kw):
        o, i = self.ap(out), self.ap(in_)
        self.S.dma(q, lambda eng: eng.dma_start(out=o, in_=i, **kw), reads=[in_], writes=[out])

    def load_w(self, dst, src, rows, free_shape, key=None):
        n = 1
        for d_ in free_shape:
            n *= d_
        assert n <= 2048, n
        st = self._wst[self._wsi % len(self._wst)]
        self._wsi += 1
        sv = st[0:rows, 0:n]
        if len(free_shape) == 2:
            sv = sv.rearrange("p (a b) -> p a b", b=free_shape[1])
        self.dma(sv, src)
        self.copy(dst if key is None else (dst, key), sv, e="pool")

    def gather(self, out, in_, idx, bounds_check=None, **kw):
        o, i, ix = self.ap(out), self.ap(in_), self.ap(idx)
        regs = self.__dict__.setdefault("_bregs", {})

        def fn(eng):
            extra = dict(kw)
            if bounds_check is not None:
                if bounds_check not in regs:
                    regs[bounds_check] = eng.to_reg(bounds_check)
                extra["bounds_check"] = regs[bounds_check]
            return eng.indirect_dma_start(out=o, out_offset=None, in_=i,
                                          in_offset=bass.IndirectOffsetOnAxis(ap=ix, axis=0), **extra)
        self.S.dma("pool", fn, reads=[in_, idx], writes=[out])


def build_program(stages):
    nc = bass.Bass("TRN2", target_bir_lowering=False)
    kb = KB(nc)
    kb.S.open()
    blk_cm = nc.Block()
    kb.S.block = blk_cm.__enter__()
    dram = {}

    def din(name, shape, dt=F32):
        dram[name] = nc.dram_tensor(name, list(shape), dt, kind="ExternalInput").ap()
        return dram[name]

    def dout(name, shape, dt=F32):
        dram[name] = nc.dram_tensor(name, list(shape), dt, kind="ExternalOutput").ap()
        return dram[name]

    xT_all = din("xT_all", [D, TALL])
    c_onesb = din("c_onesb", [128, 128], BF16)
    c_identb = din("c_identb", [128, 128], BF16)
    p_normw = din("p_normw", [128, KT])
    w_kv = din("w_kv", [D, 1152])
    rope_all = din("rope_all", [128, 18, 2, 64])
    rope_idx = din("rope_idx", [128, 18, 2, 32])
    o_k = dout("o_k", [TALL, 512])
    o_v = dout("o_v", [TALL, 512])
    o_ki = dout("o_ki", [TALL, 64])

    onesb = kb.sb("onesb", [128, 128], BF16)
    identb = kb.sb("identb", [128, 128], BF16)
    normw = kb.sb("normw", [128, KT], F32)
    kb.dma(onesb[:], c_onesb)
    kb.dma(identb[:], c_identb)
    kb.dma(normw[:], p_normw)

    kb._wst = [kb.sb("wstage%d" % i, [128, 2048], F32) for i in range(1)]
    kb._wsi = 0
    yT = kb.sb("yT", [128, 8, NOWN], BF16)
    yTs = kb.sb("yTs", [128, 8, 128], BF16)
    kb.memset(yTs[:], 0.0)
    kT = kb.sb("kT", [128, 4, TALL], BF16)
    vtok = kb.sb("vtok", [128, 18, 512], BF16)
    kiT = kb.sb("kiT", [128, TALL], BF16)
    m_x = kb.mark()
    xnT = kb.sb("xnT", [128, KT, TALL], BF16)
    m_a1 = kb.mark()
    xstage = [kb.sb("xstage%d" % i, [128, KT, 256], F32) for i in range(2)]
    sq = [kb.sb("sq%d" % i, [128, 256], BF16) for i in range(2)]
    rstd = kb.sb("rstd", [128, 256], F32)
    ps_ss = kb.ps("ps_ss", [128, 512])
    xT_v = xT_all.rearrange("(kt p) t -> p kt t", p=128)
    blocks = [(0, 512), (512, 512), (1024, 512), (1536, 512), (2048, 256)]
    for bi, t0 in enumerate(range(0, TALL, 256)):
        nb = 256
        xs = xstage[bi % 2]
        for half in range(2):
            kb.dma((xs[:, half * 8:(half + 1) * 8, 0:nb], half), xT_v[:, half * 8:(half + 1) * 8, t0:t0 + nb])
        for kt in range(KT):
            s = sq[kt % 2]
            kb.act(s[:, 0:nb], (xs[:, kt, 0:nb], kt // 8), AF.Square)
            kb.mm(ps_ss[:, 0:nb], onesb[:], s[:, 0:nb], start=(kt == 0), stop=(kt == KT - 1))
        kb.act(rstd[:, 0:nb], ps_ss[:, 0:nb], AF.Sqrt, bias=EPS, scale=1.0 / D)
        kb.recip(rstd[:, 0:nb], rstd[:, 0:nb])
        for kt in range(KT):
            kb.stt(xnT[:, kt, t0:t0 + nb], (xs[:, kt, 0:nb], kt // 8), normw[:, kt:kt + 1], rstd[:, 0:nb],
                   ALU.mult, ALU.mult)
    kb.release(m_a1)


    if "a3" in stages:
        m_a3 = kb.mark()
        PS = [kb.ps("PS%d" % i, [128, 512]) for i in range(6)]
        PSB = [kb.ps("PSB%d" % i, [128, 1024], BF16) for i in range(2)]
        cst = {}
        for nm, shp, dt in [("c_blkb", [128, 128], BF16), ("c_mask12", [128, 2, 2, 128], F32),
                            ("c_mask3", [128, 2, 128], F32), ("c_identbd", [128, 2, 128], F32),
                            ("c_resetm", [128, 256], F32), ("c_padm", [128, 256], F32),
                            ("c_emask", [128, 2, 128], I32),
                            ("p_mu", [128, 25], F32), ("p_w0", [128, 8], F32), ("p_a0", [128, 8], F32),
                            ("p_kk", [128, 8], F32), ("p_ka", [128, 8], F32), ("p_rk", [128, 8], F32),
                            ("p_gnw", [128, 8, 64], F32), ("p_gnb", [128, 8, 64], F32),
                            ("sprevT", [128, 25, 4], F32)]:
            d = din(nm, shp, dt)
            t = kb.sb("s_" + nm, shp, dt)
            kb.dma(t[:], d)
            cst[nm] = t
        blkb, mask12, mask3, identbd = cst["c_blkb"], cst["c_mask12"], cst["c_mask3"], cst["c_identbd"]
        resetm, padm, emask = cst["c_resetm"], cst["c_padm"], cst["c_emask"]
        mu, sprevT = cst["p_mu"], cst["sprevT"]
        omm = kb.sb("omm", [128, 25], F32)
        kb.ts(omm[:], mu[:], -1.0, ALU.mult, 1.0, ALU.add)
        w0h = kb.sb("w0h", [128, 8], F32)
        a0h = kb.sb("a0h", [128, 8], F32)
        omka = kb.sb("omka", [128, 8], F32)
        kb.ts(w0h[:], cst["p_w0"][:], 0.5, ALU.mult)
        kb.ts(a0h[:], cst["p_a0"][:], 0.5, ALU.mult)
        kb.ts(omka[:], cst["p_ka"][:], -1.0, ALU.mult, 1.0, ALU.add)
        w_c = din("w_c", [D, 128])
        w_rkv = din("w_rkv", [8, D, 384])
        w2a2_d = din("w2a2", [128, 1024])
        st_wkvT = din("st_wkvT", [4, 8, 128, 64])
        o_shift = dout("o_shift", [128, 25, 5])
        o_wkv = dout("o_wkv", [5, 8, 128, 64])
        w2a2 = kb.sb("w2a2s", [128, 1024], BF16)
        kb.load_w(w2a2[:], w2a2_d, 128, [1024])
        shst = kb.sb("shst", [128, 25, 5], F32)

        lora_in = kb.sb("lora_in", [128, TALL], BF16)
        m_a2 = kb.mark()
        wc = kb.sb("wc", [128, KT, 128], BF16)
        kb.load_w(wc[:], w_c.rearrange("(kt p) c -> p kt c", p=128), 128, [KT, 128])
        sh24 = kb.sb("sh24", [128, TALL + 1], F32)
        tmp24 = kb.sb("tmp24", [128, TALL], F32)
        xs24 = kb.sb("xs24", [128, TALL], F32)
        kb.memset(sh24[:, 0:1], 0.0)
        for bi, (t0, nb) in enumerate(blocks):
            pp = PS[bi % 2]
            for kt in range(KT):
                kb.mm(pp[:, 0:nb], wc[:, kt, :], xnT[:, kt, t0:t0 + nb], start=(kt == 0), stop=(kt == KT - 1))
            kb.copy(sh24[:, 1 + t0:1 + t0 + nb], pp[:, 0:nb], e="act")
        kb.ts(tmp24[:], sh24[:, 1:TALL + 1], omm[:, 24:25], ALU.mult)
        kb.stt(xs24[:], sh24[:, 0:TALL], mu[:, 24:25], tmp24[:], ALU.mult, ALU.add)
        kb.stt(xs24[:, SEQ:TALL:64], sprevT[:, 24, :], mu[:, 24:25], tmp24[:, SEQ:TALL:64], ALU.mult, ALU.add)
        kb.copy(shst[:, 24, 0:1], sh24[:, SEQ:SEQ + 1])
        kb.copy(shst[:, 24, 1:5], sh24[:, 1 + SEQ + 3:1 + TALL:64])
        kb.act(lora_in[0:64, :], xs24[0:64, :], AF.Tanh)
        kb.copy(lora_in[64:128, :], xs24[64:128, :], e="act")
        kb.release(m_a2)

        wr = [kb.sb("wr0", [128, KT, 384], BF16)] * 2
        blocks3 = [(t0_, 256) for t0_ in range(0, SEQ, 256)] + [(SEQ, 256)]
        LASTP, SAMPB = 7, 8
        raw = [kb.sb("raw0", [128, 3, 257], F32)] * 2
        carry = kb.sb("carry", [128, 3, 1], F32)
        xs = kb.sb("xs", [128, 3, 256], F32)
        f = {nm: kb.sb("f_" + nm, [128, 256], F32) for nm in
             ["logw", "a", "kk", "nrm", "kmod", "beta", "cum", "Pinv", "Pex"]}
        f["P"] = f["nrm"]
        f["btf"] = f["cum"]
        f["ktf"] = f["a"]
        f["thw"] = f["logw"]
        f["kkn"] = f["kk"]
        f["t1"] = f["kmod"]
        f["dl"] = f["Pex"]
        kksq = kb.sb("kksq", [128, 256], BF16)
        NCH = 4
        ARbd = kb.sb("ARbd", [128, NCH, 2, 128], BF16)
        Btbd = kb.sb("Btbd", [128, NCH, 128], BF16)
        Ktbd = kb.sb("Ktbd", [128, NCH, 128], BF16)
        Bhbd = kb.sb("Bhbd", [128, NCH, 128], BF16)
        Khbd = kb.sb("Khbd", [128, NCH, 128], BF16)
        Vbd = kb.sb("Vbd", [128, NCH, 128], BF16)
        prodbd = kb.sb("prodbd", [128, NCH, 128], BF16)
        ybd = kb.sb("ybd", [128, 4, 128], BF16)
        for t_ in (ARbd, Btbd, Ktbd, Bhbd, Khbd, Vbd, prodbd, ybd):
            kb.memset(t_[:], 0.0, e="pool")
        BhS = kb.sb("BhS", [128, NCH, 128], BF16)
        KhS = kb.sb("KhS", [128, NCH, 128], BF16)
        VS = kb.sb("VS", [128, NCH, 64], BF16)
        S1 = kb.sb("S1", [128, 2, 2, 128], BF16)
        S2 = kb.sb("S2", [128, 2, 2, 128], BF16)
        XX = [kb.sb("XX%d" % i, [128, 2, 2, 128], BF16) for i in range(2)]
        TT = [kb.sb("TT%d" % i, [128, 2, 128], BF16) for i in range(2)]
        Z = kb.sb("Z", [128, 64], F32)
        Zb = kb.sb("Zb", [128, 64], BF16)
        Zs = kb.sb("Zs", [128, 4, 64], F32)
        Wb = kb.sb("Wb", [128, 64], BF16)
        Ub = kb.sb("Ub", [128, 64], BF16)
        Ybuf = kb.sb("Ybuf", [128, NCH, 64], F32)
        cen = kb.sb("cen", [128, NCH, 64], F32)
        csq = Ybuf
        st8 = {nm: kb.sb("st_" + nm, [128, NCH], F32) for nm in ["mean", "var", "rstd", "bon"]}
        yfin = Ybuf

        def cview(t_, nb):
            return t_[:, 0:nb].rearrange("p (c t) -> p c t", t=64)

        import os as _os
        for p in range(int(_os.environ.get('A3_PAIRS', '8'))):
            wrp = wr[p % 2]
            wv = w_rkv[p].rearrange("(kt p) c -> p kt c", p=128)
            for q4 in range(4):
                kb.load_w(wrp[:, q4 * 4:(q4 + 1) * 4, :], wv[:, q4 * 4:(q4 + 1) * 4, :], 128, [4, 384], key=q4 // 2)
            kb.memset(carry[:], 0.0)
            kb.memset(Z[:], 0.0)
            kb.memset(Zb[:], 0.0)
            kb.dma(Zs[:], st_wkvT[:, p].rearrange("s q v -> q s v"))
            for bi, (t0, nb) in enumerate(blocks3):
                nch = nb // 64
                rw = raw[bi % 2]
                samp = (bi == SAMPB)
                for j in range(3):
                    pp = PS[j % 2]
                    for kt in range(KT):
                        kb.mm(pp[:, 0:nb], (wrp[:, kt, j * 128:(j + 1) * 128], kt // 8), xnT[:, kt, t0:t0 + nb],
                              start=(kt == 0), stop=(kt == KT - 1))
                    kb.copy(rw[:, j, 0:1], carry[:, j, :])
                    kb.copy(rw[:, j, 1:1 + nb], pp[:, 0:nb], e="act")
                    tj = j * 8 + p
                    tmpj = f[("P", "Pinv", "Pex")[j]]
                    kb.ts(tmpj[:, 0:nb], rw[:, j, 1:1 + nb], omm[:, tj:tj + 1], ALU.mult)
                    kb.stt(xs[:, j, 0:nb], rw[:, j, 0:nb], mu[:, tj:tj + 1], tmpj[:, 0:nb], ALU.mult, ALU.add)
                    if samp:
                        kb.stt(xs[:, j, 0:nb:64], sprevT[:, tj, :], mu[:, tj:tj + 1], tmpj[:, 0:nb:64],
                               ALU.mult, ALU.add)
                        kb.copy(shst[:, tj, 1:5], rw[:, j, 4:1 + nb:64])
                        if j > 0:
                            kb.tt(xs[:, j, 0:nb], xs[:, j, 0:nb], padm[:, 0:nb], ALU.mult)
                    else:
                        kb.copy(carry[:, j, :], rw[:, j, nb:nb + 1])
                        if bi == LASTP:
                            kb.copy(shst[:, tj, 0:1], rw[:, j, nb:nb + 1])
                xr, xk, xv = xs[:, 0, 0:nb], xs[:, 1, 0:nb], xs[:, 2, 0:nb]
                kb.mm(PS[0][:, 0:nb], w2a2[0:64, p * 128:(p + 1) * 128], lora_in[0:64, t0:t0 + nb])
                kb.mm(PS[1][:, 0:nb], w2a2[64:128, p * 128:(p + 1) * 128], lora_in[64:128, t0:t0 + nb])
                kb.act(f["thw"][:, 0:nb], PS[0][:, 0:nb], AF.Tanh, bias=w0h[:, p:p + 1], scale=0.5)
                kb.ts(f["logw"][:, 0:nb], f["thw"][:, 0:nb], -0.30326533, ALU.mult, -0.30326533, ALU.add)
                if samp:
                    kb.tt(f["logw"][:, 0:nb], f["logw"][:, 0:nb], padm[:, 0:nb], ALU.mult)
                kb.act(f["a"][:, 0:nb], PS[1][:, 0:nb], AF.Tanh, bias=a0h[:, p:p + 1], scale=0.5)
                kb.ts(f["a"][:, 0:nb], f["a"][:, 0:nb], 0.5, ALU.mult, 0.5, ALU.add)
                kb.ts(f["kk"][:, 0:nb], xk, cst["p_kk"][:, p:p + 1], ALU.mult)
                kb.act(kksq[:, 0:nb], f["kk"][:, 0:nb], AF.Square)
                kb.mm(PS[0][:, 0:nb], blkb[:], kksq[:, 0:nb])
                kb.act(f["nrm"][:, 0:nb], PS[0][:, 0:nb], AF.Sqrt)
                kb.ts(f["nrm"][:, 0:nb], f["nrm"][:, 0:nb], 1e-12, ALU.max)
                kb.recip(f["nrm"][:, 0:nb], f["nrm"][:, 0:nb])
                kb.tt(f["kkn"][:, 0:nb], f["kk"][:, 0:nb], f["nrm"][:, 0:nb], ALU.mult)
                kb.ts(f["t1"][:, 0:nb], f["a"][:, 0:nb], cst["p_ka"][:, p:p + 1], ALU.mult, omka[:, p:p + 1], ALU.add)
                kb.tt(f["kmod"][:, 0:nb], xk, f["t1"][:, 0:nb], ALU.mult)
                kb.tt(f["beta"][:, 0:nb], f["kkn"][:, 0:nb], f["a"][:, 0:nb], ALU.mult)
                kb.scan(f["cum"][:, 0:nb], resetm[:, 0:nb], f["logw"][:, 0:nb], 0.0, ALU.mult, ALU.add)
                kb.act(f["P"][:, 0:nb], f["cum"][:, 0:nb], AF.Exp)
                kb.act(f["Pinv"][:, 0:nb], f["cum"][:, 0:nb], AF.Exp, scale=-1.0)
                kb.tt(f["dl"][:, 0:nb], f["cum"][:, 0:nb], f["logw"][:, 0:nb], ALU.subtract)
                kb.act(f["Pex"][:, 0:nb], f["dl"][:, 0:nb], AF.Exp)
                kb.tt(f["btf"][:, 0:nb], f["beta"][:, 0:nb], f["Pinv"][:, 0:nb], ALU.mult)
                kb.tt(f["ktf"][:, 0:nb], f["kmod"][:, 0:nb], f["Pinv"][:, 0:nb], ALU.mult)
                PCb = cview(f["P"], nb)[:, :, 63:64].to_broadcast([128, nch, 64])
                for h2 in range(2):
                    hs = slice(h2 * 64, h2 * 64 + 64)
                    cs = slice(h2 * 64, h2 * 64 + 64)
                    kb.stt(ARbd[hs, 0:nch, 0, cs], cview(f["kkn"], nb)[hs], -1.0, cview(f["Pex"], nb)[hs],
                           ALU.mult, ALU.mult)
                    kb.tt(ARbd[hs, 0:nch, 1, cs], cview(xs[:, 0, :], nb)[hs], cview(f["P"], nb)[hs], ALU.mult,
                          e="pool")
                    kb.copy(Btbd[hs, 0:nch, cs], cview(f["btf"], nb)[hs], e="pool")
                    kb.copy(Ktbd[hs, 0:nch, cs], cview(f["ktf"], nb)[hs], e="pool")
                    kb.tt(Bhbd[hs, 0:nch, cs], cview(f["btf"], nb)[hs], PCb[hs], ALU.mult)
                    kb.tt(Khbd[hs, 0:nch, cs], cview(f["ktf"], nb)[hs], PCb[hs], ALU.mult, e="pool")
                    kb.copy(Vbd[hs, 0:nch, cs], cview(xs[:, 2, :], nb)[hs], e="pool")
                    kb.stt(prodbd[hs, 0:nch, cs], cview(xs[:, 0, :], nb)[hs], cst["p_rk"][hs, p:p + 1],
                           cview(f["kmod"], nb)[hs], ALU.mult, ALU.mult)
                for src, dst in ((Bhbd, BhS), (Khbd, KhS)):
                    pb = PSB[0]
                    for c in range(nch):
                        kb.tr(pb[:, c * 128:(c + 1) * 128], src[:, c, :], identb[:])
                    kb.copy(dst[:, 0:nch, :].rearrange("p c k -> p (c k)"), pb[:, 0:nch * 128], e="act")
                pb = PSB[1]
                for c in range(nch):
                    kb.tr(pb[:, c * 128:(c + 1) * 128], Vbd[:, c, :], identb[:])
                for h2 in range(2):
                    hs = slice(h2 * 64, h2 * 64 + 64)
                    kb.copy(VS[hs, 0:nch, :], pb[hs, 0:nch * 128].rearrange("p (c k) -> p c k", k=128)[:, :, h2 * 64:h2 * 64 + 64],
                            e="act")
                for c in range(nch):
                    kb.mm(PS[5][:, 448 + c * 2:450 + c * 2], prodbd[:, c, :], onesb[:, 0:2])
                kb.copy(st8["bon"][:, 0:nch], PS[5][:, 448:448 + 2 * nch:2])
                for cg in range(nch // 2):
                    ps1 = PS[2][:].rearrange("p (g w k) -> p g w k", w=2, g=2)
                    ps2 = PS[3][:].rearrange("p (g w k) -> p g w k", w=2, g=2)
                    ps3 = PS[4][:, 0:256].rearrange("p (g k) -> p g k", g=2)
                    pB = PS[4][:, 256:512].rearrange("p (g k) -> p g k", g=2)
                    for g in range(2):
                        c = cg * 2 + g
                        kb.mm(ps1[:, g, :, :], Btbd[:, c, :], ARbd[:, c, :, :])
                        kb.mm(ps2[:, g, :, :], Ktbd[:, c, :], ARbd[:, c, :, :])
                        kb.mm(ps3[:, g, :], ARbd[:, c, 0, :], Btbd[:, c, :])
                    kb.tt(S1[:], ps1, mask12[:], ALU.mult)
                    kb.tt(S2[:], ps2, mask12[:], ALU.mult)
                    X0 = XX[0]
                    kb.tt(X0[:, :, 0, :], ps3, mask3[:], ALU.mult)
                    kb.copy(X0[:, :, 1, :], S1[:, :, 0, :], e="pool")
                    kb.tt(TT[0][:], S1[:, :, 0, :], identbd[:], ALU.add, e="pool")
                    cur = 0
                    tcur = 0
                    for lvl in range(1, 6):
                        Xc, Xn = XX[cur], XX[1 - cur]
                        pA = PS[cur][:].rearrange("p (g w k) -> p g w k", g=2, w=2)
                        for g in range(2):
                            kb.mm(pA[:, g, 0, :], Xc[:, g, 1, :], Xc[:, g, 0, :])
                            if lvl < 5:
                                kb.mm(pA[:, g, 1, :], Xc[:, g, 0, :], Xc[:, g, 1, :])
                        if lvl >= 2:
                            Tc, Tn = TT[tcur], TT[1 - tcur]
                            for g in range(2):
                                kb.mm(pB[:, g, :], Xc[:, g, 0, :], Tc[:, g, :])
                        if lvl < 5:
                            kb.copy(Xn[:], pA, e="act")
                        else:
                            kb.copy(Xn[:, :, 0, :], pA[:, :, 0, :], e="act")
                        if lvl >= 2:
                            kb.tt(Tn[:], pB, Tc[:], ALU.add)
                            tcur = 1 - tcur
                        cur = 1 - cur
                    Xc = XX[cur]
                    Tc, Tn = TT[tcur], TT[1 - tcur]
                    for g in range(2):
                        kb.mm(pB[:, g, :], Xc[:, g, 0, :], Tc[:, g, :])
                    kb.tt(Tn[:], pB, Tc[:], ALU.add)
                    tcur = 1 - tcur
                    cur = tcur
                    Tf = TT[cur]
                    for g in range(2):
                        c = cg * 2 + g
                        sq_ = PS[5]
                        if samp:
                            kb.copy(Z[:], Zs[:, c, :])
                            kb.copy(Zb[:], Zs[:, c, :], e="act")
                        kb.mm(sq_[:, 0:64], ARbd[:, c, 0, :], Zb[:], start=True, stop=False)
                        kb.mm(sq_[:, 0:64], S2[:, g, 0, :], VS[:, c, :], start=False, stop=True)
                        kb.copy(Wb[:], sq_[:, 0:64], e="act")
                        kb.mm(sq_[:, 64:128], Tf[:, g, :], Wb[:])
                        kb.copy(Ub[:], sq_[:, 64:128])
                        kb.mm(sq_[:, 192:256], BhS[:, c, :], Ub[:], start=True, stop=False)
                        kb.mm(sq_[:, 192:256], KhS[:, c, :], VS[:, c, :], start=False, stop=True)
                        py_ = PS[4][:, 0:64]
                        kb.mm(py_, ARbd[:, c, 1, :], Zb[:], start=True, stop=False)
                        kb.mm(py_, S1[:, g, 1, :], Ub[:], start=False, stop=False)
                        kb.mm(py_, S2[:, g, 1, :], VS[:, c, :], start=False, stop=True)
                        kb.stt(Zb[:], Z[:], f["P"][:, c * 64 + 63:c * 64 + 64], sq_[:, 192:256], ALU.mult, ALU.add)
                        kb.stt(Z[:], Z[:], f["P"][:, c * 64 + 63:c * 64 + 64], sq_[:, 192:256], ALU.mult, ALU.add)
                        kb.copy(Ybuf[:, c, :], py_, e="act")
                        if samp:
                            kb.dma(o_wkv[1 + c, p], Z[:])
                if bi == LASTP:
                    kb.dma(o_wkv[0, p], Z[:])
                Yb = Ybuf[:, 0:nch, :]
                kb.reduce(st8["mean"][:, 0:nch], Yb, ALU.add)
                kb.ts(st8["mean"][:, 0:nch], st8["mean"][:, 0:nch], 1.0 / 64, ALU.mult)
                kb.tt(cen[:, 0:nch, :], Yb, st8["mean"][:, 0:nch].rearrange("p (c o) -> p c o", o=1).to_broadcast([128, nch, 64]),
                      ALU.subtract)
                kb.act(csq[:, 0:nch, :], cen[:, 0:nch, :], AF.Square)
                kb.reduce(st8["var"][:, 0:nch], csq[:, 0:nch, :], ALU.add)
                kb.act(st8["rstd"][:, 0:nch], st8["var"][:, 0:nch], AF.Sqrt, bias=GN_EPS, scale=1.0 / 64)
                kb.recip(st8["rstd"][:, 0:nch], st8["rstd"][:, 0:nch])
                kb.tt(cen[:, 0:nch, :], cen[:, 0:nch, :],
                      st8["rstd"][:, 0:nch].rearrange("p (c o) -> p c o", o=1).to_broadcast([128, nch, 64]), ALU.mult)
                kb.tt(cen[:, 0:nch, :], cen[:, 0:nch, :], cst["p_gnw"][:, p:p + 1, :].to_broadcast([128, nch, 64]), ALU.mult)
                kb.tt(cen[:, 0:nch, :], cen[:, 0:nch, :], cst["p_gnb"][:, p:p + 1, :].to_broadcast([128, nch, 64]), ALU.add)
                kb.tt(csq[:, 0:nch, :], VS[:, 0:nch, :],
                      st8["bon"][:, 0:nch].rearrange("p (c o) -> p c o", o=1).to_broadcast([128, nch, 64]), ALU.mult)
                kb.tt(yfin[:, 0:nch, :], cen[:, 0:nch, :], csq[:, 0:nch, :], ALU.add)
                pb = PSB[0]
                if not samp:
                    yv = yfin[:, 0:4, :].rearrange("p (m two) v -> p m two v", two=2)
                    for h2 in range(2):
                        hs = slice(h2 * 64, h2 * 64 + 64)
                        cs = slice(h2 * 64, h2 * 64 + 64)
                        kb.copy(ybd[hs, 0:2, cs], yv[hs, :, 0, :])
                        kb.cpred(ybd[hs, 0:2, cs], emask[hs, 0:2, 0:64], yv[hs, :, 1, :])
                    for m_ in range(2):
                        kb.tr(pb[:, m_ * 128:(m_ + 1) * 128], ybd[:, m_, :], identb[:])
                    pv4 = pb[:, 0:256].rearrange("p (m k) -> p m k", k=128)
                    for h2 in range(2):
                        hs = slice(h2 * 64, h2 * 64 + 64)
                        kb.copy(yT[hs, p, bi * 128:(bi + 1) * 128].rearrange("p (m t) -> p m t", t=64),
                                pv4[hs, :, h2 * 64:h2 * 64 + 64], e="act")
                else:
                    for h2 in range(2):
                        hs = slice(h2 * 64, h2 * 64 + 64)
                        cs = slice(h2 * 64, h2 * 64 + 64)
                        kb.copy(ybd[hs, :, cs], yfin[hs, 0:4, :])
                    for m_ in range(4):
                        kb.tr(pb[:, m_ * 128:(m_ + 1) * 128], ybd[:, m_, :], identb[:])
                    pv4 = pb[:, 0:512].rearrange("p (m k) -> p m k", k=128)
                    for h2 in range(2):
                        hs = slice(h2 * 64, h2 * 64 + 64)
                        for sb_ in range(4):
                            kb.copy(yTs[hs, p, SROWS[sb_]:SROWS[sb_] + 4], pv4[hs, sb_, h2 * 64:h2 * 64 + 4], e="act")
        kb.dma(o_shift, shst[:])
        kb.release(m_a3)

    if "a4" in stages:
        m_a4 = kb.mark()
        ropeA = kb.sb("ropeA", [128, 18, 2, 64], F32)
        kb.dma(ropeA[:], rope_all)
        ropeI = kb.sb("ropeI", [128, 18, 2, 32], F32)
        kb.dma(ropeI[:], rope_idx)
        import os as _os
        wkv = kb.sb("wkv", [128, KT, 512], BF16)
        w_kv_v = w_kv.rearrange("(kt p) c -> p kt c", p=128)
        ps_k = [kb.ps("ps_k%d" % i, [128, 512]) for i in range(2)]
        ps_t = kb.ps("ps_t", [128, 4, 128], BF16)
        kraw = [kb.sb("kraw%d" % i, [128, 4, 2, 64], F32) for i in range(2)]
        kro = [kb.sb("kro%d" % i, [128, 4, 2, 64], F32) for i in range(2)]
        krob = [kb.sb("krob%d" % i, [128, 4, 128], BF16) for i in range(2)]
        rt = [kb.sb("rt%d" % i, [128, 4, 64], F32) for i in range(4)]

        def load_cols(c0, ncol):
            if _os.environ.get('A4_CASTDMA'):
                for q4 in range(4):
                    kb.dma((wkv[:, q4 * 4:(q4 + 1) * 4, 0:ncol], q4), w_kv_v[:, q4 * 4:(q4 + 1) * 4, c0:c0 + ncol], q="pool")
                return
            for q4 in range(4):
                kb.load_w(wkv[:, q4 * 4:(q4 + 1) * 4, 0:ncol], w_kv_v[:, q4 * 4:(q4 + 1) * 4, c0:c0 + ncol],
                          128, [4, ncol], key=q4)

        def proj(pk, tok, ncol):
            for kt in range(KT):
                kb.mm(pk[:, 0:ncol], xnT[:, kt, tok], (wkv[:, kt, 0:ncol], kt // 4), start=(kt == 0), stop=(kt == KT - 1))

        load_cols(0, 512)
        for tt in range(18 if 'k' in _os.environ.get('A4_PARTS', 'kvi') else 0):
            pk = ps_k[tt % 2]
            tok = slice(tt * 128, (tt + 1) * 128)
            proj(pk, tok, 512)
            kr, ko, kbf = kraw[tt % 2], kro[tt % 2], krob[tt % 2]
            kb.copy(kr[:].rearrange("p a b c -> p (a b c)"), pk[:], e="act")
            cos = ropeA[:, tt:tt + 1, 0, :].to_broadcast([128, 4, 64])
            sin = ropeA[:, tt:tt + 1, 1, :].to_broadcast([128, 4, 64])
            x1, x2 = kr[:, :, 0, :], kr[:, :, 1, :]
            kb.tt(rt[0][:], x1, cos, ALU.mult)
            kb.tt(rt[1][:], x2, sin, ALU.mult, e="pool")
            kb.tt(ko[:, :, 0, :], rt[0][:], rt[1][:], ALU.subtract)
            kb.tt(rt[2][:], x2, cos, ALU.mult, e="pool")
            kb.tt(rt[3][:], x1, sin, ALU.mult)
            kb.tt(ko[:, :, 1, :], rt[2][:], rt[3][:], ALU.add, e="pool")
            kb.dma(o_k[tok, :], ko[:].rearrange("p a b c -> p (a b c)"))
            kb.copy(kbf[:], ko[:].rearrange("p a b c -> p a (b c)"), e="act")
            for h in range(4):
                kb.tr(ps_t[:, h, :], kbf[:, h, :], identb[:])
            kb.copy(kT[:, :, tok], ps_t[:], e="dve")
        load_cols(512, 512)
        for tt in range(int(_os.environ.get('A4_NT', '18')) if 'v' in _os.environ.get('A4_PARTS', 'kvi') else 0):
            pk = ps_k[tt % 2]
            tok = slice(tt * 128, (tt + 1) * 128)
            proj(pk, tok, 512)
            vf_ = kraw[tt % 2]
            kb.copy(vf_[:].rearrange("p a b c -> p (a b c)"), pk[:], e="act")
            kb.dma(o_v[tok, :], vf_[:].rearrange("p a b c -> p (a b c)"))
            kb.copy(vtok[:, tt, :], vf_[:].rearrange("p a b c -> p (a b c)"), e="dve")
        load_cols(1024, 128)
        kiraw = kb.sb("kiraw", [128, 2, 2, 32], F32)
        kiro = [kb.sb("kiro%d" % i, [128, 2, 2, 32], F32) for i in range(2)]
        kib = kb.sb("kib", [128, 128], BF16)
        rti = [kb.sb("rti%d" % i, [128, 2, 32], F32) for i in range(4)]
        for tt in range(18 if 'i' in _os.environ.get('A4_PARTS', 'kvi') else 0):
            pk = ps_k[tt % 2]
            tok = slice(tt * 128, (tt + 1) * 128)
            proj(pk, tok, 128)
            kb.copy(kiraw[:].rearrange("p a b c -> p (a b c)"), pk[:, 0:128], e="act")
            kio = kiro[tt % 2]
            cosi = ropeI[:, tt:tt + 1, 0, :].to_broadcast([128, 2, 32])
            sini = ropeI[:, tt:tt + 1, 1, :].to_broadcast([128, 2, 32])
            y1, y2 = kiraw[:, :, 0, :], kiraw[:, :, 1, :]
            kb.tt(rti[0][:], y1, cosi, ALU.mult)
            kb.tt(rti[1][:], y2, sini, ALU.mult, e="pool")
            kb.tt(kio[:, :, 0, :], rti[0][:], rti[1][:], ALU.subtract)
            kb.tt(rti[2][:], y2, cosi, ALU.mult, e="pool")
            kb.tt(rti[3][:], y1, sini, ALU.mult)
            kb.tt(kio[:, :, 1, :], rti[2][:], rti[3][:], ALU.add, e="pool")
            kb.dma(o_ki[tok, :], kio[:, 0, :, :].rearrange("p b c -> p (b c)"))
            kb.copy(kib[:], kio[:].rearrange("p a b c -> p (a b c)"), e="act")
            kb.tr(ps_t[:, 0, :], kib[:], identb[:])
            kb.copy(kiT[:, tok], ps_t[:, 0, :], e="dve")
        kb.release(m_a4)


    kb.release(m_x)
    if "b" in stages:
        import os as _os
        NT = int(_os.environ.get("B_NT", "8"))
        xT_own = din("xT_own", [D, NOWNS])
        memT = din("memT", [D, 256])
        p_memnw = din("p_memnw", [128, KT])
        w_gr = din("w_gr", [D, 1024])
        w_qm = din("w_qm", [D, 512])
        w_tok = din("w_tok", [D, 2576])
        w_mem = din("w_mem", [D, 1024])
        w_out_d = din("w_out", [D, D])
        rope_own = din("rope_own", [128, 9, 2, 64])
        rope_owni = din("rope_owni", [128, 9, 2, 32])
        qpos_d = din("qpos", [128, 9])
        kpos_d = din("c_kpos", [128, SEQ])
        pw2_d = din("c_pw2", [128, 24])
        fnw_d = din("p_fnw", [128, D])
        x_own_tok = din("x_own_tok", [NOWNS, D])
        o_memk = dout("o_memk", [128, 4, 256])
        o_memv = dout("o_memv", [256, 512])
        o_y = dout("o_y", [NOWNS, D])
        SC = 128.0 ** -0.5

        yamT = kb.sb("yamT", [128, 8, NOWNS], BF16)
        m_bp = kb.mark()
        qT = kb.sb("qT", [128, 4, NOWNS], BF16)
        qmT = kb.sb("qmT", [128, 4, NOWNS], BF16)
        qiT = kb.sb("qiT", [128, 8, NOWNS], BF16)
        sga = kb.sb("sga", [128, 9, 512], BF16)
        sgm = kb.sb("sgm", [128, 9, 512], BF16)
        wis = kb.sb("wis", [128, 9, 16], F32)
        mkT = kb.sb("mkT", [128, 4, 256], BF16)
        mvb = kb.sb("mvb", [128, 2, 512], BF16)
        qsb = kb.sb("qsb", [128, 512], BF16)
        qpos = kb.sb("qpos_s", [128, 9], F32)
        kb.dma(qpos[:], qpos_d)
        PS = [kb.ps("QS%d" % i, [128, 512]) for i in range(7)]
        PSB = kb.ps("QSB", [128, 1024], BF16)

        m_b1 = kb.mark()
        ropeO = kb.sb("ropeO", [128, 9, 2, 64], F32)
        ropeOI = kb.sb("ropeOI", [128, 9, 2, 32], F32)
        memnw = kb.sb("memnw", [128, KT], F32)
        kb.dma(ropeO[:], rope_own)
        kb.dma(ropeOI[:], rope_owni)
        kb.dma(memnw[:], p_memnw)

        def norm_T(src_d, dst, nw, blks, xst, sqb, rstdb):
            sv = src_d.rearrange("(kt p) t -> p kt t", p=128)
            for (t0, nb) in blks:
                for half in range(2):
                    kb.dma((xst[:, half * 8:(half + 1) * 8, 0:nb], half), sv[:, half * 8:(half + 1) * 8, t0:t0 + nb])
                for kt in range(KT):
                    s_ = sqb[kt % 2]
                    kb.act(s_[:, 0:nb], (xst[:, kt, 0:nb], kt // 8), AF.Square)
                    kb.mm(PS[0][:, 0:nb], onesb[:], s_[:, 0:nb], start=(kt == 0), stop=(kt == KT - 1))
                kb.act(rstdb[:, 0:nb], PS[0][:, 0:nb], AF.Sqrt, bias=EPS, scale=1.0 / D)
                kb.recip(rstdb[:, 0:nb], rstdb[:, 0:nb])
                for kt in range(KT):
                    kb.stt(dst[:, kt, t0:t0 + nb], (xst[:, kt, 0:nb], kt // 8), nw[:, kt:kt + 1], rstdb[:, 0:nb],
                           ALU.mult, ALU.mult)

        def staging(tag):
            return (kb.sb("xst" + tag, [128, KT, 128], F32),
                    [kb.sb("sqb%s%d" % (tag, i_), [128, 128], BF16) for i_ in range(2)],
                    kb.sb("rstdb" + tag, [128, 128], F32))

        def mk_load_wt(wt_):
            def load_wt(src_d, c0, ncol):
                sv = src_d.rearrange("(kt p) c -> p kt c", p=128)
                step = min(max(1, 2048 // ncol), 4)
                for k0 in range(0, KT, step):
                    kb.load_w(wt_[:, k0:k0 + step, 0:ncol], sv[:, k0:k0 + step, c0:c0 + ncol], 128, [step, ncol],
                              key=k0 // 4)
            return load_wt

        m_mem = kb.mark()
        wt = kb.sb("wtm", [128, KT, 512], BF16)
        load_wt = mk_load_wt(wt)
        memnT = kb.sb("memnT", [128, KT, 256], BF16)
        mkf = kb.sb("mkf", [128, 4, 256], F32)
        xst, sqb, rstdb = staging("m")
        norm_T(memT, memnT, memnw, [(0, 128), (128, 128)], xst, sqb, rstdb)
        load_wt(w_mem, 0, 512)
        for h in range(4):
            pg = PS[h % 2]
            for kt in range(KT):
                kb.mm(pg[:, 0:256], (wt[:, kt, h * 128:(h + 1) * 128], kt // 4), memnT[:, kt, :],
                      start=(kt == 0), stop=(kt == KT - 1))
            kb.copy(mkf[:, h, :], pg[:, 0:256], e="act")
        kb.dma(o_memk, mkf[:])
        kb.copy(mkT[:], mkf[:], e="dve")
        load_wt(w_mem, 512, 512)
        mvf = mkf[:].rearrange("p a b -> p (a b)").rearrange("p (m c) -> p m c", c=512)
        for mt in range(2):
            pg = PS[mt % 2]
            for kt in range(KT):
                kb.mm(pg[:], memnT[:, kt, mt * 128:(mt + 1) * 128], (wt[:, kt, :], kt // 4),
                      start=(kt == 0), stop=(kt == KT - 1))
            kb.copy(mvf[:, mt, :], pg[:], e="act")
            kb.dma(o_memv[mt * 128:(mt + 1) * 128, :], mvf[:, mt, :])
        kb.copy(mvb[:], mvf, e="dve")
        kb.release(m_mem)

        xnTo = kb.sb("xnTo", [128, KT, NOWNS], BF16)
        m_b0 = kb.mark()
        xst, sqb, rstdb = staging("o")
        norm_T(xT_own, xnTo, normw, [(t_, 128) for t_ in range(0, NOWNS, 128)], xst, sqb, rstdb)
        kb.release(m_b0)
        oblocks = [(0, 512), (512, 512), (1024, 128)]
        wt = kb.sb("wt", [128, KT, 512], BF16)
        load_wt = mk_load_wt(wt)
        tf1 = kb.sb("tf1", [128, 512], F32)
        tb1 = kb.sb("tb1", [128, 512], BF16)
        gtmp = [tf1, tf1]
        tf = [tf1, tf1]
        tb = [tb1, tb1]
        rr = [kb.sb("rr0", [128, 256], F32), kb._wst[0][:, 0:256]]
        for half in range(2):
            load_wt(w_gr, half * 512, 512)
            for pp_ in range(4):
                p = half * 4 + pp_
                for bi_, (t0, nb) in enumerate(oblocks):
                    pg = PS[bi_ % 2]
                    for kt in range(KT):
                        kb.mm(pg[:, 0:nb], (wt[:, kt, pp_ * 128:(pp_ + 1) * 128], kt // 4), xnTo[:, kt, t0:t0 + nb],
                              start=(kt == 0), stop=(kt == KT - 1))
                    g0 = gtmp[bi_ % 2]
                    kb.act(g0[:, 0:nb], pg[:, 0:nb], AF.Tanh, scale=0.5)
                    kb.stt(g0[:, 0:nb], g0[:, 0:nb], 1.0, pg[:, 0:nb], ALU.add, ALU.mult)
                    ydst = yT[:, p, t0:t0 + nb] if t0 < NOWN else yTs[:, p, :]
                    kb.stt(ydst, g0[:, 0:nb], 0.5, ydst, ALU.mult, ALU.mult)
        load_wt(w_qm, 0, 512)
        for h in range(4):
            for bi_, (t0, nb) in enumerate(oblocks):
                pg = PS[bi_ % 2]
                for kt in range(KT):
                    kb.mm(pg[:, 0:nb], (wt[:, kt, h * 128:(h + 1) * 128], kt // 4), xnTo[:, kt, t0:t0 + nb],
                          start=(kt == 0), stop=(kt == KT - 1))
                kb.copy(qmT[:, h, t0:t0 + nb], pg[:, 0:nb], e="act")
        groups = [("q", 0, 512), ("ga", 512, 512), ("gm", 1024, 512), ("qi0", 1536, 512), ("qi1", 2048, 512),
                  ("wi", 2560, 16)]
        for gname, c0, ncol in groups:
            load_wt(w_tok, c0, ncol)
            for i in range(9):
                nr = 128
                tok = slice(i * 128, i * 128 + nr)
                pg = PS[i % 2]
                for kt in range(KT):
                    kb.mm(pg[0:nr, 0:ncol], xnTo[:, kt, tok], (wt[:, kt, 0:ncol], kt // 4),
                          start=(kt == 0), stop=(kt == KT - 1))
                f_, b_ = tf[i % 2], tb[i % 2]
                if gname in ("ga", "gm"):
                    dst = sga if gname == "ga" else sgm
                    kb.act(f_[0:nr, :], pg[0:nr, :], AF.Tanh, scale=0.5)
                    kb.stt(f_[0:nr, :], f_[0:nr, :], 1.0, pg[0:nr, :], ALU.add, ALU.mult)
                    kb.ts(dst[0:nr, i, :], f_[0:nr, :], 0.5, ALU.mult)
                elif gname == "wi":
                    kb.act(wis[0:nr, i, :], pg[0:nr, 0:16], AF.Copy, scale=1.0 / 32.0)
                else:
                    hd = 64 if gname == "q" else 32
                    nh = 512 // (2 * hd)
                    tab = ropeO if gname == "q" else ropeOI
                    kb.copy(f_[0:nr, :], pg[0:nr, :], e="act")
                    xv = f_[0:nr, :].rearrange("p (h two d) -> p h two d", two=2, d=hd)
                    cos = tab[0:nr, i:i + 1, 0, :].to_broadcast([nr, nh, hd])
                    sin = tab[0:nr, i:i + 1, 1, :].to_broadcast([nr, nh, hd])
                    ov = b_[0:nr, :].rearrange("p (h two d) -> p h two d", two=2, d=hd)
                    r4 = [r_[0:nr, 0:256].rearrange("p (h d) -> p h d", d=hd) for r_ in rr]
                    kb.tt(r4[0], xv[:, :, 0, :], cos, ALU.mult)
                    kb.tt(r4[1], xv[:, :, 1, :], sin, ALU.mult, e="pool")
                    kb.tt(ov[:, :, 0, :], r4[0], r4[1], ALU.subtract)
                    kb.tt(r4[0], xv[:, :, 1, :], cos, ALU.mult, e="pool")
                    kb.tt(r4[1], xv[:, :, 0, :], sin, ALU.mult)
                    kb.tt(ov[:, :, 1, :], r4[0], r4[1], ALU.add, e="pool")
                    for j in range(4):
                        kb.tr(PSB[:, j * 128:j * 128 + nr], b_[0:nr, j * 128:(j + 1) * 128], identb[0:nr, 0:nr])
                    pv_ = PSB[:, 0:512].rearrange("p (j t) -> p j t", t=128)[:, :, 0:nr]
                    if gname == "q":
                        kb.copy(qT[:, :, tok], pv_, e="act")
                        if i == 8:
                            kb.copy(qsb[:], b_[:], e="pool")
                    else:
                        j0 = 0 if gname == "qi0" else 4
                        kb.copy(qiT[:, j0:j0 + 4, tok], pv_, e="act")
        kb.release(m_b1)

        m_b2 = kb.mark()
        kpos = kb.sb("kpos", [128, SEQ], F32)
        pw2 = kb.sb("pw2", [128, 24], F32)
        kb.dma(kpos[:], kpos_d)
        kb.dma(pw2[:], pw2_d)
        acc = kb.sb("acc", [128, SEQ], F32)
        junk = kb.sb("junk", [128, SEQ], BF16)
        sel = kb.sb("sel", [128, SEQ], BF16)
        ebuf = kb.sb("ebuf", [128, SEQ], BF16)
        pbuf = kb.sb("pbuf", [128, SEQ], BF16)
        pT = kb.sb("pT", [128, 16, 128], BF16)
        rl = [kb.sb("rl%d" % i, [128, 512], F32) for i in range(2)]
        cm = kb.sb("cm", [128, 256], F32)
        nbm = kb.sb("nbm", [128, 256], F32)
        sm = {nm: kb.sb("sm_" + nm, [128, 1], F32) for nm in
              ["amax", "w0", "lo", "mid", "cnt", "g", "m", "negm", "rs", "rinv"]}
        wk = kb.sb("wk", [128, 24], F32)
        mx4 = kb.sb("mx4", [128, 4], F32)
        yab = kb.sb("yab", [128, 1024], BF16)
        BIG = 1.0e30
        for i in range(NT):
            tok = slice(i * 128, (i + 1) * 128)
            L = (2 * i + 2) * 128
            nkb = L // 128
            chunks = [(c0, min(512, L - c0)) for c0 in range(0, L, 512)]
            kb.ts(cm[:], kpos[:, L - 256:L], qpos[:, i:i + 1], ALU.is_le)
            if i == 0:
                kb.copy(sel[:, 0:256], cm[:])
            else:
                for h in range(16):
                    hp, hh = h // 2, h % 2
                    hs = slice(hh * 64, hh * 64 + 64)
                    for ci, (c0, cn) in enumerate(chunks):
                        pg = PS[4 + (h * len(chunks) + ci) % 2]
                        kb.mm(pg[:, 0:cn], qiT[hs, hp, tok], kiT[hs, c0:c0 + cn])
                        r_ = rl[(h * len(chunks) + ci) % 2]
                        kb.act(r_[:, 0:cn], pg[:, 0:cn], AF.Relu)
                        if h == 0:
                            kb.ts(acc[:, c0:c0 + cn], r_[:, 0:cn], wis[:, i, h:h + 1], ALU.mult)
                        else:
                            kb.stt(acc[:, c0:c0 + cn], r_[:, 0:cn], wis[:, i, h:h + 1], acc[:, c0:c0 + cn],
                                   ALU.mult, ALU.add)
                kb.reduce(sm["amax"][:], acc[:, 0:L], ALU.max)
                kb.reduce(sm["g"][:], acc[:, 0:L], ALU.min)
                kb.ts(sm["g"][:], sm["g"][:], -1.0, ALU.mult)
                kb.tt(sm["amax"][:], sm["amax"][:], sm["g"][:], ALU.max)
                kb.ts(nbm[:], cm[:], BIG, ALU.mult, -BIG, ALU.add)
                kb.tt(acc[:, L - 256:L], acc[:, L - 256:L], cm[:], ALU.mult)
                kb.tt(acc[:, L - 256:L], acc[:, L - 256:L], nbm[:], ALU.add)
                kb.ts(sm["w0"][:], sm["amax"][:], 2.0, ALU.mult, 2.0, ALU.add)
                kb.ts(sm["lo"][:], sm["amax"][:], -1.0, ALU.mult, -1.0, ALU.add)
                kb.ts(wk[:], pw2[:], sm["w0"][:], ALU.mult)
                for k_ in range(24):
                    kb.tt(sm["mid"][:], sm["lo"][:], wk[:, k_:k_ + 1], ALU.add)
                    kb.ts(junk[:, 0:L], acc[:, 0:L], sm["mid"][:], ALU.is_ge, None, ALU.add, accum=sm["cnt"][:])
                    kb.ts(sm["g"][:], sm["cnt"][:], 255.5, ALU.is_ge, wk[:, k_:k_ + 1], ALU.mult)
                    kb.tt(sm["lo"][:], sm["lo"][:], sm["g"][:], ALU.add)
                kb.ts(sel[:, 0:L], acc[:, 0:L], sm["lo"][:], ALU.is_ge)
            for h in range(4):
                for ci, (c0, cn) in enumerate(chunks):
                    kb.mm(PS[ci][:, 0:cn], qT[:, h, tok], kT[:, h, c0:c0 + cn])
                    kb.reduce(mx4[:, ci:ci + 1], PS[ci][:, 0:cn], ALU.max)
                kb.reduce(sm["m"][:], mx4[:, 0:len(chunks)], ALU.max)
                kb.ts(sm["negm"][:], sm["m"][:], -SC, ALU.mult)
                for ci, (c0, cn) in enumerate(chunks):
                    kb.act(ebuf[:, c0:c0 + cn], PS[ci][:, 0:cn], AF.Exp, bias=sm["negm"][:], scale=SC)
                kb.stt(pbuf[:, 0:L], ebuf[:, 0:L], 1.0, sel[:, 0:L], ALU.mult, ALU.mult, accum=sm["rs"][:])
                kb.recip(sm["rinv"][:], sm["rs"][:])
                for k0 in range(0, nkb, 8):
                    kn = min(8, nkb - k0)
                    for kk_ in range(kn):
                        kb.tr(PSB[:, kk_ * 128:(kk_ + 1) * 128], pbuf[:, (k0 + kk_) * 128:(k0 + kk_ + 1) * 128], identb[:])
                    kb.copy(pT[:, k0:k0 + kn, :].rearrange("p a b -> p (a b)"), PSB[:, 0:kn * 128], e="act")
                po = PS[6]
                for kk_ in range(nkb):
                    kb.mm(po[:, 0:128], pT[:, kk_, :], vtok[:, kk_, h * 128:(h + 1) * 128],
                          start=(kk_ == 0), stop=(kk_ == nkb - 1))
                kb.stt(yab[:, h * 128:(h + 1) * 128], po[:, 0:128], sm["rinv"][:], sga[:, i, h * 128:(h + 1) * 128],
                       ALU.mult, ALU.mult)
            for h in range(4):
                pg = PS[h % 2]
                kb.mm(pg[:, 0:256], qmT[:, h, tok], mkT[:, h, :])
                kb.reduce(sm["m"][:], pg[:, 0:256], ALU.max)
                kb.ts(sm["negm"][:], sm["m"][:], -SC, ALU.mult)
                kb.act(ebuf[:, 0:256], pg[:, 0:256], AF.Exp, bias=sm["negm"][:], scale=SC, accum=sm["rs"][:])
                kb.recip(sm["rinv"][:], sm["rs"][:])
                for mt in range(2):
                    kb.tr(PSB[:, mt * 128:(mt + 1) * 128], ebuf[:, mt * 128:(mt + 1) * 128], identb[:])
                kb.copy(pT[:, 0:2, :].rearrange("p a b -> p (a b)"), PSB[:, 0:256], e="act")
                po = PS[6]
                for mt in range(2):
                    kb.mm(po[:, 0:128], pT[:, mt, :], mvb[:, mt, h * 128:(h + 1) * 128], start=(mt == 0), stop=(mt == 1))
                kb.stt(yab[:, 512 + h * 128:512 + (h + 1) * 128], po[:, 0:128], sm["rinv"][:],
                       sgm[:, i, h * 128:(h + 1) * 128], ALU.mult, ALU.mult)
            for j in range(8):
                kb.tr(PSB[:, j * 128:(j + 1) * 128], yab[:, j * 128:(j + 1) * 128], identb[:])
            kb.copy(yamT[:, :, tok], PSB[:].rearrange("p (j t) -> p j t", t=128), e="act")
        kb.release(m_b2)

        if "s" in stages:
            U32 = mybir.dt.uint32
            m_s = kb.mark()
            BIGS = 1.0e30
            NPOOLR = 2560 * 128
            pt_d = din("pt", [4, 64], I32)
            cache_ki2 = din("cache_ki2", [8 * 2560, 1024])
            cache_kv = din("cache_kv", [NPOOLR, 1024])
            mem_kT = din("mem_kT", [4, 4, 128, 256])
            mem_v = din("mem_v", [4, 256, 512])
            iota64 = kb.sb("iota64", [128, 64], F32)
            identf = kb.sb("identf", [128, 128], F32)
            newcm = kb.sb("newcm", [128, 8], F32)
            kb.dma(iota64[:], din("c_iota64", [128, 64]))
            kb.dma(identf[:], din("c_identf", [128, 128]))
            kb.dma(newcm[:], din("c_newcm", [128, 8]))
            onesf = kb.sb("onesf", [128, 128], F32)
            kb.memset(onesf[:], 1.0)
            SP = PS[0:6]
            SPB = PSB
            idxu = kb.sb("idxu", [128, 256], U32)
            vmx = kb.sb("vmx", [128, 8], F32)
            inew = kb.sb("inew", [128, 4], F32)
            yabs = kb.sb("yabs", [128, 1024], BF16)
            idxp = [kb.sb("idxp%d" % pr, [128, 1], I32) for pr in range(2)]
            idxpf = kb.sb("idxpf", [128, 2], F32)
            idxcf = kb.sb("idxcf", [128, 8, 2], F32)
            idxc = kb.sb("idxc", [128, 8, 2], I32)
            for pr in range(2):
                kb.dma(idxp[pr][:], pt_d[2 * pr:2 * pr + 2, :].rearrange("b (p o) -> (b p) o", o=1))
                kb.copy(idxpf[:, pr:pr + 1], idxp[pr][:])
            for ch_ in range(8):
                kb.ts(idxcf[:, ch_, :], idxpf[:], float(ch_ * 2560), ALU.add)
            kb.copy(idxc[:], idxcf[:])
            qipad = kb.sb("qipad", [128, 8, 2, 8], BF16)
            qmpad = kb.sb("qmpad", [128, 4, 2, 8], BF16)
            kb.memset(qipad[:], 0.0)
            kb.memset(qmpad[:], 0.0)
            kb.copy(qipad[:, :, 0, 0:4], qiT[:, :, 1024 + 64:1024 + 68])
            kb.copy(qipad[:, :, 1, 4:8], qiT[:, :, 1024 + 68:1024 + 72])
            kb.copy(qmpad[:, :, 0, 0:4], qmT[:, :, 1024 + 64:1024 + 68])
            kb.copy(qmpad[:, :, 1, 4:8], qmT[:, :, 1024 + 68:1024 + 72])

            def mm_rows(sb_, out_tile, cols, lhs_plain, lhs_pad, rhs, first=True, last=True):
                if sb_ < 2:
                    kb.mm(out_tile[SROWS[sb_]:SROWS[sb_] + 4, cols], lhs_plain, rhs, start=first, stop=last)
                else:
                    kb.mm(out_tile[64:72, cols], lhs_pad, rhs, start=(first and sb_ == 2), stop=(last and sb_ == 3))

            m_s1 = kb.mark()
            Isc = kb.sb("Isc", [128, 8200], F32)
            kb.memset(Isc[:, 8192:8200], -BIGS)
            kig = kb.sb("s_kig", [128, 16, 64], F32)
            kib = kb.sb("s_kib", [128, 16, 2, 64], BF16)
            kiTc = kb.sb("s_kiTc", [128, 16, 128], BF16)
            rls = [kb.sb("rls%d" % i_, [128, 512], F32) for i_ in range(2)]
            for ch in range(8):
                for pr in range(2):
                    kb.gather(kig[:].rearrange("p a b -> p (a b)"), cache_ki2, idxc[:, ch, pr:pr + 1],
                              bounds_check=8 * 2560 - 1, oob_is_err=False)
                    kb.copy(kib[:, :, 0, :], kig[:], e="act")
                    kb.copy(kib[:, :, 1, :], kig[:], e="pool")
                    for t8 in range(2):
                        for j in range(8):
                            kb.tr(SPB[:, j * 128:(j + 1) * 128],
                                  kib[:, t8 * 8 + j, :, :].rearrange("p a b -> p (a b)"), identb[:])
                        kb.copy(kiTc[:, t8 * 8:(t8 + 1) * 8, :].rearrange("p a b -> p (a b)"), SPB[:], e="act")
                    pslc = slice(pr * 64, pr * 64 + 64)
                    for h in range(16):
                        hp, hh = h // 2, h % 2
                        hs = slice(hh * 64, hh * 64 + 64)
                        for q2 in range(2):
                            pg = SP[(h * 2 + q2) % 4]
                            for bb in range(2):
                                sb_ = 2 * pr + bb
                                mm_rows(sb_, pg, slice(0, 512),
                                        qiT[hs, hp, 1024 + SROWS[sb_]:1024 + SROWS[sb_] + 4], qipad[hs, hp, sb_ % 2, :],
                                        kiTc[hs, q2 * 8:(q2 + 1) * 8, bb * 64:(bb + 1) * 64])
                            r_ = rls[(h * 2 + q2) % 2]
                            kb.act(r_[pslc, :], pg[pslc, :], AF.Relu)
                            c0 = ch * 1024 + q2 * 512
                            if h == 0:
                                kb.ts(Isc[pslc, c0:c0 + 512], r_[pslc, :], wis[pslc, 8, h:h + 1], ALU.mult)
                            else:
                                kb.stt(Isc[pslc, c0:c0 + 512], r_[pslc, :], wis[pslc, 8, h:h + 1], Isc[pslc, c0:c0 + 512],
                                       ALU.mult, ALU.add)
            for h in range(16):
                hp, hh = h // 2, h % 2
                hs = slice(hh * 64, hh * 64 + 64)
                pg = SP[4 + h % 2]
                for sb_ in range(4):
                    mm_rows(sb_, pg, slice(0, 4), qiT[hs, hp, 1024 + SROWS[sb_]:1024 + SROWS[sb_] + 4],
                            qipad[hs, hp, sb_ % 2, :], kiT[hs, SEQ + sb_ * 64:SEQ + sb_ * 64 + 4])
                r_ = rls[h % 2]
                kb.act(r_[:, 0:4], pg[:, 0:4], AF.Relu)
                if h == 0:
                    kb.ts(Isc[:, 8192:8196], r_[:, 0:4], wis[:, 8, h:h + 1], ALU.mult)
                else:
                    kb.stt(Isc[:, 8192:8196], r_[:, 0:4], wis[:, 8, h:h + 1], Isc[:, 8192:8196], ALU.mult, ALU.add)
            kb.tt(Isc[:, 8192:8196], Isc[:, 8192:8196], newcm[:, 0:4], ALU.mult)
            kb.tt(Isc[:, 8192:8196], Isc[:, 8192:8196], newcm[:, 4:8], ALU.add)
            kb.copy(inew[:], Isc[:, 8192:8196])
            for r in range(32):
                kb.vmax(vmx[:], Isc[:])
                kb.vmaxidx(idxu[:, r * 8:(r + 1) * 8], vmx[:], Isc[:])
                kb.vmr(Isc[:], vmx[:], Isc[:], -BIGS)
            kb.release(m_s1)
            idxf = kb.sb("s_idxf", [128, 256], F32)
            kb.copy(idxf[:], idxu[:])
            for half in range(2):
                kb.tr(SP[0][:, half * 128:(half + 1) * 128], idxf[:, half * 128:(half + 1) * 128], identf[:])
            qf = {nm: kb.sb("qf_" + nm, [128, 2, 4, 4], F32) for nm in ["idx", "pg", "tau", "isnew", "notnew", "ptv", "phys"]}
            qi_ = {nm: kb.sb("qi_" + nm, [128, 2, 4, 4], I32) for nm in ["idx", "pg", "tau", "phys"]}
            for sb_ in range(4):
                kb.copy(qf["idx"][:, :, sb_, :],
                        SP[0][:, 0:256].rearrange("p (a r) -> p a r", a=2)[:, :, SROWS[sb_]:SROWS[sb_] + 4])
            kb.copy(qi_["idx"][:], qf["idx"][:])
            kb.ts(qi_["pg"][:], qi_["idx"][:], 63, ALU.bitwise_and)
            kb.ts(qi_["tau"][:], qi_["idx"][:], 6, ALU.arith_shift_right)
            kb.copy(qf["pg"][:], qi_["pg"][:])
            kb.copy(qf["tau"][:], qi_["tau"][:])
            kb.ts(qf["isnew"][:], qf["idx"][:], 8191.5, ALU.is_ge)
            kb.ts(qf["notnew"][:], qf["isnew"][:], -1.0, ALU.mult, 1.0, ALU.add)
            ptbi = kb.sb("s_ptbi", [128, 256], I32)
            ptb = kb.sb("s_ptb", [128, 4, 64], F32)
            kb.dma(ptbi[:], pt_d.rearrange("(o b) p -> o (b p)", o=1).to_broadcast([128, 256]))
            kb.copy(ptb[:].rearrange("p a b -> p (a b)"), ptbi[:])
            oh = kb.sb("s_oh", [128, 8, 64], F32)
            for sb_ in range(4):
                ohv = oh[:].rearrange("p (a t) j -> p a t j", a=2)
                kb.tt(ohv, iota64[:].rearrange("p (a t j) -> p a t j", a=1, t=1).to_broadcast([128, 2, 4, 64]),
                      qf["pg"][:, :, sb_, :].rearrange("p a (t o) -> p a t o", o=1).to_broadcast([128, 2, 4, 64]),
                      ALU.is_equal)
                kb.tt(ohv, ohv, ptb[:, sb_:sb_ + 1, :].rearrange("p (a t) j -> p a t j", a=1).to_broadcast([128, 2, 4, 64]),
                      ALU.mult)
                kb.reduce(qf["ptv"][:, :, sb_, :], ohv, ALU.add)
            kb.stt(qf["phys"][:], qf["ptv"][:], 128.0, qf["tau"][:], ALU.mult, ALU.add)
            kb.tt(qf["phys"][:], qf["phys"][:], qf["notnew"][:], ALU.mult)
            kb.stt(qf["phys"][:], qf["isnew"][:], 330000.0, qf["phys"][:], ALU.mult, ALU.add)
            kb.copy(qi_["phys"][:], qf["phys"][:])
            fl = kb.sb("s_fl", [128, 4], F32)
            kb.ts(fl[:], inew[:], vmx[:, 7:8], ALU.is_ge)
            kb.tt(fl[:], fl[:], newcm[:, 0:4], ALU.mult)
            flTs = kb.sb("flTs", [128, 128], F32)
            kb.memset(flTs[:], 0.0)
            kb.tr(SP[0][0:4, 0:128], fl[:], identf[:])
            kb.copy(flTs[0:4, :], SP[0][0:4, 0:128])
            KVg = [kb.sb("KVg%d" % i_, [128, 1024], F32) for i_ in range(2)]
            KVn = kb.sb("KVn", [128, 1024], F32)
            Ph = [kb.sb("Ph%d" % i_, [128, 4, 128], F32) for i_ in range(2)]
            for t_ in KVg + [KVn] + Ph:
                kb.memset(t_[:], 0.0, e="pool")
            zr = kb.sb("s_zr", [128, 512], BF16)
            qbs = kb.sb("qbs", [128, 512], F32)
            tmpd = kb.sb("tmpd", [128, 4, 128], F32)
            scq = kb.sb("scq", [128, 3, 4], F32)
            pq = kb.sb("s_pq", [128, 3, 4], F32)
            m4 = kb.sb("s_m4", [128, 1], F32)
            dm = kb.sb("s_dm", [128, 4], F32)
            yacc, ysum = SP[1], SP[2]
            nmm = 16 * 3 * 4
            cnt = [0, 0]
            pcnt = 0
            for sb_ in range(4):
                r0 = SEQ + sb_ * 64
                kb.dma(KVn[0:4, 0:512], o_k[r0:r0 + 4, :])
                kb.dma(KVn[0:4, 512:1024], o_v[r0:r0 + 4, :])
                for t in range(4):
                    q = SROWS[sb_] + t
                    kb.ts(zr[:], qsb[:], identb[:, q:q + 1], ALU.mult)
                    kb.mm(SP[3][:], onesb[:], zr[:])
                    kb.copy(qbs[:], SP[3][:], e="act")
                    for half in range(2):
                        g = KVg[half]
                        kb.gather(g[:], cache_kv, qi_["phys"][:, half, sb_, t:t + 1],
                                  bounds_check=NPOOLR - 1, oob_is_err=False)
                        kb.tt(tmpd[:].rearrange("p a b -> p (a b)"), g[:, 0:512], qbs[:], ALU.mult)
                        kb.reduce(scq[:, half, :], tmpd[:], ALU.add)
                    kb.tt(tmpd[:].rearrange("p a b -> p (a b)"), KVn[:, 0:512], qbs[:], ALU.mult)
                    kb.reduce(scq[:, 2, :], tmpd[:], ALU.add)
                    for j in range(3):
                        kb.tr(SP[4][0:4, j * 128:(j + 1) * 128], scq[:, j, :], identf[:])
                    kb.reduce(m4[0:4, :], SP[4][0:4, 0:384], ALU.max)
                    kb.ts(dm[0:4, :], identf[0:4, 0:4], m4[0:4, 0:1], ALU.mult)
                    kb.mm(SP[5][:, 0:4], onesf[0:4, :], dm[0:4, :])
                    kb.tt(scq[:], scq[:], SP[5][:, 0:4].rearrange("p (a h) -> p a h", a=1).to_broadcast([128, 3, 4]),
                          ALU.subtract)
                    kb.act(pq[:].rearrange("p a h -> p (a h)"), scq[:].rearrange("p a h -> p (a h)"), AF.Exp, scale=SC)
                    kb.ts(pq[:, 0, :], pq[:, 0, :], qf["notnew"][:, 0, sb_, t:t + 1], ALU.mult)
                    kb.ts(pq[:, 1, :], pq[:, 1, :], qf["notnew"][:, 1, sb_, t:t + 1], ALU.mult)
                    kb.ts(pq[:, 2, :], pq[:, 2, :], flTs[:, q:q + 1], ALU.mult)
                    for j in range(3):
                        Pj = Ph[pcnt % 2]
                        pcnt += 1
                        src = KVg[j] if j < 2 else KVn
                        kb.copy(Pj[:, :, q], pq[:, j, :])
                        for h in range(4):
                            kb.mm(yacc[:, h * 128:(h + 1) * 128], Pj[:, h, :], src[:, 512 + h * 128:512 + (h + 1) * 128],
                                  start=(cnt[0] == 0), stop=(cnt[0] == nmm - 1))
                            cnt[0] += 1
                            kb.mm(ysum[:, 2 * h:2 * h + 2], Pj[:, h, :], onesf[:, 0:2],
                                  start=(cnt[1] == 0), stop=(cnt[1] == nmm - 1))
                            cnt[1] += 1
                        kb.memset(Pj[:, :, q], 0.0)
            rinvs = kb.sb("rinvs", [128, 4], F32)
            kb.ts(rinvs[:], ysum[:, 0:8:2], 1.0e-30, ALU.add)
            kb.recip(rinvs[:], rinvs[:])
            kb.tt(tmpd[:], yacc[:].rearrange("p (h d) -> p h d", d=128),
                  rinvs[:].rearrange("p (h o) -> p h o", o=1).to_broadcast([128, 4, 128]), ALU.mult)
            kb.tt(yabs[:, 0:512], tmpd[:].rearrange("p a b -> p (a b)"), sga[:, 8, :], ALU.mult)
            mkTs = kb.sb("mkTs", [128, 4, 4, 256], BF16)
            mvs = kb.sb("mvs", [128, 4, 2, 512], BF16)
            for sb_ in range(4):
                kb.load_w(mkTs[:, sb_, :, :], mem_kT[sb_].rearrange("h d m -> d h m"), 128, [4, 256])
                kb.load_w(mvs[:, sb_, :, :], mem_v[sb_].rearrange("(mt m) c -> m mt c", m=128), 128, [2, 512])
            e2 = kb.sb("s_e2", [128, 256], BF16)
            pT2 = kb.sb("pT2", [128, 2, 128], BF16)
            pT2pad = kb.sb("pT2pad", [128, 2, 2, 8], BF16)
            kb.memset(pT2pad[:], 0.0)
            sm2 = {nm: kb.sb("sm2_" + nm, [128, 1], F32) for nm in ["m", "negm", "rs", "rinv"]}
            for h in range(4):
                pg = SP[0]
                for sb_ in range(4):
                    mm_rows(sb_, pg, slice(0, 256), qmT[:, h, 1024 + SROWS[sb_]:1024 + SROWS[sb_] + 4],
                            qmpad[:, h, sb_ % 2, :], mkTs[:, sb_, h, :])
                kb.reduce(sm2["m"][:], pg[:, 0:256], ALU.max)
                kb.ts(sm2["negm"][:], sm2["m"][:], -SC, ALU.mult)
                kb.act(e2[:], pg[:, 0:256], AF.Exp, bias=sm2["negm"][:], scale=SC, accum=sm2["rs"][:])
                kb.recip(sm2["rinv"][:], sm2["rs"][:])
                for mt in range(2):
                    kb.tr(SPB[:, mt * 128:(mt + 1) * 128], e2[:, mt * 128:(mt + 1) * 128], identb[:])
                kb.copy(pT2[:].rearrange("p a b -> p (a b)"), SPB[:, 0:256], e="act")
                kb.copy(pT2pad[:, :, 0, 0:4], pT2[:, :, 64:68])
                kb.copy(pT2pad[:, :, 1, 4:8], pT2[:, :, 68:72])
                po = SP[3]
                for sb_ in range(4):
                    for mt in range(2):
                        mm_rows(sb_, po, slice(0, 128), pT2[:, mt, SROWS[sb_]:SROWS[sb_] + 4], pT2pad[:, mt, sb_ % 2, :],
                                mvs[:, sb_, mt, h * 128:(h + 1) * 128], first=(mt == 0), last=(mt == 1))
                kb.stt(yabs[:, 512 + h * 128:512 + (h + 1) * 128], po[:, 0:128], sm2["rinv"][:],
                       sgm[:, 8, h * 128:(h + 1) * 128], ALU.mult, ALU.mult)
            for j in range(8):
                kb.tr(SPB[:, j * 128:(j + 1) * 128], yabs[:, j * 128:(j + 1) * 128], identb[:])
            kb.copy(yamT[:, :, 1024:1152], SPB[:].rearrange("p (j t) -> p j t", t=128), e="act")
            kb.release(m_s)
        kb.release(m_bp)

        m_c = kb.mark()
        PS = [kb.ps("CS%d" % i, [128, 512]) for i in range(2)]
        wo = kb.sb("wo", [128, KT, 512], BF16)
        NTC = 9 if "s" in stages else NT
        hbuf = kb.sb("hbuf", [128, 9, D], F32)
        xres = [kb.sb("xres%d" % i, [128, 512], F32) for i in range(2)]
        fnw = kb.sb("fnw", [128, D], F32)
        kb.dma(fnw[:], fnw_d)
        wov = w_out_d.rearrange("(kt p) c -> p kt c", p=128)
        for nb_ in range(4):
            for k0 in range(0, KT, 4):
                kb.load_w(wo[:, k0:k0 + 4, :], wov[:, k0:k0 + 4, nb_ * 512:(nb_ + 1) * 512], 128, [4, 512], key=k0 // 4)
            for i in range(NTC):
                tok = slice(i * 128, (i + 1) * 128)
                pg = PS[i % 2]
                for kt in range(KT):
                    if kt < 8:
                        lhs = yT[:, kt, tok] if i < 8 else yTs[:, kt, :]
                    else:
                        lhs = yamT[:, kt - 8, tok]
                    kb.mm(pg[:], lhs, (wo[:, kt, :], kt // 4), start=(kt == 0), stop=(kt == KT - 1))
                xr_ = xres[i % 2]
                kb.dma(xr_[:], x_own_tok[tok, nb_ * 512:(nb_ + 1) * 512])
                kb.tt(hbuf[:, i, nb_ * 512:(nb_ + 1) * 512], pg[:], xr_[:], ALU.add)
        ysq = kb.sb("ysq", [128, D], BF16)
        fs = {nm: kb.sb("fs_" + nm, [128, 1], F32) for nm in ["ss", "rstd"]}
        yo = [kb.sb("yo0", [128, D], F32)] * 2
        for i in range(NTC):
            tok = slice(i * 128, (i + 1) * 128)
            kb.act(ysq[:], hbuf[:, i, :], AF.Square, accum=fs["ss"][:])
            kb.act(fs["rstd"][:], fs["ss"][:], AF.Sqrt, bias=EPS, scale=1.0 / D)
            kb.recip(fs["rstd"][:], fs["rstd"][:])
            kb.stt(yo[i % 2][:], hbuf[:, i, :], fs["rstd"][:], fnw[:], ALU.mult, ALU.mult)
            kb.dma(o_y[tok, :], yo[i % 2][:])
        kb.release(m_c)

    kb.S.finish()
    kb.S.flush()
    blk_cm.__exit__(None, None, None)
    while kb.cms:
        kb.cms.pop().__exit__(None, None, None)
    kb.S.close()
    return nc


def _rope_table(pos, d):
    half = d // 2
    inv_freq = (np.float32(10000.0) ** (-np.arange(half, dtype=np.float32) * np.float32(2.0 / d))).astype(np.float32)
    ang = pos.astype(np.float32)[:, None] * inv_freq[None, :]
    return np.cos(ang).astype(np.float32), np.sin(ang).astype(np.float32)


def _pos_all():
    pos = np.zeros(TALL, np.int64)
    pos[:SEQ] = np.arange(SEQ)
    for sb in range(4):
        pos[SEQ + sb * 64: SEQ + (sb + 1) * 64] = 8192 + np.minimum(np.arange(64), 3)
    return pos


_NC_CACHE = {}


def kernel(**inp):
    import os as _os
    stages = inp.pop("_stages", tuple(_os.environ.get("STAGES", "a4,a3,b,s").split(",")))
    f32 = np.float32
    x_prompt = np.asarray(inp["x_prompt"], f32)
    x_sample = np.asarray(inp["x_sample"], f32)
    w_in = np.asarray(inp["w_in"], f32)[0]
    key = tuple(stages)
    if key not in _NC_CACHE:
        _NC_CACHE[key] = build_program(stages)
    nc = _NC_CACHE[key]

    pos = _pos_all()
    ca, sa = _rope_table(pos, 128)
    ci, si = _rope_table(pos, 64)
    rope_all = np.stack([ca, sa], axis=1).reshape(18, 128, 2, 64).transpose(1, 0, 2, 3).copy()
    rope_idx = np.stack([ci, si], axis=1).reshape(18, 128, 2, 32).transpose(1, 0, 2, 3).copy()
    o_att = 3200 + 1024
    w_kv = np.concatenate([w_in[:, o_att + 512:o_att + 1024], w_in[:, o_att + 1024:o_att + 1536],
                           w_in[:, o_att + 2048 + 1024 + 16:o_att + 2048 + 1024 + 16 + 64],
                           w_in[:, o_att + 2048 + 1024 + 16:o_att + 2048 + 1024 + 16 + 64]], axis=1)
    w_kv = np.ascontiguousarray(w_kv)
    normw = np.ascontiguousarray(np.asarray(inp["norm_w"], f32)[0].reshape(KT, 128).T)
    onesb = np.ones((128, 128), NBF)
    identb = np.eye(128, dtype=f32).astype(NBF)
    g = lambda k: np.asarray(inp[k], f32)
    pi = np.arange(128)
    h2i, si = pi // 64, pi % 64
    same = (h2i[:, None] == h2i[None, :])
    strict = same & (si[:, None] < si[None, :])
    incl = same & (si[:, None] <= si[None, :])
    mask12 = np.zeros((128, 2, 2, 128), f32)
    mask12[:, :, 0, :] = strict[:, None, :]
    mask12[:, :, 1, :] = incl[:, None, :]
    mask3 = np.zeros((128, 2, 128), f32)
    mask3[:] = (same & (si[None, :] < si[:, None]))[:, None, :]
    identbd = np.zeros((128, 2, 128), f32)
    identbd[:] = np.eye(128, dtype=f32)[:, None, :]
    resetm = np.ones((128, 256), f32)
    resetm[:, ::64] = 0.0
    padm = np.zeros((128, 256), f32)
    for q_ in range(4):
        padm[:, q_::64] = 1.0
    blkb = same.astype(f32).astype(NBF)
    col = lambda v, n: np.ascontiguousarray(v.reshape(n, 128).T)
    p_mu = col(g("shift_mu")[0], 25)
    p_w0, p_a0, p_kk, p_ka = col(g("w0")[0], 8), col(g("a0")[0], 8), col(g("k_k")[0], 8), col(g("k_a")[0], 8)
    p_rk = col(g("r_k")[0].reshape(1024), 8)

    def gnl(v):
        v = v.reshape(8, 2, 64)
        o = np.zeros((128, 8, 64), f32)
        for h2 in range(2):
            o[h2 * 64:(h2 + 1) * 64] = v[None, :, h2, :]
        return o
    p_gnw, p_gnb = gnl(g("gn_w")[0]), gnl(g("gn_b")[0])
    w_c = np.ascontiguousarray(w_in[:, 3072:3200])
    w_rkv = np.stack([np.concatenate([w_in[:, p_ * 128:(p_ + 1) * 128], w_in[:, 1024 + p_ * 128:1024 + (p_ + 1) * 128],
                                      w_in[:, 2048 + p_ * 128:2048 + (p_ + 1) * 128]], axis=1) for p_ in range(8)])
    w2a2 = np.ascontiguousarray(np.concatenate([g("w2")[0], g("a2")[0]], axis=0))
    state_shift = g("state_shift")[0]
    state_wkv = g("state_wkv")[0]
    mem_prompt = g("mem_prompt")
    p_memnw = col(g("mem_norm_w")[0], KT)
    w_gr = np.ascontiguousarray(w_in[:, 3200:4224])
    w_qm = np.ascontiguousarray(w_in[:, 7376:7888])
    w_tok = np.ascontiguousarray(np.concatenate([w_in[:, 4224:4736], w_in[:, 5760:6272], w_in[:, 7888:8400],
                                                 w_in[:, 6272:7296], w_in[:, 7296:7312]], axis=1))
    w_mem = np.ascontiguousarray(g("w_mem_kv")[0])
    w_out_h = np.ascontiguousarray(g("w_out")[0])
    c_kpos = np.ascontiguousarray(np.broadcast_to(np.arange(SEQ, dtype=f32)[None, :], (128, SEQ)))
    c_pw2 = np.ascontiguousarray(np.broadcast_to((0.5 ** np.arange(1, 25)).astype(f32)[None, :], (128, 24)))
    p_fnw = np.ascontiguousarray(np.broadcast_to(g("final_norm_w")[None, :], (128, D)))
    s_inputs = {}
    if "s" in stages:
        ck = np.asarray(inp["cache_k"], f32)[0].reshape(-1, 512)
        cv = np.asarray(inp["cache_v"], f32)[0].reshape(-1, 512)
        s_inputs["cache_kv"] = np.concatenate([ck, cv], axis=1)
        cki = np.asarray(inp["cache_kidx"], f32)[0]
        s_inputs["cache_ki2"] = np.ascontiguousarray(
            np.concatenate([cki[:, ch_ * 16:(ch_ + 1) * 16, :].reshape(2560, 1024) for ch_ in range(8)], axis=0))
        s_inputs["c_iota64"] = np.ascontiguousarray(np.broadcast_to(np.arange(64, dtype=f32)[None, :], (128, 64)))
        s_inputs["c_identf"] = np.eye(128, dtype=f32)
        ncm = np.zeros((128, 8), f32)
        ncm[:, 4:8] = -1.0e30
        for sb_ in range(4):
            for t_ in range(4):
                ncm[SROWS[sb_] + t_, 0:t_ + 1] = 1.0
                ncm[SROWS[sb_] + t_, 4:4 + t_ + 1] = 0.0
        s_inputs["c_newcm"] = ncm
        page_table = np.asarray(inp["page_table"], np.int32)
        cmk = np.asarray(inp["cache_mem_k"], f32)[0]
        cmv = np.asarray(inp["cache_mem_v"], f32)[0]
    in_maps = []
    own_toks = []
    for c in range(8):
        b = c // 2
        e_ = c % 2
        xT = np.zeros((D, TALL), f32)
        xT[:, :SEQ] = x_prompt[b].T
        for sb in range(4):
            xT[:, SEQ + sb * 64: SEQ + sb * 64 + 4] = x_sample[4 * c + sb].T
        e_ = c % 2
        sprevT = np.ascontiguousarray(state_shift[4 * c:4 * c + 4].reshape(4, 25, 128).transpose(2, 1, 0))
        st_wkvT = np.ascontiguousarray(state_wkv[4 * c:4 * c + 4].transpose(0, 1, 3, 2).reshape(4, 8, 128, 64))
        own_tok = np.zeros(NOWNS, np.int64)
        own_pos = np.zeros(9 * 128, np.int64)
        for i_ in range(8):
            for j_ in range(2):
                t_ = (4 * i_ + 2 * j_ + e_) * 64 + np.arange(64)
                own_tok[i_ * 128 + j_ * 64: i_ * 128 + (j_ + 1) * 64] = t_
                own_pos[i_ * 128 + j_ * 64: i_ * 128 + (j_ + 1) * 64] = t_
        x_own_tok = np.zeros((NOWNS, D), f32)
        x_own_tok[:NOWN] = x_prompt[b][own_tok[:NOWN]]
        for sb_ in range(4):
            own_pos[1024 + SROWS[sb_]: 1024 + SROWS[sb_] + 4] = 8192 + np.arange(4)
            x_own_tok[NOWN + SROWS[sb_]: NOWN + SROWS[sb_] + 4] = x_sample[4 * c + sb_]
        co, so = _rope_table(own_pos, 128)
        coi, soi = _rope_table(own_pos, 64)
        rope_own = np.stack([co, so], axis=1).reshape(9, 128, 2, 64).transpose(1, 0, 2, 3).copy()
        rope_owni = np.stack([coi, soi], axis=1).reshape(9, 128, 2, 32).transpose(1, 0, 2, 3).copy()
        qpos = np.ascontiguousarray(own_pos.reshape(9, 128).T.astype(f32))
        extra = {
            "xT_own": np.ascontiguousarray(x_own_tok.T), "memT": np.ascontiguousarray(mem_prompt[b].T),
            "p_memnw": p_memnw, "w_gr": w_gr, "w_qm": w_qm, "w_tok": w_tok, "w_mem": w_mem, "w_out": w_out_h,
            "rope_own": rope_own, "rope_owni": rope_owni, "qpos": qpos, "c_kpos": c_kpos, "c_pw2": c_pw2,
            "p_fnw": p_fnw, "x_own_tok": x_own_tok,
        }
        own_toks.append(own_tok)
        sx = dict(s_inputs)
        if "s" in stages:
            sx["pt"] = np.ascontiguousarray(page_table[4 * c:4 * c + 4])
            sx["mem_kT"] = np.ascontiguousarray(cmk[4 * c:4 * c + 4].transpose(0, 2, 3, 1))
            sx["mem_v"] = np.ascontiguousarray(cmv[4 * c:4 * c + 4].reshape(4, 256, 512))
        in_maps.append({
            **(extra if "b" in stages else {}), **sx,
            "xT_all": xT, "c_onesb": onesb, "c_identb": identb, "p_normw": normw,
            "w_kv": w_kv, "rope_all": rope_all, "rope_idx": rope_idx,
            "c_blkb": blkb, "c_mask12": mask12, "c_mask3": mask3, "c_identbd": identbd,
            "c_resetm": resetm, "c_padm": padm, "c_emask": np.full((128, 2, 128), e_, np.int32),
            "p_mu": p_mu, "p_w0": p_w0, "p_a0": p_a0, "p_kk": p_kk, "p_ka": p_ka, "p_rk": p_rk,
            "p_gnw": p_gnw, "p_gnb": p_gnb, "sprevT": sprevT, "w_c": w_c, "w_rkv": w_rkv, "w2a2": w2a2,
            "st_wkvT": st_wkvT,
        })
    res = run_bass_kernel_spmd(nc, in_maps, core_ids=list(range(8)))
    R = [dict(r) for r in res.results]
    for r in R:
        r.setdefault("o_wkv", np.zeros((5, 8, 128, 64), f32))
        r.setdefault("o_shift", np.zeros((128, 25, 5), f32))

    def samp(name, width):
        out = np.zeros((32, 4, width), f32)
        for c in range(8):
            for sb in range(4):
                out[4 * c + sb] = R[c][name][SEQ + sb * 64: SEQ + sb * 64 + 4]
        return out
    new_k_p = np.stack([R[2 * b]["o_k"][:SEQ] for b in range(4)]).reshape(1, 4, SEQ, 4, 128)
    new_v_p = np.stack([R[2 * b]["o_v"][:SEQ] for b in range(4)]).reshape(1, 4, SEQ, 4, 128)
    new_ki_p = np.stack([R[2 * b]["o_ki"][:SEQ] for b in range(4)]).reshape(1, 4, SEQ, 64)
    new_k_s = samp("o_k", 512).reshape(1, 32, 4, 4, 128)
    new_v_s = samp("o_v", 512).reshape(1, 32, 4, 4, 128)
    new_ki_s = samp("o_ki", 64).reshape(1, 32, 4, 64)
    z = lambda *sh: np.zeros(sh, f32)
    wkvfix = lambda a: a.reshape(16, 64, 64).transpose(0, 2, 1)
    wkv_p = np.stack([wkvfix(R[2 * b]["o_wkv"][0]) for b in range(4)])[None]
    wkv_s = np.stack([wkvfix(R[c]["o_wkv"][1 + sb]) for c in range(8) for sb in range(4)])[None]
    sh_p = np.stack([R[2 * b]["o_shift"][:, :, 0].T.reshape(3200) for b in range(4)])[None]
    sh_s = np.stack([R[c]["o_shift"][:, :, 1 + sb].T.reshape(3200) for c in range(8) for sb in range(4)])[None]
    y_p = z(4, SEQ, D)
    y_s = z(32, 4, D)
    memk_p, memv_p = z(1, 4, 256, 4, 128), z(1, 4, 256, 4, 128)
    if "b" in stages:
        for c in range(8):
            y_p[c // 2][own_toks[c][:NOWN]] = R[c]["o_y"][:NOWN]
            for sb_ in range(4):
                y_s[4 * c + sb_] = R[c]["o_y"][NOWN + SROWS[sb_]:NOWN + SROWS[sb_] + 4]
        for b in range(4):
            memk_p[0, b] = R[2 * b]["o_memk"].transpose(2, 1, 0)
            memv_p[0, b] = R[2 * b]["o_memv"].reshape(256, 4, 128)
    return (y_p, y_s, wkv_p, sh_p, new_k_p, new_v_p, new_ki_p,
            memk_p, memv_p, wkv_s, sh_s,
            new_k_s, new_v_s, new_ki_s)
```
